# Optimizing a Trainium2 kernel written in Bass

```python
import math
import jax, jax.numpy as jnp
from jax import lax
import numpy as np

D_MODEL = 2048
BATCH = 2
SEQ = 4096
DEPTH = 1
DEC_BATCH = 32
DEC_SEQ = 16
PAST_LEN = 2048

CHUNK = 64
Q_BLOCK = 128
N_HEADS_A = 16
Q_LORA = 768
KV_LORA = 512
QK_NOPE = 128
QK_ROPE = 64
V_DIM = 128
ROPE_THETA = 10000.0
ATTN_SCALE = (QK_NOPE + QK_ROPE) ** -0.5
SSM_EXPAND = 2
D_INNER = SSM_EXPAND * D_MODEL
HEAD_DIM_S = 64
N_HEADS_S = D_INNER // HEAD_DIM_S
N_GROUPS_S = 8
D_STATE = 128
CONV_W = 4
CONV_DIM = D_INNER + 2 * N_GROUPS_S * D_STATE
SSD_CHUNK = 64
D_FF = 5632
EPS = 1e-6
N_BRANCH = 2
SPLITS = (Q_LORA, KV_LORA + QK_ROPE, D_INNER, CONV_DIM, N_HEADS_S, N_BRANCH * D_MODEL)
D_IN_PROJ = Q_LORA + KV_LORA + QK_ROPE + D_INNER + CONV_DIM + N_HEADS_S + N_BRANCH * D_MODEL

kernel_name = 'streaming_mla_mamba2_macaron_step'


def rmsnorm(x, g):
    xf = x.astype(jnp.float32)
    y = xf * lax.rsqrt(jnp.mean(xf * xf, axis=-1, keepdims=True) + EPS)
    return (y * g.astype(jnp.float32)).astype(x.dtype)


def rope(x, pos):
    half = QK_ROPE // 2
    inv = jnp.power(ROPE_THETA, -jnp.arange(half, dtype=jnp.float32) / half)
    ang = pos.astype(jnp.float32)[:, None] * inv[None, :]
    if x.ndim == 4:
        ang = ang[:, None, :]
    cos, sin = jnp.cos(ang), jnp.sin(ang)
    x1 = x[..., :half].astype(jnp.float32)
    x2 = x[..., half:].astype(jnp.float32)
    return jnp.concatenate([x1 * cos - x2 * sin, x1 * sin + x2 * cos], axis=-1).astype(x.dtype)


def swiglu(h, w_gate, w_up, w_down):
    return (jax.nn.silu(h @ w_gate) * (h @ w_up)) @ w_down


def split_cols(t, sizes):
    out, off = [], 0
    for s in sizes:
        out.append(t[..., off:off + s])
        off += s
    return out


def causal_conv(xbc, conv_state, w, b):
    full = jnp.concatenate([conv_state.astype(xbc.dtype), xbc], axis=1)
    T = xbc.shape[1]
    acc = b + w[0] * full[:, 0:T]
    for k in range(1, CONV_W):
        acc = acc + w[k] * full[:, k:k + T]
    return jax.nn.silu(acc), full[:, -(CONV_W - 1):]


def ssd(x, a, b_in, c_in, init_state, q):
    f32 = jnp.float32
    Bt, L, H, P = x.shape
    G, N = b_in.shape[-2], b_in.shape[-1]
    R = H // G
    nc = L // q
    x = x.astype(f32).reshape(Bt, nc, q, G, R, P)
    b_in = b_in.astype(f32).reshape(Bt, nc, q, G, N)
    c_in = c_in.astype(f32).reshape(Bt, nc, q, G, N)
    a = a.astype(f32).reshape(Bt, nc, q, G, R).transpose(0, 3, 4, 1, 2)
    a_cs = jnp.cumsum(a, axis=-1)
    causal = jnp.tril(jnp.ones((q, q), dtype=bool))
    seg = a_cs[..., :, None] - a_cs[..., None, :]
    decay = jnp.exp(jnp.where(causal, seg, -jnp.inf))
    cb = jnp.einsum('bclgn,bcsgn->bgcls', c_in, b_in)
    y_diag = jnp.einsum('bgrcls,bcsgrp->bclgrp', cb[:, :, None] * decay, x)
    to_end = jnp.exp(a_cs[..., -1:] - a_cs)
    chunk_states = jnp.einsum('bcsgn,bgrcs,bcsgrp->cbgrpn', b_in, to_end, x)
    chunk_decay = jnp.exp(a_cs[..., -1]).transpose(3, 0, 1, 2)

    def step(s, inp):
        dec, st = inp
        return dec[..., None, None] * s + st, s

    s0 = init_state.astype(f32).reshape(Bt, G, R, P, N)
    final, starts = lax.scan(step, s0, (chunk_decay, chunk_states))
    y_off = jnp.einsum('bclgn,cbgrpn,bgrcl->bclgrp', c_in, starts, jnp.exp(a_cs))
    y = (y_diag + y_off).reshape(Bt, L, H, P)
    return y, final.reshape(Bt, H, P, N)


def gated_group_rmsnorm(y, z, g):
    v = y.astype(jnp.float32) * jax.nn.silu(z.astype(jnp.float32))
    v = v.reshape(*y.shape[:-1], N_GROUPS_S, D_INNER // N_GROUPS_S)
    v = v * lax.rsqrt(jnp.mean(v * v, axis=-1, keepdims=True) + EPS)
    return (v.reshape(y.shape) * g.astype(jnp.float32)).astype(y.dtype)


def mla_prompt_attention(q_nope, q_rope, c_kv, k_rope, w_uk, w_uv):
    B, S, H, _ = q_nope.shape
    k_nope = jnp.einsum('bsc,chd->bshd', c_kv, w_uk)
    v = jnp.einsum('bsc,chd->bshd', c_kv, w_uv)
    nb = S // Q_BLOCK
    chunk_id = jnp.arange(S) // CHUNK
    qn = q_nope.reshape(B, nb, Q_BLOCK, H, QK_NOPE).transpose(1, 0, 2, 3, 4)
    qr = q_rope.reshape(B, nb, Q_BLOCK, H, QK_ROPE).transpose(1, 0, 2, 3, 4)
    qc = chunk_id.reshape(nb, Q_BLOCK)

    def block(args):
        qn_b, qr_b, qc_b = args
        s = (jnp.einsum('bqhd,bkhd->bhqk', qn_b, k_nope)
             + jnp.einsum('bqhd,bkd->bhqk', qr_b, k_rope)).astype(jnp.float32) * ATTN_SCALE
        mask = chunk_id[None, :] <= qc_b[:, None]
        p = jax.nn.softmax(jnp.where(mask, s, -jnp.inf), axis=-1).astype(v.dtype)
        return jnp.einsum('bhqk,bkhd->bqhd', p, v)

    o = lax.map(block, (qn, qr, qc))
    return o.transpose(1, 0, 2, 3, 4).reshape(B, S, H, V_DIM)


def mla_sample_attention(q_nope, q_rope, lat_all, krope_all, w_uk, w_uv):
    q_lat = jnp.einsum('bthd,chd->bthc', q_nope, w_uk)
    s = (jnp.einsum('bthc,bkc->bhtk', q_lat, lat_all)
         + jnp.einsum('bthd,bkd->bhtk', q_rope, krope_all)).astype(jnp.float32) * ATTN_SCALE
    p = jax.nn.softmax(s, axis=-1).astype(lat_all.dtype)
    o_lat = jnp.einsum('bhtk,bkc->bthc', p, lat_all)
    return jnp.einsum('bthc,chd->bthd', o_lat, w_uv)


def token_mixer(h, pos, cache_lat, cache_krope, ssm_state, conv_state, w_in, b_gate, n_qa, w_q_b,
                n_kva, w_kv_b, conv_w, conv_b, dt_bias, a_log, d_skip, n_ssm, w_o_attn, w_o_ssm, w_out):
    B, T, _ = h.shape
    q_a, kv_a, z, xbc, dt, gates = split_cols(h @ w_in, SPLITS)
    c_q = rmsnorm(q_a, n_qa)
    q = (c_q @ w_q_b).reshape(B, T, N_HEADS_A, QK_NOPE + QK_ROPE)
    q_nope = q[..., :QK_NOPE]
    q_rope = rope(q[..., QK_NOPE:], pos)
    c_kv = rmsnorm(kv_a[..., :KV_LORA], n_kva)
    k_rope = rope(kv_a[..., KV_LORA:], pos)
    w_kv = w_kv_b.reshape(KV_LORA, N_HEADS_A, QK_NOPE + V_DIM)
    w_uk, w_uv = w_kv[..., :QK_NOPE], w_kv[..., QK_NOPE:]
    if cache_lat is None:
        o_attn = mla_prompt_attention(q_nope, q_rope, c_kv, k_rope, w_uk, w_uv)
    else:
        lat_all = jnp.concatenate([cache_lat.astype(c_kv.dtype), c_kv], axis=1)
        krope_all = jnp.concatenate([cache_krope.astype(k_rope.dtype), k_rope], axis=1)
        o_attn = mla_sample_attention(q_nope, q_rope, lat_all, krope_all, w_uk, w_uv)
    attn_branch = o_attn.reshape(B, T, N_HEADS_A * V_DIM) @ w_o_attn
    xbc, new_conv = causal_conv(xbc, conv_state, conv_w, conv_b)
    xs, bs, cs = split_cols(xbc, (D_INNER, N_GROUPS_S * D_STATE, N_GROUPS_S * D_STATE))
    dt = jax.nn.softplus(dt.astype(jnp.float32) + dt_bias.astype(jnp.float32))
    A = -jnp.exp(a_log.astype(jnp.float32))
    xs_h = xs.reshape(B, T, N_HEADS_S, HEAD_DIM_S)
    q_len = SSD_CHUNK if cache_lat is None else T
    y, new_ssm = ssd(xs_h * dt[..., None], dt * A, bs.reshape(B, T, N_GROUPS_S, D_STATE),
                     cs.reshape(B, T, N_GROUPS_S, D_STATE), ssm_state, q_len)
    y = (y + d_skip.astype(jnp.float32)[:, None] * xs_h.astype(jnp.float32)).astype(h.dtype)
    y = gated_group_rmsnorm(y.reshape(B, T, D_INNER), z, n_ssm)
    ssm_branch = y @ w_o_ssm
    g = jax.nn.sigmoid(gates.reshape(B, T, N_BRANCH, D_MODEL) + b_gate)
    merged = g[..., 0, :] * attn_branch + g[..., 1, :] * ssm_branch
    return merged @ w_out, c_kv, k_rope, new_ssm, new_conv


def trunk_layer(x, pos, cache_lat, cache_krope, ssm_state, conv_state, lw):
    (n_f1, wg1, wu1, wd1, n_mix, w_in, b_gate, n_qa, w_q_b, n_kva, w_kv_b, conv_w, conv_b,
     dt_bias, a_log, d_skip, n_ssm, w_o_attn, w_o_ssm, w_out, n_f2, wg2, wu2, wd2) = lw
    x = x + 0.5 * swiglu(rmsnorm(x, n_f1), wg1, wu1, wd1)
    m, c_kv, k_rope, new_ssm, new_conv = token_mixer(
        rmsnorm(x, n_mix), pos, cache_lat, cache_krope, ssm_state, conv_state, w_in, b_gate, n_qa,
        w_q_b, n_kva, w_kv_b, conv_w, conv_b, dt_bias, a_log, d_skip, n_ssm, w_o_attn, w_o_ssm, w_out)
    x = x + m
    x = x + 0.5 * swiglu(rmsnorm(x, n_f2), wg2, wu2, wd2)
    return x, c_kv, k_rope, new_ssm, new_conv


def setup_inputs(seed: int = 0) -> dict:
    key = jax.random.key(seed)
    ks = jax.random.split(key, 40)
    f32 = jnp.float32
    L = DEPTH

    def nrm(k, shape, scale):
        return jax.random.normal(k, shape, f32) * scale

    def gain(k, n):
        return 1.0 + nrm(k, (L, n), 0.1)

    dt0 = jnp.exp(jax.random.uniform(ks[30], (L, N_HEADS_S), f32, math.log(1e-3), math.log(1e-1)))
    return {
        'x_prompt': nrm(ks[0], (BATCH, SEQ, D_MODEL), 1.0),
        'x_sample': nrm(ks[1], (DEC_BATCH, DEC_SEQ, D_MODEL), 1.0),
        'cache_kv_latent': nrm(ks[2], (L, DEC_BATCH, PAST_LEN, KV_LORA), 1.0),
        'cache_k_rope': nrm(ks[3], (L, DEC_BATCH, PAST_LEN, QK_ROPE), 1.0),
        'state_ssm': nrm(ks[4], (L, DEC_BATCH, N_HEADS_S, HEAD_DIM_S, D_STATE), 0.1),
        'state_conv': nrm(ks[5], (L, DEC_BATCH, CONV_W - 1, CONV_DIM), 1.0),
        'norm_ffn1': gain(ks[6], D_MODEL),
        'w_ffn1_gate': nrm(ks[7], (L, D_MODEL, D_FF), D_MODEL ** -0.5),
        'w_ffn1_up': nrm(ks[8], (L, D_MODEL, D_FF), D_MODEL ** -0.5),
        'w_ffn1_down': nrm(ks[9], (L, D_FF, D_MODEL), D_FF ** -0.5),
        'norm_mix': gain(ks[10], D_MODEL),
        'w_in': nrm(ks[11], (L, D_MODEL, D_IN_PROJ), D_MODEL ** -0.5),
        'b_gate': nrm(ks[12], (L, N_BRANCH, D_MODEL), 0.1),
        'norm_q_a': gain(ks[13], Q_LORA),
        'w_q_b': nrm(ks[14], (L, Q_LORA, N_HEADS_A * (QK_NOPE + QK_ROPE)), Q_LORA ** -0.5),
        'norm_kv_a': gain(ks[15], KV_LORA),
        'w_kv_b': nrm(ks[16], (L, KV_LORA, N_HEADS_A * (QK_NOPE + V_DIM)), KV_LORA ** -0.5),
        'conv_w': nrm(ks[17], (L, CONV_W, CONV_DIM), CONV_W ** -0.5),
        'conv_b': nrm(ks[18], (L, CONV_DIM), 0.1),
        'dt_bias': dt0 + jnp.log(-jnp.expm1(-dt0)),
        'a_log': jnp.log(jax.random.uniform(ks[19], (L, N_HEADS_S), f32, 1.0, 16.0)),
        'd_skip': 1.0 + nrm(ks[20], (L, N_HEADS_S), 0.1),
        'norm_ssm': gain(ks[21], D_INNER),
        'w_o_attn': nrm(ks[22], (L, N_HEADS_A * V_DIM, D_MODEL), (N_HEADS_A * V_DIM) ** -0.5),
        'w_o_ssm': nrm(ks[23], (L, D_INNER, D_MODEL), D_INNER ** -0.5),
        'w_out': nrm(ks[24], (L, D_MODEL, D_MODEL), D_MODEL ** -0.5),
        'norm_ffn2': gain(ks[25], D_MODEL),
        'w_ffn2_gate': nrm(ks[26], (L, D_MODEL, D_FF), D_MODEL ** -0.5),
        'w_ffn2_up': nrm(ks[27], (L, D_MODEL, D_FF), D_MODEL ** -0.5),
        'w_ffn2_down': nrm(ks[28], (L, D_FF, D_MODEL), D_FF ** -0.5),
        'norm_final': 1.0 + nrm(ks[29], (D_MODEL,), 0.1),
    }


def reference(x_prompt, x_sample, cache_kv_latent, cache_k_rope, state_ssm, state_conv,
              norm_ffn1, w_ffn1_gate, w_ffn1_up, w_ffn1_down, norm_mix, w_in, b_gate,
              norm_q_a, w_q_b, norm_kv_a, w_kv_b, conv_w, conv_b, dt_bias, a_log, d_skip,
              norm_ssm, w_o_attn, w_o_ssm, w_out, norm_ffn2, w_ffn2_gate, w_ffn2_up,
              w_ffn2_down, norm_final):
    yp, ys = x_prompt, x_sample
    Bp, S = x_prompt.shape[0], x_prompt.shape[1]
    Bs, T = x_sample.shape[0], x_sample.shape[1]
    past = cache_kv_latent.shape[2]
    pos_p = jnp.arange(S)
    pos_s = past + jnp.arange(T)
    p_lat, p_kr, p_ssm, p_conv = [], [], [], []
    s_lat, s_kr, s_ssm, s_conv = [], [], [], []
    for l in range(DEPTH):
        lw = (norm_ffn1[l], w_ffn1_gate[l], w_ffn1_up[l], w_ffn1_down[l], norm_mix[l], w_in[l],
              b_gate[l], norm_q_a[l], w_q_b[l], norm_kv_a[l], w_kv_b[l], conv_w[l], conv_b[l],
              dt_bias[l], a_log[l], d_skip[l], norm_ssm[l], w_o_attn[l], w_o_ssm[l], w_out[l],
              norm_ffn2[l], w_ffn2_gate[l], w_ffn2_up[l], w_ffn2_down[l])
        yp, c_kv, k_r, st, cv = trunk_layer(
            yp, pos_p, None, None,
            jnp.zeros((Bp, N_HEADS_S, HEAD_DIM_S, D_STATE), jnp.float32),
            jnp.zeros((Bp, CONV_W - 1, CONV_DIM), yp.dtype), lw)
        p_lat.append(c_kv)
        p_kr.append(k_r)
        p_ssm.append(st)
        p_conv.append(cv)
        ys, c_kv, k_r, st, cv = trunk_layer(
            ys, pos_s, cache_kv_latent[l], cache_k_rope[l], state_ssm[l], state_conv[l], lw)
        s_lat.append(c_kv)
        s_kr.append(k_r)
        s_ssm.append(st)
        s_conv.append(cv)
    y_prompt = rmsnorm(yp, norm_final)
    y_sample = rmsnorm(ys, norm_final)
    return (y_prompt, y_sample,
            jnp.stack(p_lat), jnp.stack(p_kr), jnp.stack(p_ssm), jnp.stack(p_conv),
            jnp.stack(s_lat), jnp.stack(s_kr), jnp.stack(s_ssm), jnp.stack(s_conv))
```

```python
import contextlib
import numpy as np
import concourse.bass as bass
import concourse.mybir as mybir
from concourse.bass_utils import run_bass_kernel_spmd

F32 = mybir.dt.float32
BF16 = mybir.dt.bfloat16
ALU = mybir.AluOpType
AF = mybir.ActivationFunctionType
AX = mybir.AxisListType

D = 2048
DC = 16
SEQ = 4096
NQ = 4
QT = 1024
NS = 4
TS = 16
PAST = 2048
DFF = 5632
QL = 768
KVL = 512
ROPE = 64
NH = 16
DI = 4096
CONV = 6144
HS = 64
NG = 8
NST = 128
DIN = 15744
O_QA, O_KV, O_Z, O_XBC, O_DT, O_G = 0, 768, 1344, 5440, 11584, 11648
EPS = 1e-6
SCALE = 192 ** -0.5
NEG = -30000.0


class Buf:
    __slots__ = ("t", "name", "lw", "rd", "sc")

    def __init__(self, t, name="", sc=False):
        self.t = t
        self.name = name
        self.lw = None
        self.rd = []
        self.sc = sc

    def __getitem__(self, k):
        return self.t[k]


class Op:
    __slots__ = ("eng", "fn", "deps", "dma", "sig", "sigval", "sem", "waits", "hz")

    def __init__(self, eng, fn, dma):
        self.hz = False
        self.eng = eng
        self.fn = fn
        self.dma = dma
        self.deps = []
        self.sig = False
        self.sigval = 0
        self.sem = None
        self.waits = []


ENGS = ("pe", "act", "dve", "pool", "sp")
KDMA = 8
SAFE_SYNC = False


class Prog:
    def __init__(self, nc):
        self.nc = nc
        self.ops = []
        self.by_eng = {e: [] for e in ENGS}
        self.fence_ops = []
        self.fenced = set()

    def fence(self):
        f = []
        for e in ENGS:
            comp = [o for o in self.by_eng[e] if not o.dma]
            if comp:
                f.append(comp[-1])
            f += [o for o in self.by_eng[e] if o.dma][-KDMA:]
        self.fence_ops = f
        self.fenced = set()

    def add(self, eng, fn, reads=(), writes=(), dma=False, hz=False):
        op = Op(eng, fn, dma)
        op.hz = hz
        deps = {}
        if self.fence_ops and eng not in self.fenced:
            self.fenced.add(eng)
            for d in self.fence_ops:
                deps[id(d)] = d
        strong = set()
        for b in reads:
            if b.lw is not None:
                deps[id(b.lw)] = b.lw
                if b.sc:
                    strong.add(id(b.lw))
        for b in writes:
            if b.lw is not None:
                deps[id(b.lw)] = b.lw
            for r in b.rd:
                deps[id(r)] = r
        for d in deps.values():
            if (not d.dma) and d.eng == eng and not dma and (eng == "pe" or not (SAFE_SYNC or id(d) in strong or d.hz)):
                continue
            op.deps.append(d)
        for b in reads:
            b.rd.append(op)
        for b in writes:
            b.lw = op
            b.rd = []
        self.ops.append(op)
        self.by_eng[eng].append(op)
        return op

    def emit(self, stack):
        nc = self.nc
        for op in self.ops:
            for d in op.deps:
                d.sig = True
            if op.dma:
                op.sig = True
        csem = {e: stack.enter_context(nc.semaphore("cs_" + e)) for e in ("pe", "act", "dve", "pool")}
        dsem = {e: [stack.enter_context(nc.semaphore("ds_%s%d" % (e, i))) for i in range(KDMA)]
                for e in ("sp", "pool")}
        for e in ENGS:
            cnt = 0
            dcnt = 0
            for op in self.by_eng[e]:
                if op.dma:
                    op.sem = dsem[e][dcnt % KDMA]
                    op.sigval = 16 * (dcnt // KDMA + 1)
                    if dcnt >= KDMA:
                        op.waits.append((op.sem, 16 * (dcnt // KDMA)))
                    dcnt += 1
                elif op.sig:
                    cnt += 1
                    op.sem = csem[e]
                    op.sigval = cnt
        for e in ENGS:
            seen = {}
            for op in self.by_eng[e]:
                ws = list(op.waits)
                for d in op.deps:
                    ws.append((d.sem, d.sigval))
                best = {}
                for s, v in ws:
                    k = id(s)
                    if seen.get(k, 0) >= v:
                        continue
                    if k not in best or best[k][1] < v:
                        best[k] = (s, v)
                op.waits = list(best.values())
                for s, v in op.waits:
                    seen[id(s)] = v
        final = []
        for e in ("sp", "pool"):
            last = {}
            for op in self.by_eng[e]:
                if op.dma:
                    last[id(op.sem)] = (op.sem, op.sigval)
            final += list(last.values())
        by_eng = self.by_eng

        def run(eng_name):
            def body(eng):
                for op in by_eng[eng_name]:
                    for s, v in op.waits:
                        eng.wait_ge(s, v)
                    ins = op.fn(eng)
                    if op.sig:
                        ins.then_inc(op.sem, 16 if op.dma else 1)
                if eng_name == "sp":
                    for s, v in final:
                        eng.wait_ge(s, v)
            return body

        block = stack.enter_context(nc.Block())
        block.tensor(run("pe"))
        block.scalar(run("act"))
        block.vector(run("dve"))
        block.gpsimd(run("pool"))
        block.sync(run("sp"))


def blocks_of(n, w=512):
    out = []
    c = 0
    while c < n:
        out.append((c, min(w, n - c)))
        c += w
    return out


def build_program(stages):
    nc = bass.Bass("TRN2", target_bir_lowering=False)
    st = contextlib.ExitStack()
    P = Prog(nc)

    def din(name, shape, dt=F32):
        return nc.dram_tensor(name, list(shape), dt, kind="ExternalInput").ap()

    def dout(name, shape, dt=F32):
        return nc.dram_tensor(name, list(shape), dt, kind="ExternalOutput").ap()

    def dscr(name, shape, dt=F32):
        return Buf(nc.dram_tensor(name, list(shape), dt).ap(), name)

    def sb(name, shape, dt=F32):
        return Buf(st.enter_context(nc.sbuf_tensor(name, list(shape), dt)), name)

    xp = din("xp", [SEQ, D])
    xs = din("xs", [NS * TS, D])
    c_lat = din("c_lat", [NS, PAST, KVL])
    c_kr = din("c_kr", [NS, PAST, ROPE])
    s_ssm_in = din("s_ssm_in", [NS, DI, NST])
    s_conv_in = din("s_conv_in", [NS * 3, CONV])
    W = {}
    for nm, shp in (("n_f1", [D]), ("wg1", [D, DFF]), ("wu1", [D, DFF]), ("wd1", [DFF, D]),
                    ("n_mix", [D]), ("w_in", [D, DIN]), ("b_gate", [2 * D]), ("n_qa", [QL]),
                    ("w_qb", [QL, NH * 192]), ("n_kva", [KVL]), ("w_kvb", [KVL, NH * 256]),
                    ("conv_w", [4, CONV]), ("conv_b", [CONV]), ("dt_bias", [HS]), ("a_log", [HS]),
                    ("d_skip", [HS]), ("n_ssm", [DI]), ("w_oa", [D, D]), ("w_os", [DI, D]),
                    ("w_out", [D, D]), ("n_f2", [D]), ("wg2", [D, DFF]), ("wu2", [D, DFF]),
                    ("wd2", [DFF, D]), ("n_fin", [D])):
        W[nm] = din(nm, shp)
    ropeT = din("ropeT", [2, ROPE, SEQ + TS])
    rope_tok = din("rope_tok", [SEQ + TS, 2 * ROPE])
    ident_in = din("ident_in", [128, 128])

    valid_in = din("valid_in", [SEQ, 1])
    kneg_in = din("kneg_in", [128, SEQ // 128])
    y_p = dout("y_p", [QT, D])
    y_s = dout("y_s", [NS * TS, D])
    o_plat = dout("o_plat", [QT, KVL])
    o_pkr = dout("o_pkr", [QT, ROPE])
    o_pssm = dout("o_pssm", [DI, NST])
    o_pconv = dout("o_pconv", [3, CONV])
    o_slat = dout("o_slat", [NS * TS, KVL])
    o_skr = dout("o_skr", [NS * TS, ROPE])
    o_sssm = dout("o_sssm", [NS, DI, NST])
    o_sconv = dout("o_sconv", [NS * 3, CONV])

    uid = [0]

    SC_NAMES = ("ss", "ssg", "te", "et", "decb", "dts", "a_t", "am", "vcol", "dt_t", "rstd")

    def sbp(ph, name, shape, dt=F32):
        uid[0] += 1
        nm = "%s_%d" % (name, uid[0])
        return Buf(ph.enter_context(nc.sbuf_tensor(nm, list(shape), dt)), nm, sc=(name in SC_NAMES))

    tri_in = din("tri_in", [128, 64])
    ustr_in = din("ustr_in", [128, 64])
    ident = sb("ident", [128, 128])
    onesb = sb("onesb", [128, 128], BF16)
    onesf = sb("onesf", [128, 128])
    epsc = sb("epsc", [128, 1])
    gcols = sb("gcols", [128, 4, DC])
    triV = sb("triV", [128, 64])
    ustr = sb("ustr", [128, 64])
    P.add("sp", lambda e: e.dma_start(out=ident[:], in_=ident_in), writes=[ident], dma=True)
    P.add("sp", lambda e: e.dma_start(out=triV[:], in_=tri_in), writes=[triV], dma=True)
    P.add("sp", lambda e: e.dma_start(out=ustr[:], in_=ustr_in), writes=[ustr], dma=True)
    P.add("dve", lambda e: e.memset(onesb[:], 1.0), writes=[onesb])
    P.add("dve", lambda e: e.memset(onesf[:], 1.0), writes=[onesf])
    P.add("dve", lambda e: e.memset(epsc[:], EPS), writes=[epsc])
    for i, nm in enumerate(("n_f1", "n_mix", "n_f2", "n_fin")):
        P.add("sp", lambda e, i=i, nm=nm: e.dma_start(
            out=gcols[:, i, :], in_=W[nm].rearrange("(c p) -> p c", p=128),
            allow_slow_non_contiguous=True), writes=[gcols], dma=True)

    def bc_load(dst, src1d, n):
        P.add("sp", lambda e: e.dma_start(out=dst.t[:, 0:n], in_=src1d.rearrange("(o n) -> o n", o=1).broadcast_to([128, n])),
              writes=[dst], dma=True)

    nqa_b = sb("nqa_b", [128, QL])
    nkv_b = sb("nkv_b", [128, KVL])
    dtb_b = sb("dtb_b", [128, HS])
    A_b = sb("A_b", [128, HS])
    dsk_b = sb("dsk_b", [128, HS])
    bc_load(nqa_b, W["n_qa"], QL)
    bc_load(nkv_b, W["n_kva"], KVL)
    bc_load(dtb_b, W["dt_bias"], HS)
    bc_load(A_b, W["a_log"], HS)
    bc_load(dsk_b, W["d_skip"], HS)
    P.add("act", lambda e: e.activation(out=A_b[:], in_=A_b[:], func=AF.Exp), reads=[A_b], writes=[A_b])
    P.add("dve", lambda e: e.tensor_scalar_mul(out=A_b[:], in0=A_b[:], scalar1=-1.0), reads=[A_b], writes=[A_b])
    cwc = sb("cwc", [128, 48, 5])
    for k in range(4):
        P.add("sp", lambda e, k=k: e.dma_start(out=cwc[:, :, k], in_=W["conv_w"][k, :].rearrange("(c p) -> p c", p=128),
                                               allow_slow_non_contiguous=True), writes=[cwc], dma=True)
    P.add("sp", lambda e: e.dma_start(out=cwc[:, :, 4], in_=W["conv_b"].rearrange("(c p) -> p c", p=128),
                                      allow_slow_non_contiguous=True), writes=[cwc], dma=True)
    bgc = sb("bgc", [128, 32])
    P.add("sp", lambda e: e.dma_start(out=bgc[:], in_=W["b_gate"].rearrange("(c p) -> p c", p=128),
                                      allow_slow_non_contiguous=True), writes=[bgc], dma=True)
    negm = sb("negm", [128, 1])
    P.add("dve", lambda e: e.memset(negm[:], 0.0), writes=[negm])
    P.add("dve", lambda e: e.memset(negm[64:128, :], NEG), writes=[negm])
    halo = sb("halo", [128, 48, 3])
    P.add("dve", lambda e: e.memset(halo[:], 0.0), writes=[halo])
    convout = sb("convout", [128, 48, 16])

    psb = [Buf(st.enter_context(nc.psum_tensor("ps%d" % i, [128, 512], F32)), "ps%d" % i) for i in range(8)]
    ps_ctr = [0]

    def next_ps():
        b = psb[ps_ctr[0] % 6]
        ps_ctr[0] += 1
        return b

    ev_ctr = [0]

    def ev_eng():
        ev_ctr[0] += 1
        return "act" if ev_ctr[0] % 2 else "dve"

    def copy(en, dst, src, reads, writes):
        if en == "act":
            P.add("act", lambda e: e.activation(out=dst, in_=src, func=AF.Copy), reads=reads, writes=writes)
        else:
            P.add(en, lambda e: e.tensor_copy(out=dst, in_=src), reads=reads, writes=writes)

    NTMAX = QT + NS * TS
    wbufs = [sb("wb%d" % i, [128, DC * 128], BF16) for i in range(6)]
    wb_ctr = [0]

    def load_w(wap, r0, nr, c0, ncol):
        kc = nr // 128
        wb = wbufs[wb_ctr[0] % len(wbufs)]
        wb_ctr[0] += 1
        view = wb.t[:, 0:kc * ncol].rearrange("p (c n) -> p c n", n=ncol)
        src = wap[r0:r0 + nr, c0:c0 + ncol].rearrange("(c p) n -> p c n", p=128)
        P.add("pool", lambda e: e.dma_start(out=view, in_=src), writes=[wb], dma=True)
        return wb, view

    stage_bufs = [sb("stg%d" % i, [128, D]) for i in range(2)]
    stg_ctr = [0]

    def next_stage():
        b = stage_bufs[stg_ctr[0] % len(stage_bufs)]
        stg_ctr[0] += 1
        return b

    x1T_d = dscr("x1T_d", [128, DC * NTMAX])
    cqT_d = dscr("cqT_d", [128, 6 * NTMAX], BF16)
    latT_d = dscr("latT_d", [128, 5 * SEQ], BF16)
    latS_d = dscr("latS_d", [128, 5 * NS * TS], BF16)
    z_d = dscr("z_d", [NTMAX, DI])
    dt_d = dscr("dt_d", [NTMAX, HS])
    xs_d = dscr("xs_d", [NTMAX, DI], BF16)
    b_d = dscr("b_d", [NTMAX, NG * NST], BF16)
    bcT_d = dscr("bcT_d", [128, 16 * NTMAX], BF16)
    gT_d = dscr("gT_d", [128, 32 * NTMAX])
    oT_d = dscr("oT_d", [128, DC * NTMAX], BF16)
    ynT_d = dscr("ynT_d", [128, 32 * NTMAX], BF16)
    S_d = dscr("S_d", [128, DI])
    mT_d = dscr("mT_d", [128, DC * NTMAX], BF16)

    def v3(buf, n):
        return buf.t.rearrange("p (c n) -> p c n", n=n)

    def load_xT(xT, src_ap, ntok, col0):
        xt = next_stage()
        P.add("sp", lambda e: e.dma_start(out=xt.t[0:ntok, :], in_=src_ap), writes=[xt], dma=True)
        for g in range(4):
            ps = next_ps()

            def tr(e, g=g, ps=ps):
                ins = None
                for j in range(4):
                    c = g * 4 + j
                    ins = e.transpose(out=ps.t[:, j * 128:j * 128 + ntok], in_=xt.t[0:ntok, c * 128:(c + 1) * 128],
                                      identity=ident.t[0:ntok, 0:ntok])
                return ins
            P.add("pe", tr, reads=[xt, ident], writes=[ps])
            copy(ev_eng(), xT.t[:, g * 4:(g + 1) * 4, col0:col0 + ntok],
                 ps.t[:, :].rearrange("p (j t) -> p j t", t=128)[:, :, 0:ntok], [ps], [xT])

    def rmsnorm_fm(xT, sqb, rstd, gi, nt, out, ph_hT=True):
        blks = blocks_of(nt)
        pss = [next_ps() for _ in blks]
        for c in range(DC):
            sq = sqb[c % 2]
            P.add("act", lambda e, c=c, sq=sq: e.activation(out=sq.t[:, 0:nt], in_=xT.t[:, c, 0:nt], func=AF.Square),
                  reads=[xT], writes=[sq])
            for (b0, bw), ps in zip(blks, pss):
                P.add("pe", lambda e, c=c, sq=sq, ps=ps, b0=b0, bw=bw: e.matmul(
                    ps.t[:, 0:bw], lhsT=onesb.t[:, :], rhs=sq.t[:, b0:b0 + bw], start=(c == 0), stop=(c == DC - 1)),
                    reads=[sq, onesb], writes=[ps])
        for (b0, bw), ps in zip(blks, pss):
            P.add("act", lambda e, ps=ps, b0=b0, bw=bw: e.activation(
                out=rstd.t[:, b0:b0 + bw], in_=ps.t[:, 0:bw], func=AF.Sqrt, scale=1.0 / D, bias=epsc.t[:, 0:1]),
                reads=[ps, epsc], writes=[rstd])
            P.add("dve", lambda e, b0=b0, bw=bw: e.reciprocal(out=rstd.t[:, b0:b0 + bw], in_=rstd.t[:, b0:b0 + bw]),
                  reads=[rstd], writes=[rstd], hz=True)
        for c in range(DC):
            P.add("dve", lambda e, c=c: e.scalar_tensor_tensor(
                out=out.t[:, c, 0:nt], in0=xT.t[:, c, 0:nt], scalar=gcols.t[:, gi, c:c + 1], in1=rstd.t[:, 0:nt],
                op0=ALU.mult, op1=ALU.mult), reads=[xT, rstd, gcols], writes=[out])

    def ffn(ph, xT, hT, wg, wu, wd, nt):
        blks = blocks_of(nt)
        hb = sbp(ph, "hid", [128, 11, NTMAX], BF16)
        silu_t = [sbp(ph, "silu", [128, 512]) for i in range(2)]
        sctr = 0
        for fq in range(4):
            for fl in range(11):
                f = fq * 11 + fl
                wgb, wgv = load_w(wg, 0, D, f * 128, 128)
                wub, wuv = load_w(wu, 0, D, f * 128, 128)
                for (b0, bw) in blks:
                    pg, pu = next_ps(), next_ps()

                    def mm(e, wv=wgv, ps=pg, b0=b0, bw=bw):
                        ins = None
                        for c in range(DC):
                            ins = e.matmul(ps.t[:, 0:bw], lhsT=wv[:, c, :], rhs=hT.t[:, c, b0:b0 + bw],
                                           start=(c == 0), stop=(c == DC - 1))
                        return ins
                    P.add("pe", mm, reads=[wgb, hT], writes=[pg])
                    P.add("pe", lambda e, wv=wuv, ps=pu, b0=b0, bw=bw, mm=mm: mm(e, wv, ps, b0, bw),
                          reads=[wub, hT], writes=[pu])
                    sl = silu_t[sctr % 2]
                    sctr += 1
                    P.add("act", lambda e, sl=sl, pg=pg, bw=bw: e.activation(out=sl.t[:, 0:bw], in_=pg.t[:, 0:bw],
                                                                           func=AF.Silu), reads=[pg], writes=[sl])
                    P.add("dve", lambda e, sl=sl, pu=pu, fl=fl, b0=b0, bw=bw: e.tensor_tensor(
                        out=hb.t[:, fl, b0:b0 + bw], in0=sl.t[:, 0:bw], in1=pu.t[:, 0:bw], op=ALU.mult),
                        reads=[sl, pu], writes=[hb])
            for d in range(DC):
                wdb, wdv = load_w(wd, fq * 11 * 128, 11 * 128, d * 128, 128)
                for (b0, bw) in blks:
                    ps = next_ps()

                    def mm2(e, wv=wdv, ps=ps, b0=b0, bw=bw):
                        ins = None
                        for fl in range(11):
                            ins = e.matmul(ps.t[:, 0:bw], lhsT=wv[:, fl, :], rhs=hb.t[:, fl, b0:b0 + bw],
                                           start=(fl == 0), stop=(fl == 10))
                        return ins
                    P.add("pe", mm2, reads=[wdb, hb], writes=[ps])
                    P.add("dve", lambda e, ps=ps, d=d, b0=b0, bw=bw: e.scalar_tensor_tensor(
                        out=xT.t[:, d, b0:b0 + bw], in0=ps.t[:, 0:bw], scalar=0.5, in1=xT.t[:, d, b0:b0 + bw],
                        op0=ALU.mult, op1=ALU.add), reads=[ps, xT], writes=[xT])

    def tr_to(dst_fn, src, ncols_list, ntok, reads, writes):
        for g0 in range(0, len(ncols_list), 4):
            ps = next_ps()
            grp = ncols_list[g0:g0 + 4]

            def tr(e, ps=ps, grp=grp):
                ins = None
                for j, (c0, cw) in enumerate(grp):
                    ins = e.transpose(out=ps.t[0:cw, j * 128:j * 128 + ntok], in_=src.t[0:ntok, c0:c0 + cw],
                                      identity=ident.t[0:ntok, 0:ntok])
                return ins
            P.add("pe", tr, reads=[src, ident], writes=[ps])
            for j, (c0, cw) in enumerate(grp):
                dst_fn(g0 + j, ps, ps.t[0:cw, j * 128:j * 128 + ntok])

    def store_tok(dst_ap, srcT, nchunks, col0, ntok):
        stg = next_stage()
        for g in range((nchunks + 3) // 4):
            ps = next_ps()
            nj = min(4, nchunks - g * 4)

            def tr(e, g=g, ps=ps, nj=nj):
                ins = None
                for j in range(nj):
                    ins = e.transpose(out=ps.t[0:ntok, j * 128:(j + 1) * 128],
                                      in_=srcT.t[:, g * 4 + j, col0:col0 + ntok], identity=ident.t[:, :])
                return ins
            P.add("pe", tr, reads=[srcT, ident], writes=[ps])
            copy(ev_eng(), stg.t[0:ntok, g * 512:g * 512 + nj * 128], ps.t[0:ntok, 0:nj * 128], [ps], [stg])
        P.add("sp", lambda e: e.dma_start(out=dst_ap, in_=stg.t[0:ntok, 0:nchunks * 128]), reads=[stg], dma=True)

    def lin_tm(hT, c0, ncols, tts, cb):
        tiles = []
        c = 0
        while c < ncols:
            cw = min(128, ncols - c)
            wb, wv = load_w(W["w_in"], 0, D, c0 + c, cw)
            tiles.append((c, cw, wb, wv))
            c += cw
        def emit_mm(t0, tn):
            banks = []
            for g0 in range(0, len(tiles), 4):
                ps = next_ps()
                grp = tiles[g0:g0 + 4]

                def mm(e, ps=ps, grp=grp, t0=t0, tn=tn):
                    ins = None
                    for (cc, cw, wb, wv) in grp:
                        off = cc - grp[0][0]
                        for k in range(DC):
                            ins = e.matmul(ps.t[0:tn, off:off + cw], lhsT=hT.t[:, k, t0:t0 + tn], rhs=wv[:, k, :],
                                           start=(k == 0), stop=(k == DC - 1))
                    return ins
                P.add("pe", mm, reads=[hT] + [t[2] for t in grp], writes=[ps])
                banks.append((ps, grp[0][0], sum(t[1] for t in grp)))
            return banks
        pend = emit_mm(*tts[0])
        for ti, (t0, tn) in enumerate(tts):
            banks = pend
            if ti + 1 < len(tts) and len(tiles) <= 8:
                pend = emit_mm(*tts[ti + 1])
                cb(ti, t0, tn, banks)
            else:
                cb(ti, t0, tn, banks)
                if ti + 1 < len(tts):
                    pend = emit_mm(*tts[ti + 1])

    def rs_of(ph_bufs, src, n, tn):
        sqt, ss = ph_bufs
        P.add("act", lambda e: e.activation(out=sqt.t[0:tn, 0:n], in_=src.t[0:tn, 0:n], func=AF.Square),
              reads=[src], writes=[sqt])
        P.add("dve", lambda e: e.reduce_sum(out=ss.t[0:tn, 0:1], in_=sqt.t[0:tn, 0:n], axis=AX.X), reads=[sqt], writes=[ss])
        P.add("act", lambda e: e.activation(out=ss.t[0:tn, 0:1], in_=ss.t[0:tn, 0:1], func=AF.Sqrt, scale=1.0 / n,
                                            bias=epsc.t[0:tn, 0:1]), reads=[ss, epsc], writes=[ss])
        P.add("dve", lambda e: e.reciprocal(out=ss.t[0:tn, 0:1], in_=ss.t[0:tn, 0:1]), reads=[ss], writes=[ss], hz=True)

    def phase1(q, last):
        prefix = not last
        nt = QT + (NS * TS if last else 0)
        tts = [(i * 128, 128) for i in range(QT // 128)] + ([(QT, NS * TS)] if last else [])
        blks = blocks_of(nt)
        with contextlib.ExitStack() as phH:
            hT = sbp(phH, "hT", [128, DC, NTMAX], BF16)
            with contextlib.ExitStack() as phA:
                xT = sbp(phA, "xT", [128, DC, NTMAX])
                sqb = [sbp(phA, "sq", [128, NTMAX], BF16) for i in range(2)]
                rstd = sbp(phA, "rstd", [128, NTMAX])
                for tt in range(QT // 128):
                    load_xT(xT, xp[q * QT + tt * 128:q * QT + (tt + 1) * 128, :], 128, tt * 128)
                if last:
                    load_xT(xT, xs[:, :], NS * TS, QT)
                rmsnorm_fm(xT, sqb, rstd, 0, nt, hT)
                with contextlib.ExitStack() as phF:
                    ffn(phF, xT, hT, W["wg1"], W["wu1"], W["wd1"], nt)
                P.fence()
                if stages.get("upto") == "ffn1":
                    for tt in range(QT // 128):
                        store_tok(y_p[tt * 128:(tt + 1) * 128, :], xT, DC, tt * 128, 128)
                    return
                rmsnorm_fm(xT, sqb, rstd, 1, nt, hT)
                if not prefix:
                    P.add("sp", lambda e: e.dma_start(out=v3(x1T_d, NTMAX)[:, :, 0:nt], in_=xT.t[:, :, 0:nt]),
                          reads=[xT], writes=[x1T_d], dma=True)
            P.fence()
            with contextlib.ExitStack() as phB:
                qa_sb = sbp(phB, "qa_sb", [128, QL])
                sqt = sbp(phB, "sqt", [128, QL])
                ss = sbp(phB, "ss", [128, 1])
                cq = sbp(phB, "cq", [128, QL])
                cqs = sbp(phB, "cqs", [128, 6, 128], BF16)
                kv_sb = sbp(phB, "kv_sb", [128, 576])
                lat = sbp(phB, "lat", [128, 576])
                rtk = sbp(phB, "rtk", [128, 128])
                rt1 = sbp(phB, "rt1", [128, 64])
                rt2 = sbp(phB, "rt2", [128, 64])
                lts = sbp(phB, "lts", [128, 5, 128], BF16)

                def cb_qa(ti, t0, tn, banks):
                    for (ps, co, w) in banks:
                        copy(ev_eng(), qa_sb.t[0:tn, co:co + w], ps.t[0:tn, 0:w], [ps], [qa_sb])
                    rs_of((sqt, ss), qa_sb, QL, tn)
                    P.add("dve", lambda e: e.scalar_tensor_tensor(out=cq.t[0:tn, :], in0=qa_sb.t[0:tn, :], scalar=ss.t[0:tn, 0:1],
                                                                  in1=nqa_b.t[0:tn, :], op0=ALU.mult, op1=ALU.mult),
                          reads=[qa_sb, ss, nqa_b], writes=[cq])
                    tr_to(lambda j, ps, ap: copy(ev_eng(), cqs.t[:, j, 0:tn], ap, [ps], [cqs]), cq,
                          [(j * 128, 128) for j in range(6)], tn, [cq], [cqs])
                    P.add("sp", lambda e: e.dma_start(out=v3(cqT_d, NTMAX)[:, :, t0:t0 + tn], in_=cqs.t[:, :, 0:tn]),
                          reads=[cqs], writes=[cqT_d], dma=True)
                if not prefix:
                    lin_tm(hT, O_QA, QL, tts, cb_qa)

                def cb_kv(ti, t0, tn, banks):
                    for (ps, co, w) in banks:
                        copy(ev_eng(), kv_sb.t[0:tn, co:co + w], ps.t[0:tn, 0:w], [ps], [kv_sb])
                    rs_of((sqt, ss), kv_sb, KVL, tn)
                    P.add("dve", lambda e: e.scalar_tensor_tensor(out=lat.t[0:tn, 0:KVL], in0=kv_sb.t[0:tn, 0:KVL],
                                                                  scalar=ss.t[0:tn, 0:1], in1=nkv_b.t[0:tn, :],
                                                                  op0=ALU.mult, op1=ALU.mult),
                          reads=[kv_sb, ss, nkv_b], writes=[lat])
                    if t0 < QT:
                        P.add("sp", lambda e: e.dma_start(out=rtk.t[0:tn, :], in_=rope_tok[q * QT + t0:q * QT + t0 + tn, :]),
                              writes=[rtk], dma=True)
                    else:
                        for j in range(NS):
                            P.add("sp", lambda e, j=j: e.dma_start(out=rtk.t[j * TS:(j + 1) * TS, :], in_=rope_tok[SEQ:SEQ + TS, :]),
                                  writes=[rtk], dma=True)

                    def rp(e):
                        e.tensor_tensor(out=rt1.t[0:tn, :], in0=kv_sb.t[0:tn, 512:576], in1=rtk.t[0:tn, 0:64], op=ALU.mult)
                        e.tensor_tensor(out=rt2.t[0:tn, 0:32], in0=kv_sb.t[0:tn, 544:576], in1=rtk.t[0:tn, 64:96], op=ALU.mult)
                        e.tensor_tensor(out=rt2.t[0:tn, 32:64], in0=kv_sb.t[0:tn, 512:544], in1=rtk.t[0:tn, 96:128], op=ALU.mult)
                        return e.tensor_tensor(out=lat.t[0:tn, 512:576], in0=rt1.t[0:tn, :], in1=rt2.t[0:tn, :], op=ALU.add)
                    P.add("dve", rp, reads=[kv_sb, rtk], writes=[lat, rt1, rt2])
                    if t0 < QT:
                        if not prefix:
                            P.add("sp", lambda e: e.dma_start(out=o_plat[t0:t0 + tn, :], in_=lat.t[0:tn, 0:KVL]), reads=[lat], dma=True)
                            P.add("sp", lambda e: e.dma_start(out=o_pkr[t0:t0 + tn, :], in_=lat.t[0:tn, 512:576]), reads=[lat], dma=True)
                    else:
                        P.add("sp", lambda e: e.dma_start(out=o_slat[:, :], in_=lat.t[0:tn, 0:KVL]), reads=[lat], dma=True)
                        P.add("sp", lambda e: e.dma_start(out=o_skr[:, :], in_=lat.t[0:tn, 512:576]), reads=[lat], dma=True)
                    pieces = [(j * 128, 128) for j in range(4)] + [(512, 64)]
                    tr_to(lambda j, ps, ap: copy(ev_eng(), lts.t[0:pieces[j][1], j, 0:tn], ap, [ps], [lts]), lat, pieces, tn,
                          [lat], [lts])
                    if t0 < QT:
                        r0 = q * QT + t0
                        P.add("sp", lambda e: e.dma_start(out=v3(latT_d, SEQ)[:, :, r0:r0 + tn], in_=lts.t[:, :, 0:tn]),
                              reads=[lts], writes=[latT_d], dma=True)
                    else:
                        P.add("sp", lambda e: e.dma_start(out=v3(latS_d, NS * TS)[:, :, :], in_=lts.t[:, :, 0:tn]),
                              reads=[lts], writes=[latS_d], dma=True)
                lin_tm(hT, O_KV, 576, tts, cb_kv)

                zst = [sbp(phB, "zst", [128, 512]) for i in range(2)]
                zc = [0]
                for pc in (range(8) if not prefix else ()):
                    def cb_z(ti, t0, tn, banks, pc=pc):
                        (ps, co, w) = banks[0]
                        zb = zst[zc[0] % 2]
                        zc[0] += 1
                        copy(ev_eng(), zb.t[0:tn, :], ps.t[0:tn, 0:512], [ps], [zb])
                        P.add("sp", lambda e: e.dma_start(out=z_d.t[t0:t0 + tn, pc * 512:(pc + 1) * 512], in_=zb.t[0:tn, :]),
                              reads=[zb], writes=[z_d], dma=True)
                    lin_tm(hT, O_Z + pc * 512, 512, tts, cb_z)

                dts = sbp(phB, "dts", [128, HS])
                vcol = sbp(phB, "vcol", [128, 1])

                def cb_dt(ti, t0, tn, banks):
                    (ps, co, w) = banks[0]
                    P.add("dve", lambda e: e.tensor_tensor(out=dts.t[0:tn, :], in0=ps.t[0:tn, 0:HS], in1=dtb_b.t[0:tn, :], op=ALU.add),
                          reads=[ps, dtb_b], writes=[dts])
                    P.add("act", lambda e: e.activation(out=dts.t[0:tn, :], in_=dts.t[0:tn, :], func=AF.Exp), reads=[dts], writes=[dts])
                    P.add("act", lambda e: e.activation(out=dts.t[0:tn, :], in_=dts.t[0:tn, :], func=AF.Ln, bias=onesf.t[0:tn, 0:1]),
                          reads=[dts, onesf], writes=[dts])
                    if t0 < QT:
                        P.add("sp", lambda e: e.dma_start(out=vcol.t[0:tn, :], in_=valid_in[q * QT + t0:q * QT + t0 + tn, :]), writes=[vcol], dma=True)
                        P.add("dve", lambda e: e.tensor_scalar_mul(out=dts.t[0:tn, :], in0=dts.t[0:tn, :], scalar1=vcol.t[0:tn, 0:1]), reads=[dts, vcol], writes=[dts])
                    P.add("sp", lambda e: e.dma_start(out=dt_d.t[t0:t0 + tn, :], in_=dts.t[0:tn, :]), reads=[dts], writes=[dt_d], dma=True)
                lin_tm(hT, O_DT, HS, tts, cb_dt)

                halo_s = sbp(phB, "halo_s", [128, 48, NS * 3])
                if last:
                    sct = sbp(phB, "sct", [NS * 3, CONV])
                    P.add("sp", lambda e: e.dma_start(out=sct.t[:, :], in_=s_conv_in), writes=[sct], dma=True)
                    tr_to(lambda j, ps, ap: copy(ev_eng(), halo_s.t[:, j, :], ap, [ps], [halo_s]), sct,
                          [(j * 128, 128) for j in range(48)], NS * 3, [sct], [halo_s])
                xb = [sbp(phB, "xb", [128, 3 + QT]) for i in range(2)]
                xbs = [sbp(phB, "xbs", [128, NS, 3 + TS]) for i in range(2)]
                accs = [sbp(phB, "acc", [128, NTMAX]) for i in range(2)]
                fmb = [sbp(phB, "fmb", [128, NTMAX], BF16) for i in range(2)]
                xsts = [sbp(phB, "xst", [128, 9, 512], BF16) for i in range(2)]
                deferred = [None]
                for ct in range(48):
                    wb, wv = load_w(W["w_in"], 0, D, O_XBC + ct * 128, 128)
                    X, XS, FB = xb[ct % 2], xbs[ct % 2], fmb[ct % 2]
                    acc = accs[ct % 2]
                    P.add("pool", lambda e, X=X, ct=ct: e.tensor_copy(out=X.t[:, 0:3], in_=halo.t[:, ct, :]), reads=[halo], writes=[X])
                    if last:
                        P.add("pool", lambda e, XS=XS, ct=ct: e.tensor_copy(
                            out=XS.t[:, :, 0:3], in_=halo_s.t[:, ct, :].rearrange("p (j k) -> p j k", k=3)),
                            reads=[halo_s], writes=[XS])
                    for (b0, bw) in blks:
                        ps = next_ps()

                        def mm(e, wv=wv, ps=ps, b0=b0, bw=bw):
                            ins = None
                            for k in range(DC):
                                ins = e.matmul(ps.t[:, 0:bw], lhsT=wv[:, k, :], rhs=hT.t[:, k, b0:b0 + bw],
                                               start=(k == 0), stop=(k == DC - 1))
                            return ins
                        P.add("pe", mm, reads=[wb, hT], writes=[ps])
                        if b0 < QT:
                            copy(ev_eng(), X.t[:, 3 + b0:3 + b0 + bw], ps.t[:, 0:bw], [ps], [X])
                        else:
                            copy(ev_eng(), XS.t[:, :, 3:3 + TS], ps.t[:, 0:bw].rearrange("p (j t) -> p j t", t=TS), [ps], [XS])

                    def conv(e, X=X, XS=XS, ct=ct, acc=acc):
                        ins = e.tensor_scalar(out=acc.t[:, 0:QT], in0=X.t[:, 0:QT], scalar1=cwc.t[:, ct, 0:1],
                                              scalar2=cwc.t[:, ct, 4:5], op0=ALU.mult, op1=ALU.add)
                        for k in range(1, 4):
                            ins = e.scalar_tensor_tensor(out=acc.t[:, 0:QT], in0=X.t[:, k:k + QT], scalar=cwc.t[:, ct, k:k + 1],
                                                         in1=acc.t[:, 0:QT], op0=ALU.mult, op1=ALU.add)
                        if last:
                            a3 = acc.t[:, QT:NTMAX].rearrange("p (j t) -> p j t", t=TS)
                            ins = e.tensor_scalar(out=a3, in0=XS.t[:, :, 0:TS], scalar1=cwc.t[:, ct, 0:1],
                                                  scalar2=cwc.t[:, ct, 4:5], op0=ALU.mult, op1=ALU.add)
                            for k in range(1, 4):
                                ins = e.scalar_tensor_tensor(out=a3, in0=XS.t[:, :, k:k + TS], scalar=cwc.t[:, ct, k:k + 1],
                                                             in1=a3, op0=ALU.mult, op1=ALU.add)
                        e.tensor_copy(out=halo.t[:, ct, :], in_=X.t[:, QT:QT + 3])
                        ins = e.tensor_copy(out=convout.t[:, ct, 0:3], in_=X.t[:, QT:QT + 3])
                        if last:
                            ins = e.tensor_copy(out=convout.t[:, ct, 3:15].rearrange("p (j k) -> p j k", k=3),
                                                in_=XS.t[:, :, TS:TS + 3])
                        return ins
                    P.add("dve", conv, reads=[X, XS, cwc], writes=[acc, halo, convout])
                    if ct >= 32:
                        P.add("act", lambda e, FB=FB, acc=acc: e.activation(out=FB.t[:, 0:nt], in_=acc.t[:, 0:nt], func=AF.Silu), reads=[acc], writes=[FB])
                    if ct < 40:
                        P.add("act", lambda e, acc=acc: e.activation(out=acc.t[:, 0:nt], in_=acc.t[:, 0:nt], func=AF.Silu), reads=[acc], writes=[acc])
                    if ct >= 32:
                        P.add("sp", lambda e, FB=FB, ct=ct: e.dma_start(out=v3(bcT_d, NTMAX)[:, ct - 32, 0:nt], in_=FB.t[:, 0:nt]),
                              reads=[FB], writes=[bcT_d], dma=True)
                    if deferred[0] is not None:
                        deferred[0]()
                        deferred[0] = None
                    if ct < 40:
                        def emit_tr(ct=ct, acc=acc):
                            XST = xsts[(ct // 4) % 2]
                            for g0 in range(0, len(tts), 4):
                                ps = next_ps()
                                grp = tts[g0:g0 + 4]

                                def tr(e, ps=ps, grp=grp, acc=acc):
                                    ins = None
                                    for j, (t0, tn) in enumerate(grp):
                                        ins = e.transpose(out=ps.t[0:tn, j * 128:(j + 1) * 128], in_=acc.t[:, t0:t0 + tn],
                                                          identity=ident.t[:, :])
                                    return ins
                                P.add("pe", tr, reads=[acc, ident], writes=[ps])
                                if all(tn == 128 for (_, tn) in grp):
                                    ng = len(grp)
                                    copy(ev_eng(), XST.t[:, g0:g0 + ng, (ct % 4) * 128:(ct % 4 + 1) * 128],
                                         ps.t[:, 0:ng * 128].rearrange("p (j c) -> p j c", c=128), [ps], [XST])
                                else:
                                    for j, (t0, tn) in enumerate(grp):
                                        copy(ev_eng(), XST.t[0:tn, g0 + j, (ct % 4) * 128:(ct % 4 + 1) * 128],
                                             ps.t[0:tn, j * 128:(j + 1) * 128], [ps], [XST])
                            if ct % 4 == 3:
                                c0 = (ct // 4) * 512
                                for ti, (t0, tn) in enumerate(tts):
                                    if ct < 32:
                                        P.add("sp", lambda e, ti=ti, t0=t0, tn=tn, c0=c0: e.dma_start(
                                            out=xs_d.t[t0:t0 + tn, c0:c0 + 512], in_=XST.t[0:tn, ti, :]), reads=[XST], writes=[xs_d], dma=True)
                                    else:
                                        P.add("sp", lambda e, ti=ti, t0=t0, tn=tn, c0=c0: e.dma_start(
                                            out=b_d.t[t0:t0 + tn, c0 - DI:c0 - DI + 512], in_=XST.t[0:tn, ti, :]), reads=[XST], writes=[b_d], dma=True)
                        deferred[0] = emit_tr
                if deferred[0] is not None:
                    deferred[0]()
                    deferred[0] = None

                gst = [sbp(phB, "gst", [128, NTMAX]) for i in range(2)]
                for ct in (range(32) if not prefix else ()):
                    wb, wv = load_w(W["w_in"], 0, D, O_G + ct * 128, 128)
                    G = gst[ct % 2]
                    for (b0, bw) in blks:
                        ps = next_ps()

                        def mm(e, wv=wv, ps=ps, b0=b0, bw=bw):
                            ins = None
                            for k in range(DC):
                                ins = e.matmul(ps.t[:, 0:bw], lhsT=wv[:, k, :], rhs=hT.t[:, k, b0:b0 + bw],
                                               start=(k == 0), stop=(k == DC - 1))
                            return ins
                        P.add("pe", mm, reads=[wb, hT], writes=[ps])
                        copy(ev_eng(), G.t[:, b0:b0 + bw], ps.t[:, 0:bw], [ps], [G])
                    P.add("sp", lambda e, G=G, ct=ct: e.dma_start(out=v3(gT_d, NTMAX)[:, ct, 0:nt], in_=G.t[:, 0:nt]),
                          reads=[G], writes=[gT_d], dma=True)
                if last:
                    cst = next_stage()
                    for g0 in range(0, 48, 4):
                        ps = next_ps()

                        def tr(e, ps=ps, g0=g0):
                            ins = None
                            for j in range(4):
                                ins = e.transpose(out=ps.t[0:15, j * 128:(j + 1) * 128], in_=convout.t[:, g0 + j, 0:15],
                                                  identity=ident.t[:, :])
                            return ins
                        P.add("pe", tr, reads=[convout, ident], writes=[ps])
                        half = (g0 // 16)
                        if g0 % 16 == 0 and g0 > 0:
                            pass
                        copy(ev_eng(), cst.t[0:15, (g0 % 16) * 128:(g0 % 16) * 128 + 512], ps.t[0:15, 0:512], [ps], [cst])
                        if g0 % 16 == 12:
                            cc0 = half * 2048
                            P.add("sp", lambda e, cc0=cc0, cst=cst: e.dma_start(out=o_pconv[:, cc0:cc0 + 2048], in_=cst.t[0:3, :]), reads=[cst], dma=True)
                            P.add("sp", lambda e, cc0=cc0, cst=cst: e.dma_start(out=o_sconv[:, cc0:cc0 + 2048], in_=cst.t[3:15, :]), reads=[cst], dma=True)
                            cst = next_stage()
        P.fence()

    cst_p = din("cst_p", [128, 128 + 128 + 1])
    cst_s = din("cst_s", [128, 128 + 128 + 4])

    def attend(kh, krT, vh, qn_ap, qr_ap, ncols, ktiles, out_ap, pT, rden, masked=False):
        pacc = pacc_ref[0]
        po, pd = psb[6], psb[7]
        nk_t = len(ktiles)
        pss = {}

        def emit_sc(i):
            (k0, kn, c_lo, diag) = ktiles[i]
            ps = next_ps()
            pss[i] = ps

            def sc(e, ps=ps, k0=k0, kn=kn, c_lo=c_lo):
                e.matmul(ps.t[0:kn, c_lo:ncols], lhsT=kh.t[:, k0:k0 + kn], rhs=qn_ap[:, c_lo:ncols], start=True, stop=False)
                return e.matmul(ps.t[0:kn, c_lo:ncols], lhsT=krT[0:64, k0:k0 + kn], rhs=qr_ap[0:64, c_lo:ncols], start=False, stop=True)
            P.add("pe", sc, reads=attn_reads, writes=[ps])

        def emit_rest(i):
            (k0, kn, c_lo, diag) = ktiles[i]
            ps = pss.pop(i)
            pt = pT[i % len(pT)]

            def ex(e, ps=ps, pt=pt, kn=kn, c_lo=c_lo, diag=diag, kb=(k0 // 128 if (masked and not diag) else None)):
                if diag:
                    e.activation(out=pt.t[0:kn, c_lo:c_lo + 64], in_=ps.t[0:kn, c_lo:c_lo + 64], func=AF.Exp, scale=SCALE,
                                 bias=negm.t[0:kn, 0:1])
                    return e.activation(out=pt.t[0:kn, c_lo + 64:ncols], in_=ps.t[0:kn, c_lo + 64:ncols], func=AF.Exp, scale=SCALE)
                if kb is not None:
                    return e.activation(out=pt.t[0:kn, c_lo:ncols], in_=ps.t[0:kn, c_lo:ncols], func=AF.Exp, scale=SCALE,
                                        bias=kneg.t[0:kn, kb:kb + 1])
                return e.activation(out=pt.t[0:kn, c_lo:ncols], in_=ps.t[0:kn, c_lo:ncols], func=AF.Exp, scale=SCALE)
            P.add("act", ex, reads=[ps, negm, kneg], writes=[pt])

            def pv(e, pt=pt, i=i, k0=k0, kn=kn, c_lo=c_lo):
                return e.matmul(po.t[:, c_lo:ncols], lhsT=vh.t[0:kn, k0 // 128, :], rhs=pt.t[0:kn, c_lo:ncols], start=(i == 0), stop=(i == nk_t - 1))
            P.add("pe", pv, reads=[pt, vh], writes=[po])
            if i == 0:
                P.add("dve", lambda e, pt=pt, kn=kn: e.tensor_copy(out=pacc.t[0:kn, 0:ncols], in_=pt.t[0:kn, 0:ncols]), reads=[pt], writes=[pacc])
            else:
                P.add("dve", lambda e, pt=pt, kn=kn, c_lo=c_lo: e.tensor_tensor(out=pacc.t[0:kn, c_lo:ncols], in0=pacc.t[0:kn, c_lo:ncols],
                                                                           in1=pt.t[0:kn, c_lo:ncols], op=ALU.add), reads=[pt, pacc], writes=[pacc])

        LOOK = 2
        for i in range(min(LOOK, nk_t)):
            emit_sc(i)
        for i in range(nk_t):
            if i + LOOK < nk_t:
                emit_sc(i + LOOK)
            emit_rest(i)
        P.add("pe", lambda e: e.matmul(pd.t[:, 0:ncols], lhsT=onesf.t[:, :], rhs=pacc.t[:, 0:ncols], start=True, stop=True), reads=[pacc, onesf], writes=[pd])
        P.add("dve", lambda e: e.reciprocal(out=pacc.t[:, 0:ncols], in_=pd.t[:, 0:ncols]), reads=[pd], writes=[pacc], hz=True)
        P.add("dve", lambda e: e.tensor_tensor(out=out_ap, in0=po.t[:, 0:ncols], in1=pacc.t[:, 0:ncols], op=ALU.mult),
              reads=[po, pacc], writes=[oT_all_ref[0]])

    attn_reads = []
    pacc_ref = [None]
    kneg = sb("kneg", [128, SEQ // 128])
    P.add("sp", lambda e: e.dma_start(out=kneg[:], in_=kneg_in), writes=[kneg], dma=True)
    oT_all_ref = [None]

    def phase2_attn(q, last):
        nt = QT + (NS * TS if last else 0)
        blks = blocks_of(nt)
        nk = (q + 1) * QT
        with contextlib.ExitStack() as ph:
            cqT = sbp(ph, "cqT", [128, 6, NTMAX], BF16)
            latT = sbp(ph, "latT", [128, 5, nk], BF16)
            cosT = sbp(ph, "cosT", [64, NTMAX])
            sinT = sbp(ph, "sinT", [64, NTMAX])
            qn = sbp(ph, "qn", [128, NTMAX], BF16)
            qr = sbp(ph, "qr", [64, NTMAX], BF16)
            wsw = sbp(ph, "wsw", [128, 6, 64], BF16)
            kh = sbp(ph, "kh", [128, SEQ], BF16)
            vh = sbp(ph, "vh", [128, SEQ // 128, 128], BF16)
            pT = [sbp(ph, "pT", [128, 512], BF16) for i in range(3)]
            rden = None
            pacc = sbp(ph, "pacc", [128, 512])
            pacc_ref[0] = pacc
            t1 = sbp(ph, "t1", [64, 512])
            t2 = sbp(ph, "t2", [64, 512])
            oT_all = sbp(ph, "oT_all", [128, DC, NTMAX], BF16)
            qs_n = sbp(ph, "qs_n", [128, NH, NS * TS], BF16)
            qs_r = sbp(ph, "qs_r", [64, NH, NS * TS], BF16)
            oT_all_ref[0] = oT_all
            attn_reads[:] = [kh, latT, qn, qr]
            P.add("sp", lambda e: e.dma_start(out=cqT.t[:, :, 0:nt], in_=v3(cqT_d, NTMAX)[:, :, 0:nt]), reads=[cqT_d], writes=[cqT], dma=True)
            P.add("sp", lambda e: e.dma_start(out=latT.t[:, :, :], in_=v3(latT_d, SEQ)[:, :, 0:nk]), reads=[latT_d], writes=[latT], dma=True)
            P.add("sp", lambda e: e.dma_start(out=cosT.t[:, 0:QT], in_=ropeT[0, :, q * QT:(q + 1) * QT]), writes=[cosT], dma=True)
            P.add("sp", lambda e: e.dma_start(out=sinT.t[:, 0:QT], in_=ropeT[1, :, q * QT:(q + 1) * QT]), writes=[sinT], dma=True)
            if last:
                for j in range(NS):
                    P.add("sp", lambda e, j=j: e.dma_start(out=cosT.t[:, QT + j * TS:QT + (j + 1) * TS], in_=ropeT[0, :, SEQ:SEQ + TS]), writes=[cosT], dma=True)
                    P.add("sp", lambda e, j=j: e.dma_start(out=sinT.t[:, QT + j * TS:QT + (j + 1) * TS], in_=ropeT[1, :, SEQ:SEQ + TS]), writes=[sinT], dma=True)

            def kv_for_head(wkb, wkv, srcT, nkeys):
                for (k0, kw) in blocks_of(nkeys):
                    ps = next_ps()

                    def mm(e, ps=ps, k0=k0, kw=kw):
                        ins = None
                        for k in range(4):
                            ins = e.matmul(ps.t[:, 0:kw], lhsT=wkv[:, k, 0:128], rhs=srcT.t[:, k, k0:k0 + kw], start=(k == 0), stop=(k == 3))
                        return ins
                    P.add("pe", mm, reads=[wkb, srcT], writes=[ps])
                    copy(ev_eng(), kh.t[:, k0:k0 + kw], ps.t[:, 0:kw], [ps], [kh])
                kts = blocks_of(nkeys, 128)
                for g0 in range(0, len(kts), 4):
                    ps = next_ps()
                    grp = kts[g0:g0 + 4]

                    def mv(e, ps=ps, grp=grp):
                        ins = None
                        for j, (k0, kn) in enumerate(grp):
                            for k in range(4):
                                ins = e.matmul(ps.t[0:kn, j * 128:(j + 1) * 128], lhsT=srcT.t[:, k, k0:k0 + kn], rhs=wkv[:, k, 128:256],
                                               start=(k == 0), stop=(k == 3))
                        return ins
                    P.add("pe", mv, reads=[wkb, srcT], writes=[ps])
                    for j, (k0, kn) in enumerate(grp):
                        copy(ev_eng(), vh.t[0:kn, k0 // 128, :], ps.t[0:kn, j * 128:(j + 1) * 128], [ps], [vh])

            for h in range(NH):
                wqb, wqv = load_w(W["w_qb"], 0, QL, h * 192, 192)
                wkb, wkv = load_w(W["w_kvb"], 0, KVL, h * 256, 256)
                P.add("pool", lambda e, wqv=wqv: e.tensor_copy(out=wsw.t[:, :, 0:32], in_=wqv[:, :, 160:192]), reads=[wqb], writes=[wsw])
                P.add("pool", lambda e, wqv=wqv: e.tensor_copy(out=wsw.t[:, :, 32:64], in_=wqv[:, :, 128:160]), reads=[wqb], writes=[wsw])
                for (b0, bw) in blks:
                    pn, pa, pb_ = next_ps(), next_ps(), next_ps()

                    def mq(e, wqv=wqv, pn=pn, pa=pa, pb_=pb_, b0=b0, bw=bw):
                        ins = None
                        for k in range(6):
                            ins = e.matmul(pn.t[:, 0:bw], lhsT=wqv[:, k, 0:128], rhs=cqT.t[:, k, b0:b0 + bw], start=(k == 0), stop=(k == 5))
                        for k in range(6):
                            ins = e.matmul(pa.t[0:64, 0:bw], lhsT=wqv[:, k, 128:192], rhs=cqT.t[:, k, b0:b0 + bw], start=(k == 0), stop=(k == 5))
                        for k in range(6):
                            ins = e.matmul(pb_.t[0:64, 0:bw], lhsT=wsw.t[:, k, :], rhs=cqT.t[:, k, b0:b0 + bw], start=(k == 0), stop=(k == 5))
                        return ins
                    P.add("pe", mq, reads=[wqb, wsw, cqT], writes=[pn, pa, pb_])
                    copy("act", qn.t[:, b0:b0 + bw], pn.t[:, 0:bw], [pn], [qn])

                    def rp(e, pa=pa, pb_=pb_, b0=b0, bw=bw):
                        e.tensor_tensor(out=t1.t[:, 0:bw], in0=pa.t[0:64, 0:bw], in1=cosT.t[:, b0:b0 + bw], op=ALU.mult)
                        e.tensor_tensor(out=t2.t[:, 0:bw], in0=pb_.t[0:64, 0:bw], in1=sinT.t[:, b0:b0 + bw], op=ALU.mult)
                        return e.tensor_tensor(out=qr.t[:, b0:b0 + bw], in0=t1.t[:, 0:bw], in1=t2.t[:, 0:bw], op=ALU.add)
                    P.add("dve", rp, reads=[pa, pb_, cosT, sinT], writes=[qr, t1, t2])
                if last:
                    P.add("pool", lambda e, h=h: e.tensor_copy(out=qs_n.t[:, h, :], in_=qn.t[:, QT:NTMAX]), reads=[qn], writes=[qs_n])
                    P.add("pool", lambda e, h=h: e.tensor_copy(out=qs_r.t[:, h, :], in_=qr.t[:, QT:NTMAX]), reads=[qr], writes=[qs_r])
                kv_for_head(wkb, wkv, latT, nk)
                for qb in range(2):
                    q0 = qb * 512
                    base_kt = (q * QT + q0) // 128
                    ktiles = []
                    for kt in range(base_kt + 4):
                        j = kt - base_kt
                        ktiles.append((kt * 128, 128, 128 * j if j > 0 else 0, j >= 0))
                    attend(kh, latT.t[:, 4, :], vh, qn.t[:, q0:q0 + 512], qr.t[:, q0:q0 + 512], 512, ktiles,
                           oT_all.t[:, h, q0:q0 + 512], pT, rden, masked=True)
            if last:
                NKS = PAST + TS
                with contextlib.ExitStack() as ph2:
                    latS = sbp(ph2, "latS", [128, 5, NKS], BF16)
                    cst = [sbp(ph2, "cstg", [128, 576]) for i in range(2)]
                    attn_reads[:] = [kh, latS, qs_n, qs_r]
                    for j in range(NS):
                        for kt in range(PAST // 128):
                            cs_ = cst[kt % 2]
                            P.add("sp", lambda e, cs_=cs_, j=j, kt=kt: e.dma_start(out=cs_.t[:, 0:KVL], in_=c_lat[j, kt * 128:(kt + 1) * 128, :]), writes=[cs_], dma=True)
                            P.add("sp", lambda e, cs_=cs_, j=j, kt=kt: e.dma_start(out=cs_.t[:, KVL:576], in_=c_kr[j, kt * 128:(kt + 1) * 128, :]), writes=[cs_], dma=True)
                            pieces = [(c * 128, 128) for c in range(4)] + [(512, 64)]
                            tr_to(lambda c, ps, ap, kt=kt: copy(ev_eng(), latS.t[0:pieces[c][1], c, kt * 128:(kt + 1) * 128], ap, [ps], [latS]),
                                  cs_, pieces, 128, [cs_], [latS])
                        P.add("sp", lambda e, j=j: e.dma_start(out=latS.t[:, :, PAST:NKS], in_=v3(latS_d, NS * TS)[:, :, j * TS:(j + 1) * TS]),
                              reads=[latS_d], writes=[latS], dma=True)
                        for h in range(NH):
                            wkb, wkv = load_w(W["w_kvb"], 0, KVL, h * 256, 256)
                            kv_for_head(wkb, wkv, latS, NKS)
                            ktiles = [(k0, kn, 0, False) for (k0, kn) in blocks_of(NKS, 128)]
                            attend(kh, latS.t[:, 4, :], vh, qs_n.t[:, h, j * TS:(j + 1) * TS], qs_r.t[:, h, j * TS:(j + 1) * TS], TS, ktiles,
                                   oT_all.t[:, h, QT + j * TS:QT + (j + 1) * TS], pT, rden)
            P.add("sp", lambda e: e.dma_start(out=v3(oT_d, NTMAX)[:, :, 0:nt], in_=oT_all.t[:, :, 0:nt]), reads=[oT_all], writes=[oT_d], dma=True)
        P.fence()

    def phase2_ssd(q, first, last):
        prefix = not last
        with contextlib.ExitStack() as ph:
            S32 = sbp(ph, "S32", [128, DI])
            S16 = sbp(ph, "S16", [128, DI], BF16)
            cp = sbp(ph, "cp", [128, 257])
            cs = sbp(ph, "cs", [128, 260])
            nssm_b = sbp(ph, "nssm_b", [128, DI])
            xs_t = sbp(ph, "xs_t", [128, DI], BF16)
            b_t = sbp(ph, "b_t", [128, NG * NST], BF16)
            dt_t = sbp(ph, "dt_t", [128, HS])
            a_t = sbp(ph, "a_t", [128, HS])
            am = sbp(ph, "am", [128, HS])
            te = sbp(ph, "te", [128, HS])
            et = sbp(ph, "et", [128, HS])
            decb = sbp(ph, "decb", [128, HS])
            BT = sbp(ph, "BT", [128, NG, 128], BF16)
            CT = sbp(ph, "CT", [128, NG, 128], BF16)
            xdt = sbp(ph, "xdt", [128, DI], BF16)
            xw = sbp(ph, "xw", [128, DI], BF16)
            xwm = sbp(ph, "xwm", [128, DI], BF16)
            AV2 = sbp(ph, "AV2", [128, 16 * 128])
            MT2 = sbp(ph, "MT2", [128, HS * 128], BF16)
            CBm = sbp(ph, "CBm", [128, NG * 128])
            y_sb = sbp(ph, "y_sb", [128, DI])
            tmp = sbp(ph, "tmp", [128, 512])
            zt = [sbp(ph, "zt", [128, 512]) for i in range(2)]
            ssg = sbp(ph, "ssg", [128, NG])
            ynst = sbp(ph, "ynst", [128, 32, 128], BF16)
            P.add("sp", lambda e: e.dma_start(out=cp.t[:, :], in_=cst_p), writes=[cp], dma=True)
            P.add("sp", lambda e: e.dma_start(out=cs.t[:, :], in_=cst_s), writes=[cs], dma=True)
            bc_load(nssm_b, W["n_ssm"], DI)
            if first:
                P.add("dve", lambda e: e.memset(S32.t[:, :], 0.0), writes=[S32])
            else:
                P.add("sp", lambda e: e.dma_start(out=S32.t[:, :], in_=S_d.t[:, :]), reads=[S_d], writes=[S32], dma=True)
            P.add("act", lambda e: e.activation(out=S16.t[:, :], in_=S32.t[:, :], func=AF.Copy), reads=[S32], writes=[S16])

            def state_in(src2d):
                for hb in range(2):
                    stg = next_stage()
                    P.add("sp", lambda e, stg=stg, hb=hb: e.dma_start(
                        out=stg.t[:, :].rearrange("p (b n) -> p b n", n=128),
                        in_=src2d[hb * 2048:(hb + 1) * 2048, :].rearrange("(b p) n -> p b n", p=128)), writes=[stg], dma=True)
                    tr_to(lambda b, ps, ap, hb=hb: copy(ev_eng(), S32.t[:, (hb * 16 + b) * 128:(hb * 16 + b + 1) * 128], ap, [ps], [S32]),
                          stg, [(b * 128, 128) for b in range(16)], 128, [stg], [S32])
                P.add("act", lambda e: e.activation(out=S16.t[:, :], in_=S32.t[:, :], func=AF.Copy), reads=[S32], writes=[S16])

            def state_out(dst2d):
                for hb in range(2):
                    stg = next_stage()
                    tr_to(lambda b, ps, ap, stg=stg: copy(ev_eng(), stg.t[:, b * 128:(b + 1) * 128], ap, [ps], [stg]),
                          Buf(S32.t[:, hb * 2048:(hb + 1) * 2048], "S32v"), [(b * 128, 128) for b in range(16)], 128, [S32], [stg])
                    P.add("sp", lambda e, stg=stg, hb=hb: e.dma_start(
                        out=dst2d[hb * 2048:(hb + 1) * 2048, :].rearrange("(b p) n -> p b n", p=128),
                        in_=stg.t[:, :].rearrange("p (b n) -> p b n", n=128)), reads=[stg], dma=True)

            tiles = [(i * 128, 128, cp, 1, 256) for i in range(QT // 128)]
            if last:
                tiles.append((QT, NS * TS, cs, NS, 256))
            for (t0, R, C, nseg, mo) in tiles:
                sample = (t0 >= QT)
                if sample:
                    state_out(o_pssm)
                tri2 = C.t[0:R, 0:R]
                us2 = C.t[0:R, 128:128 + R]
                P.add("sp", lambda e, t0=t0, R=R: e.dma_start(out=xs_t.t[0:R, :], in_=xs_d.t[t0:t0 + R, :]), reads=[xs_d], writes=[xs_t], dma=True)
                P.add("sp", lambda e, t0=t0, R=R: e.dma_start(out=b_t.t[0:R, :], in_=b_d.t[t0:t0 + R, :]), reads=[b_d], writes=[b_t], dma=True)
                P.add("sp", lambda e, t0=t0, R=R: e.dma_start(out=dt_t.t[0:R, :], in_=dt_d.t[t0:t0 + R, :]), reads=[dt_d], writes=[dt_t], dma=True)
                P.add("sp", lambda e, t0=t0, R=R: e.dma_start(out=BT.t[:, :, 0:R], in_=v3(bcT_d, NTMAX)[:, 0:8, t0:t0 + R]), reads=[bcT_d], writes=[BT], dma=True)
                P.add("sp", lambda e, t0=t0, R=R: e.dma_start(out=CT.t[:, :, 0:R], in_=v3(bcT_d, NTMAX)[:, 8:16, t0:t0 + R]), reads=[bcT_d], writes=[CT], dma=True)
                P.add("dve", lambda e, R=R: e.tensor_tensor(out=a_t.t[0:R, :], in0=dt_t.t[0:R, :], in1=A_b.t[0:R, :], op=ALU.mult), reads=[dt_t, A_b], writes=[a_t])
                x3 = lambda b, R=R: b.t[0:R, :].rearrange("p (r c) -> p r c", c=64)
                P.add("dve", lambda e, R=R, x3=x3: e.tensor_tensor(out=x3(xdt), in0=x3(xs_t), in1=dt_t.t[0:R, :].unsqueeze(2).broadcast_to([R, HS, 64]), op=ALU.mult),
                      reads=[xs_t, dt_t], writes=[xdt])
                if not prefix:
                    P.add("dve", lambda e, R=R, x3=x3: e.tensor_tensor(out=x3(y_sb), in0=x3(xs_t), in1=dsk_b.t[0:R, :].unsqueeze(2).broadcast_to([R, HS, 64]), op=ALU.mult),
                          reads=[xs_t, dsk_b], writes=[y_sb])
                for (dst, msk) in ((te, us2), (et, tri2)):
                    ps = next_ps()
                    P.add("pe", lambda e, ps=ps, msk=msk, R=R: e.matmul(ps.t[0:R, 0:HS], lhsT=msk, rhs=a_t.t[0:R, :], start=True, stop=True), reads=[a_t, C], writes=[ps])
                    P.add("act", lambda e, ps=ps, dst=dst, R=R: e.activation(out=dst.t[0:R, :], in_=ps.t[0:R, 0:HS], func=AF.Exp), reads=[ps], writes=[dst])
                P.add("dve", lambda e, R=R, x3=x3: e.tensor_tensor(out=x3(xw), in0=x3(xdt), in1=te.t[0:R, :].unsqueeze(2).broadcast_to([R, HS, 64]), op=ALU.mult),
                      reads=[xdt, te], writes=[xw])
                if not prefix:
                    gpb = 512 // R
                    for g0 in range(0, NG, gpb):
                        ps = next_ps()

                        def mcb(e, ps=ps, g0=g0, R=R, gpb=gpb):
                            ins = None
                            for gi in range(gpb):
                                ins = e.matmul(ps.t[0:R, gi * R:(gi + 1) * R], lhsT=BT.t[:, g0 + gi, 0:R], rhs=CT.t[:, g0 + gi, 0:R], start=True, stop=True)
                            return ins
                        P.add("pe", mcb, reads=[BT, CT], writes=[ps])
                        P.add("dve", lambda e, ps=ps, g0=g0, R=R, gpb=gpb, tri2=tri2: e.tensor_tensor(
                            out=CBm.t[0:R, g0 * R:(g0 + gpb) * R].rearrange("p (g l) -> p g l", l=R),
                            in0=ps.t[0:R, 0:gpb * R].rearrange("p (g l) -> p g l", l=R),
                            in1=tri2.unsqueeze(1).broadcast_to([R, gpb, R]), op=ALU.mult), reads=[ps, C], writes=[CBm])
                    for hq in range(4):
                        P.add("dve", lambda e, hq=hq, R=R, tri2=tri2: e.tensor_tensor(
                            out=AV2.t[0:R, 0:16 * R].rearrange("p (r l) -> p r l", l=R),
                            in0=a_t.t[0:R, hq * 16:(hq + 1) * 16].unsqueeze(2).broadcast_to([R, 16, R]),
                            in1=tri2.unsqueeze(1).broadcast_to([R, 16, R]), op=ALU.mult), reads=[a_t, C], writes=[AV2])
                        nb = 16 * R // 512
                        for bk in range(nb):
                            ps = next_ps()
                            P.add("pe", lambda e, ps=ps, bk=bk, us2=us2, R=R: e.matmul(ps.t[0:R, 0:512], lhsT=us2, rhs=AV2.t[0:R, bk * 512:(bk + 1) * 512],
                                                                                  start=True, stop=True), reads=[AV2, C], writes=[ps])
                            o0 = hq * 16 * R + bk * 512
                            P.add("act", lambda e, ps=ps, o0=o0, R=R: e.activation(out=MT2.t[0:R, o0:o0 + 512], in_=ps.t[0:R, 0:512], func=AF.Exp),
                                  reads=[ps], writes=[MT2])
                        P.add("dve", lambda e, hq=hq, R=R: e.tensor_tensor(
                            out=MT2.t[0:R, hq * 16 * R:(hq + 1) * 16 * R].rearrange("p (g r l) -> p g r l", r=8, l=R),
                            in0=MT2.t[0:R, hq * 16 * R:(hq + 1) * 16 * R].rearrange("p (g r l) -> p g r l", r=8, l=R),
                            in1=CBm.t[0:R, hq * 2 * R:(hq * 2 + 2) * R].rearrange("p (g l) -> p g l", l=R).unsqueeze(2).broadcast_to([R, 2, 8, R]),
                            op=ALU.mult), reads=[MT2, CBm], writes=[MT2])
                    for g in range(NG):
                        ps = next_ps()

                        def myd(e, ps=ps, g=g, R=R):
                            ins = None
                            for rr in range(8):
                                r = g * 8 + rr
                                ins = e.matmul(ps.t[0:R, rr * 64:(rr + 1) * 64], lhsT=MT2.t[0:R, r * R:(r + 1) * R], rhs=xdt.t[0:R, r * 64:(r + 1) * 64],
                                               start=True, stop=True)
                            return ins
                        P.add("pe", myd, reads=[MT2, xdt], writes=[ps])
                        P.add("dve", lambda e, ps=ps, g=g, R=R: e.tensor_tensor(out=y_sb.t[0:R, g * 512:(g + 1) * 512], in0=ps.t[0:R, 0:512],
                                                                               in1=y_sb.t[0:R, g * 512:(g + 1) * 512], op=ALU.add), reads=[ps, y_sb], writes=[y_sb])
                for sg in range(nseg):
                    mcol = C.t[0:R, mo + sg:mo + sg + 1]
                    if sample:
                        state_in(s_ssm_in[sg])
                    AM = am if nseg > 1 else a_t
                    XWM = xwm if nseg > 1 else xw
                    if nseg > 1:
                        P.add("dve", lambda e, mcol=mcol, R=R: e.tensor_scalar_mul(out=am.t[0:R, :], in0=a_t.t[0:R, :], scalar1=mcol), reads=[a_t, C], writes=[am])
                    ps = next_ps()
                    P.add("pe", lambda e, ps=ps, R=R, AM=AM: e.matmul(ps.t[:, 0:HS], lhsT=onesf.t[0:R, :], rhs=AM.t[0:R, :], start=True, stop=True), reads=[AM, onesf], writes=[ps])
                    P.add("act", lambda e, ps=ps: e.activation(out=decb.t[:, :], in_=ps.t[:, 0:HS], func=AF.Exp), reads=[ps], writes=[decb])
                    if nseg > 1:
                        P.add("dve", lambda e, mcol=mcol, R=R: e.tensor_scalar_mul(out=xwm.t[0:R, :], in0=xw.t[0:R, :], scalar1=mcol), reads=[xw, C], writes=[xwm])
                    if not prefix:
                        for g in range(NG):
                            ps = next_ps()
                            P.add("pe", lambda e, ps=ps, g=g, R=R: e.matmul(ps.t[0:R, 0:512], lhsT=CT.t[:, g, 0:R], rhs=S16.t[:, g * 512:(g + 1) * 512], start=True, stop=True),
                                  reads=[CT, S16], writes=[ps])
                            P.add("dve", lambda e, ps=ps, g=g, R=R, mcol=mcol: e.scalar_tensor_tensor(
                                out=tmp.t[0:R, :].rearrange("p (r c) -> p r c", c=64), in0=ps.t[0:R, 0:512].rearrange("p (r c) -> p r c", c=64), scalar=mcol,
                                in1=et.t[0:R, g * 8:(g + 1) * 8].unsqueeze(2).broadcast_to([R, 8, 64]), op0=ALU.mult, op1=ALU.mult), reads=[ps, et, C], writes=[tmp])
                            P.add("dve", lambda e, g=g, R=R: e.tensor_tensor(out=y_sb.t[0:R, g * 512:(g + 1) * 512], in0=tmp.t[0:R, :],
                                                                            in1=y_sb.t[0:R, g * 512:(g + 1) * 512], op=ALU.add), reads=[tmp, y_sb], writes=[y_sb])
                    for g in range(NG):
                        ps = next_ps()
                        P.add("pe", lambda e, ps=ps, g=g, R=R, XWM=XWM: e.matmul(ps.t[:, 0:512], lhsT=b_t.t[0:R, g * 128:(g + 1) * 128], rhs=XWM.t[0:R, g * 512:(g + 1) * 512],
                                                                       start=True, stop=True), reads=[b_t, XWM], writes=[ps])
                        P.add("dve", lambda e, g=g: e.tensor_tensor(
                            out=S32.t[:, g * 512:(g + 1) * 512].rearrange("p (r c) -> p r c", c=64),
                            in0=S32.t[:, g * 512:(g + 1) * 512].rearrange("p (r c) -> p r c", c=64),
                            in1=decb.t[:, g * 8:(g + 1) * 8].unsqueeze(2).broadcast_to([128, 8, 64]), op=ALU.mult), reads=[S32, decb], writes=[S32])
                        P.add("dve", lambda e, ps=ps, g=g: e.tensor_tensor(out=S32.t[:, g * 512:(g + 1) * 512], in0=ps.t[:, 0:512],
                                                                          in1=S32.t[:, g * 512:(g + 1) * 512], op=ALU.add), reads=[ps, S32], writes=[S32])
                    if not prefix:
                        P.add("act", lambda e: e.activation(out=S16.t[:, :], in_=S32.t[:, :], func=AF.Copy), reads=[S32], writes=[S16])
                    if sample:
                        state_out(o_sssm[sg])
                if not prefix:
                    for g in range(NG):
                        Z = zt[g % 2]
                        P.add("sp", lambda e, Z=Z, g=g, t0=t0, R=R: e.dma_start(out=Z.t[0:R, :], in_=z_d.t[t0:t0 + R, g * 512:(g + 1) * 512]), reads=[z_d], writes=[Z], dma=True)
                        P.add("act", lambda e, Z=Z, R=R: e.activation(out=Z.t[0:R, :], in_=Z.t[0:R, :], func=AF.Silu), reads=[Z], writes=[Z])
                        P.add("dve", lambda e, Z=Z, g=g, R=R: e.tensor_tensor(out=y_sb.t[0:R, g * 512:(g + 1) * 512], in0=y_sb.t[0:R, g * 512:(g + 1) * 512],
                                                                             in1=Z.t[0:R, :], op=ALU.mult), reads=[Z, y_sb], writes=[y_sb])
                        P.add("act", lambda e, Z=Z, g=g, R=R: e.activation(out=Z.t[0:R, :], in_=y_sb.t[0:R, g * 512:(g + 1) * 512], func=AF.Square), reads=[y_sb], writes=[Z])
                        P.add("dve", lambda e, Z=Z, g=g, R=R: e.reduce_sum(out=ssg.t[0:R, g:g + 1], in_=Z.t[0:R, :], axis=AX.X), reads=[Z], writes=[ssg])
                    P.add("act", lambda e, R=R: e.activation(out=ssg.t[0:R, :], in_=ssg.t[0:R, :], func=AF.Sqrt, scale=1.0 / 512, bias=epsc.t[0:R, 0:1]), reads=[ssg, epsc], writes=[ssg])
                    P.add("dve", lambda e, R=R: e.reciprocal(out=ssg.t[0:R, :], in_=ssg.t[0:R, :]), reads=[ssg], writes=[ssg], hz=True)
                    P.add("dve", lambda e, R=R: e.tensor_tensor(out=y_sb.t[0:R, :].rearrange("p (g c) -> p g c", c=512), in0=y_sb.t[0:R, :].rearrange("p (g c) -> p g c", c=512),
                                                               in1=ssg.t[0:R, :].unsqueeze(2).broadcast_to([R, NG, 512]), op=ALU.mult), reads=[ssg, y_sb], writes=[y_sb])
                    P.add("dve", lambda e, R=R: e.tensor_tensor(out=y_sb.t[0:R, :], in0=y_sb.t[0:R, :], in1=nssm_b.t[0:R, :], op=ALU.mult), reads=[nssm_b, y_sb], writes=[y_sb])
                    tr_to(lambda c, ps, ap, R=R: copy(ev_eng(), ynst.t[:, c, 0:R], ap, [ps], [ynst]), y_sb, [(c * 128, 128) for c in range(32)], R, [y_sb], [ynst])
                    P.add("sp", lambda e, t0=t0, R=R: e.dma_start(out=v3(ynT_d, NTMAX)[:, :, t0:t0 + R], in_=ynst.t[:, :, 0:R]), reads=[ynst], writes=[ynT_d], dma=True)
            if last and not any(t[0] >= QT for t in tiles):
                state_out(o_pssm)
            if not last:
                P.add("sp", lambda e: e.dma_start(out=S_d.t[:, :], in_=S32.t[:, :]), reads=[S32], writes=[S_d], dma=True)
        P.fence()

    def phase3(q, last):
        nt = QT + (NS * TS if last else 0)
        blks = blocks_of(nt)
        if True:
            with contextlib.ExitStack() as ph:
                mst = [sbp(ph, "mst", [128, NTMAX], BF16) for i in range(2)]
                oT = sbp(ph, "oT3", [128, DC, NTMAX], BF16)
                ynT = sbp(ph, "ynT3", [128, 32, NTMAX], BF16)
                gA = [sbp(ph, "gA", [128, NTMAX]) for i in range(2)]
                gB = [sbp(ph, "gB", [128, NTMAX]) for i in range(2)]
                P.add("sp", lambda e: e.dma_start(out=oT.t[:, :, 0:nt], in_=v3(oT_d, NTMAX)[:, :, 0:nt]), reads=[oT_d], writes=[oT], dma=True)
                P.add("sp", lambda e: e.dma_start(out=ynT.t[:, :, 0:nt], in_=v3(ynT_d, NTMAX)[:, :, 0:nt]), reads=[ynT_d], writes=[ynT], dma=True)
                for d in range(DC):
                    wab, wav = load_w(W["w_oa"], 0, D, d * 128, 128)
                    wsb1, wsv1 = load_w(W["w_os"], 0, 2048, d * 128, 128)
                    wsb2, wsv2 = load_w(W["w_os"], 2048, 2048, d * 128, 128)
                    GA, GB = gA[d % 2], gB[d % 2]
                    P.add("sp", lambda e, GA=GA, d=d: e.dma_start(out=GA.t[:, 0:nt], in_=v3(gT_d, NTMAX)[:, d, 0:nt]), reads=[gT_d], writes=[GA], dma=True)
                    P.add("sp", lambda e, GB=GB, d=d: e.dma_start(out=GB.t[:, 0:nt], in_=v3(gT_d, NTMAX)[:, 16 + d, 0:nt]), reads=[gT_d], writes=[GB], dma=True)
                    P.add("act", lambda e, GA=GA, d=d: e.activation(out=GA.t[:, 0:nt], in_=GA.t[:, 0:nt], func=AF.Sigmoid, bias=bgc.t[:, d:d + 1]), reads=[GA, bgc], writes=[GA])
                    P.add("act", lambda e, GB=GB, d=d: e.activation(out=GB.t[:, 0:nt], in_=GB.t[:, 0:nt], func=AF.Sigmoid, bias=bgc.t[:, 16 + d:17 + d]), reads=[GB, bgc], writes=[GB])
                    for (b0, bw) in blks:
                        pa, pb_ = next_ps(), next_ps()

                        def mm(e, pa=pa, pb_=pb_, b0=b0, bw=bw, wav=wav, wsv1=wsv1, wsv2=wsv2):
                            ins = None
                            for k in range(DC):
                                ins = e.matmul(pa.t[:, 0:bw], lhsT=wav[:, k, :], rhs=oT.t[:, k, b0:b0 + bw], start=(k == 0), stop=(k == DC - 1))
                            for k in range(32):
                                wv = wsv1 if k < 16 else wsv2
                                ins = e.matmul(pb_.t[:, 0:bw], lhsT=wv[:, k % 16, :], rhs=ynT.t[:, k, b0:b0 + bw], start=(k == 0), stop=(k == 31))
                            return ins
                        P.add("pe", mm, reads=[wab, wsb1, wsb2, oT, ynT], writes=[pa, pb_])
                        P.add("dve", lambda e, pa=pa, GA=GA, b0=b0, bw=bw: e.tensor_tensor(out=GA.t[:, b0:b0 + bw], in0=pa.t[:, 0:bw], in1=GA.t[:, b0:b0 + bw], op=ALU.mult),
                              reads=[pa, GA], writes=[GA])
                        P.add("dve", lambda e, pb_=pb_, GB=GB, b0=b0, bw=bw: e.tensor_tensor(out=GB.t[:, b0:b0 + bw], in0=pb_.t[:, 0:bw], in1=GB.t[:, b0:b0 + bw], op=ALU.mult),
                              reads=[pb_, GB], writes=[GB])
                    MS = mst[d % 2]
                    P.add("pool", lambda e, GA=GA, GB=GB, MS=MS: e.tensor_tensor(out=MS.t[:, 0:nt], in0=GA.t[:, 0:nt], in1=GB.t[:, 0:nt], op=ALU.add),
                          reads=[GA, GB], writes=[MS])
                    P.add("sp", lambda e, MS=MS, d=d: e.dma_start(out=v3(mT_d, NTMAX)[:, d, 0:nt], in_=MS.t[:, 0:nt]), reads=[MS], writes=[mT_d], dma=True)
            P.fence()
        with contextlib.ExitStack() as phX:
            xT = sbp(phX, "xT3", [128, DC, NTMAX])
            hT = sbp(phX, "hT3", [128, DC, NTMAX], BF16)
            P.add("sp", lambda e: e.dma_start(out=xT.t[:, :, 0:nt], in_=v3(x1T_d, NTMAX)[:, :, 0:nt]), reads=[x1T_d], writes=[xT], dma=True)
            P.add("sp", lambda e: e.dma_start(out=hT.t[:, :, 0:nt], in_=v3(mT_d, NTMAX)[:, :, 0:nt]), reads=[mT_d], writes=[hT], dma=True)
            if True:
                for d in range(DC):
                    wob, wov = load_w(W["w_out"], 0, D, d * 128, 128)
                    for (b0, bw) in blks:
                        ps = next_ps()

                        def mo_(e, ps=ps, b0=b0, bw=bw, wov=wov):
                            ins = None
                            for k in range(DC):
                                ins = e.matmul(ps.t[:, 0:bw], lhsT=wov[:, k, :], rhs=hT.t[:, k, b0:b0 + bw], start=(k == 0), stop=(k == DC - 1))
                            return ins
                        P.add("pe", mo_, reads=[wob, hT], writes=[ps])
                        P.add("dve", lambda e, ps=ps, d=d, b0=b0, bw=bw: e.tensor_tensor(out=xT.t[:, d, b0:b0 + bw], in0=ps.t[:, 0:bw], in1=xT.t[:, d, b0:b0 + bw], op=ALU.add),
                              reads=[ps, xT], writes=[xT])
            P.fence()
            with contextlib.ExitStack() as ph:
                sqb = [sbp(ph, "sq3", [128, NTMAX], BF16) for i in range(2)]
                rstd = sbp(ph, "rstd3", [128, NTMAX])
                rmsnorm_fm(xT, sqb, rstd, 2, nt, hT)
                with contextlib.ExitStack() as phF:
                    ffn(phF, xT, hT, W["wg2"], W["wu2"], W["wd2"], nt)
                P.fence()
                rmsnorm_fm(xT, sqb, rstd, 3, nt, xT)
                for tt in range(QT // 128):
                    store_tok(y_p[tt * 128:(tt + 1) * 128, :], xT, DC, tt * 128, 128)
                if last:
                    store_tok(y_s[:, :], xT, DC, QT, NS * TS)
        P.fence()

    quarters = stages.get("quarters", list(range(NQ)))
    for qi, q in enumerate(quarters):
        last = (qi == len(quarters) - 1)
        phase1(q, last)
        if stages.get("upto") == "ffn1":
            continue
        if last:
            phase2_attn(q, last)
        phase2_ssd(q, qi == 0, last)
        if last:
            phase3(q, last)

    if stages.get("dump"):
        for nm, src in (("dbg_oT", oT_d), ("dbg_ynT", ynT_d), ("dbg_mT", mT_d), ("dbg_x1T", x1T_d), ("dbg_cqT", cqT_d),
                        ("dbg_z", z_d), ("dbg_dt", dt_d), ("dbg_xs", xs_d), ("dbg_gT", gT_d)):
            dst = nc.dram_tensor(nm, list(src.t.shape), src.t.dtype, kind="ExternalOutput").ap()
            P.add("sp", lambda e, dst=dst, src=src: e.dma_start(out=dst, in_=src.t), reads=[src], dma=True)
    P.emit(st)
    st.close()
    return nc


def rope_tables(pos_local):
    half = ROPE // 2
    inv = np.power(np.float32(10000.0), -np.arange(half, dtype=np.float32) / np.float32(half)).astype(np.float32)
    pos = np.concatenate([pos_local, PAST + np.arange(TS)]).astype(np.float32)
    ang = pos[:, None] * inv[None, :]
    cos, sin = np.cos(ang).astype(np.float32), np.sin(ang).astype(np.float32)
    cos2 = np.concatenate([cos, cos], axis=1)
    sin2 = np.concatenate([-sin, sin], axis=1)
    tok = np.ascontiguousarray(np.concatenate([cos2, sin2], axis=1))
    fm = np.ascontiguousarray(np.stack([cos2.T, sin2.T]))
    return fm, tok


def ssd_consts(L, R, nseg):
    c = np.zeros((128, 128 + 128 + nseg), np.float32)
    idx = np.arange(R)
    same = (idx[:, None] // L) == (idx[None, :] // L)
    c[:R, 0:R] = same & (idx[:, None] <= idx[None, :])
    c[:R, 128:128 + R] = same & (idx[None, :] < idx[:, None])
    for s_ in range(nseg):
        c[s_ * L:(s_ + 1) * L, 256 + s_] = 1.0
    return c


_STAGES = {}
_NCORES = [8]
_LAST = [None]
_LAST_RES = [None]
_RUNKW = {}


def kernel(**inp):
    ncores = _NCORES[0]
    nc = build_program(_STAGES)
    f = lambda a: np.ascontiguousarray(np.asarray(a, dtype=np.float32))
    shared = {
        "n_f1": f(inp["norm_ffn1"][0]), "wg1": f(inp["w_ffn1_gate"][0]), "wu1": f(inp["w_ffn1_up"][0]),
        "wd1": f(inp["w_ffn1_down"][0]), "n_mix": f(inp["norm_mix"][0]), "w_in": f(inp["w_in"][0]),
        "b_gate": f(inp["b_gate"][0]).reshape(-1), "n_qa": f(inp["norm_q_a"][0]), "w_qb": f(inp["w_q_b"][0]),
        "n_kva": f(inp["norm_kv_a"][0]), "w_kvb": f(inp["w_kv_b"][0]), "conv_w": f(inp["conv_w"][0]),
        "conv_b": f(inp["conv_b"][0]), "dt_bias": f(inp["dt_bias"][0]), "a_log": f(inp["a_log"][0]),
        "d_skip": f(inp["d_skip"][0]), "n_ssm": f(inp["norm_ssm"][0]), "w_oa": f(inp["w_o_attn"][0]),
        "w_os": f(inp["w_o_ssm"][0]), "w_out": f(inp["w_out"][0]), "n_f2": f(inp["norm_ffn2"][0]),
        "wg2": f(inp["w_ffn2_gate"][0]), "wu2": f(inp["w_ffn2_up"][0]), "wd2": f(inp["w_ffn2_down"][0]),
        "n_fin": f(inp["norm_final"]), "ident_in": np.eye(128, dtype=np.float32),
        "tri_in": np.zeros((128, 64), np.float32), "ustr_in": np.zeros((128, 64), np.float32),
        "cst_p": ssd_consts(128, 128, 1), "cst_s": ssd_consts(16, 64, 4),
    }
    in_maps = []
    for c in range(ncores):
        m = dict(shared)
        b, kq = c // NQ, c % NQ
        npre = (NQ - 1 - kq) * QT
        xl = np.zeros((SEQ, D), np.float32)
        xl[npre:] = f(inp["x_prompt"][b, 0:(kq + 1) * QT])
        pos_local = np.maximum(np.arange(SEQ) - npre, 0)
        fm, tok = rope_tables(pos_local)
        valid = (np.arange(SEQ) >= npre).astype(np.float32).reshape(SEQ, 1)
        kneg = np.where(np.arange(SEQ // 128)[None, :] * 128 >= npre, 0.0, NEG).astype(np.float32)
        m["xp"] = xl
        m["ropeT"] = fm
        m["rope_tok"] = tok
        m["valid_in"] = valid
        m["kneg_in"] = np.ascontiguousarray(np.broadcast_to(kneg, (128, SEQ // 128)))
        sl = slice(NS * c, NS * (c + 1))
        m["xs"] = f(inp["x_sample"][sl]).reshape(NS * TS, D)
        m["c_lat"] = f(inp["cache_kv_latent"][0, sl])
        m["c_kr"] = f(inp["cache_k_rope"][0, sl])
        m["s_ssm_in"] = f(inp["state_ssm"][0, sl]).reshape(NS, DI, NST)
        m["s_conv_in"] = f(inp["state_conv"][0, sl]).reshape(NS * 3, CONV)
        in_maps.append(m)
    res = run_bass_kernel_spmd(nc, in_maps, core_ids=list(range(ncores)), **_RUNKW)
    _LAST_RES[0] = res
    R = list(res.results)
    _LAST[0] = R
    while len(R) < 8:
        R.append(R[len(R) % len(res.results)])
    cat = lambda k: np.concatenate([R[c][k] for c in range(8)], axis=0)
    seqcat = lambda k: np.stack([np.concatenate([R[b * NQ + j][k] for j in range(NQ)], axis=0) for b in range(2)])
    y_prompt = seqcat("y_p")
    y_sample = cat("y_s").reshape(32, TS, D)
    p_lat = seqcat("o_plat")[None]
    p_kr = seqcat("o_pkr")[None]
    p_ssm = np.stack([R[NQ - 1]["o_pssm"], R[2 * NQ - 1]["o_pssm"]]).reshape(1, 2, HS, HS, NST)
    p_conv = np.stack([R[NQ - 1]["o_pconv"], R[2 * NQ - 1]["o_pconv"]])[None]
    s_lat = cat("o_slat").reshape(1, 32, TS, KVL)
    s_kr = cat("o_skr").reshape(1, 32, TS, ROPE)
    s_ssm = cat("o_sssm").reshape(1, 32, HS, HS, NST)
    s_conv = cat("o_sconv").reshape(1, 32, 3, CONV)
    return (y_prompt, y_sample, p_lat, p_kr, p_ssm, p_conv, s_lat, s_kr, s_ssm, s_conv)
```

```python
import contextlib
import numpy as np
import concourse.bass as bass
import concourse.mybir as mybir
from concourse.bass_utils import run_bass_kernel_spmd

F32 = mybir.dt.float32
BF16 = mybir.dt.bfloat16
ALU = mybir.AluOpType
AF = mybir.ActivationFunctionType
AX = mybir.AxisListType

D = 2048
DC = 16
SEQ = 4096
NQ = 4
QT = 1024
NS = 4
TS = 16
PAST = 2048
DFF = 5632
QL = 768
KVL = 512
ROPE = 64
NH = 16
DI = 4096
CONV = 6144
HS = 64
NG = 8
NST = 128
DIN = 15744
O_QA, O_KV, O_Z, O_XBC, O_DT, O_G = 0, 768, 1344, 5440, 11584, 11648
EPS = 1e-6
SCALE = 192 ** -0.5
NEG = -30000.0


class Buf:
    __slots__ = ("t", "name", "lw", "rd", "sc")

    def __init__(self, t, name="", sc=False):
        self.t = t
        self.name = name
        self.lw = None
        self.rd = []
        self.sc = sc

    def __getitem__(self, k):
        return self.t[k]


class Op:
    __slots__ = ("eng", "fn", "deps", "dma", "sig", "sigval", "sem", "waits", "hz")

    def __init__(self, eng, fn, dma):
        self.hz = False
        self.eng = eng
        self.fn = fn
        self.dma = dma
        self.deps = []
        self.sig = False
        self.sigval = 0
        self.sem = None
        self.waits = []


ENGS = ("pe", "act", "dve", "pool", "sp")
KDMA = 8
SAFE_SYNC = False


class Prog:
    def __init__(self, nc):
        self.nc = nc
        self.ops = []
        self.by_eng = {e: [] for e in ENGS}
        self.fence_ops = []
        self.fenced = set()

    def fence(self):
        f = []
        for e in ENGS:
            comp = [o for o in self.by_eng[e] if not o.dma]
            if comp:
                f.append(comp[-1])
            f += [o for o in self.by_eng[e] if o.dma][-KDMA:]
        self.fence_ops = f
        self.fenced = set()

    def add(self, eng, fn, reads=(), writes=(), dma=False, hz=False):
        op = Op(eng, fn, dma)
        op.hz = hz
        deps = {}
        if self.fence_ops and eng not in self.fenced:
            self.fenced.add(eng)
            for d in self.fence_ops:
                deps[id(d)] = d
        strong = set()
        for b in reads:
            if b.lw is not None:
                deps[id(b.lw)] = b.lw
                if b.sc:
                    strong.add(id(b.lw))
        for b in writes:
            if b.lw is not None:
                deps[id(b.lw)] = b.lw
            for r in b.rd:
                deps[id(r)] = r
        for d in deps.values():
            if (not d.dma) and d.eng == eng and not dma and (eng == "pe" or not (SAFE_SYNC or id(d) in strong or d.hz)):
                continue
            op.deps.append(d)
        for b in reads:
            b.rd.append(op)
        for b in writes:
            b.lw = op
            b.rd = []
        self.ops.append(op)
        self.by_eng[eng].append(op)
        return op

    def emit(self, stack):
        nc = self.nc
        for op in self.ops:
            for d in op.deps:
                d.sig = True
            if op.dma:
                op.sig = True
        csem = {e: stack.enter_context(nc.semaphore("cs_" + e)) for e in ("pe", "act", "dve", "pool")}
        dsem = {e: [stack.enter_context(nc.semaphore("ds_%s%d" % (e, i))) for i in range(KDMA)]
                for e in ("sp", "pool")}
        for e in ENGS:
            cnt = 0
            dcnt = 0
            for op in self.by_eng[e]:
                if op.dma:
                    op.sem = dsem[e][dcnt % KDMA]
                    op.sigval = 16 * (dcnt // KDMA + 1)
                    if dcnt >= KDMA:
                        op.waits.append((op.sem, 16 * (dcnt // KDMA)))
                    dcnt += 1
                elif op.sig:
                    cnt += 1
                    op.sem = csem[e]
                    op.sigval = cnt
        for e in ENGS:
            seen = {}
            for op in self.by_eng[e]:
                ws = list(op.waits)
                for d in op.deps:
                    ws.append((d.sem, d.sigval))
                best = {}
                for s, v in ws:
                    k = id(s)
                    if seen.get(k, 0) >= v:
                        continue
                    if k not in best or best[k][1] < v:
                        best[k] = (s, v)
                op.waits = list(best.values())
                for s, v in op.waits:
                    seen[id(s)] = v
        final = []
        for e in ("sp", "pool"):
            last = {}
            for op in self.by_eng[e]:
                if op.dma:
                    last[id(op.sem)] = (op.sem, op.sigval)
            final += list(last.values())
        by_eng = self.by_eng

        def run(eng_name):
            def body(eng):
                for op in by_eng[eng_name]:
                    for s, v in op.waits:
                        eng.wait_ge(s, v)
                    ins = op.fn(eng)
                    if op.sig:
                        ins.then_inc(op.sem, 16 if op.dma else 1)
                if eng_name == "sp":
                    for s, v in final:
                        eng.wait_ge(s, v)
            return body

        block = stack.enter_context(nc.Block())
        block.tensor(run("pe"))
        block.scalar(run("act"))
        block.vector(run("dve"))
        block.gpsimd(run("pool"))
        block.sync(run("sp"))


def blocks_of(n, w=512):
    out = []
    c = 0
    while c < n:
        out.append((c, min(w, n - c)))
        c += w
    return out


def build_program(stages):
    nc = bass.Bass("TRN2", target_bir_lowering=False)
    st = contextlib.ExitStack()
    P = Prog(nc)

    def din(name, shape, dt=F32):
        return nc.dram_tensor(name, list(shape), dt, kind="ExternalInput").ap()

    def dout(name, shape, dt=F32):
        return nc.dram_tensor(name, list(shape), dt, kind="ExternalOutput").ap()

    def dscr(name, shape, dt=F32):
        return Buf(nc.dram_tensor(name, list(shape), dt).ap(), name)

    def sb(name, shape, dt=F32):
        return Buf(st.enter_context(nc.sbuf_tensor(name, list(shape), dt)), name)

    xp = din("xp", [SEQ, D])
    xs = din("xs", [NS * TS, D])
    c_lat = din("c_lat", [NS, PAST, KVL])
    c_kr = din("c_kr", [NS, PAST, ROPE])
    s_ssm_in = din("s_ssm_in", [NS, DI, NST])
    s_conv_in = din("s_conv_in", [NS * 3, CONV])
    W = {}
    for nm, shp in (("n_f1", [D]), ("wg1", [D, DFF]), ("wu1", [D, DFF]), ("wd1", [DFF, D]),
                    ("n_mix", [D]), ("w_in", [D, DIN]), ("b_gate", [2 * D]), ("n_qa", [QL]),
                    ("w_qb", [QL, NH * 192]), ("n_kva", [KVL]), ("w_kvb", [KVL, NH * 256]),
                    ("conv_w", [4, CONV]), ("conv_b", [CONV]), ("dt_bias", [HS]), ("a_log", [HS]),
                    ("d_skip", [HS]), ("n_ssm", [DI]), ("w_oa", [D, D]), ("w_os", [DI, D]),
                    ("w_out", [D, D]), ("n_f2", [D]), ("wg2", [D, DFF]), ("wu2", [D, DFF]),
                    ("wd2", [DFF, D]), ("n_fin", [D])):
        W[nm] = din(nm, shp)
    ropeT = din("ropeT", [2, ROPE, SEQ + TS])
    rope_tok = din("rope_tok", [SEQ + TS, 2 * ROPE])
    ident_in = din("ident_in", [128, 128])

    valid_in = din("valid_in", [SEQ, 1])
    kneg_in = din("kneg_in", [128, SEQ // 128])
    y_p = dout("y_p", [QT, D])
    y_s = dout("y_s", [NS * TS, D])
    o_plat = dout("o_plat", [QT, KVL])
    o_pkr = dout("o_pkr", [QT, ROPE])
    o_pssm = dout("o_pssm", [DI, NST])
    o_pconv = dout("o_pconv", [3, CONV])
    o_slat = dout("o_slat", [NS * TS, KVL])
    o_skr = dout("o_skr", [NS * TS, ROPE])
    o_sssm = dout("o_sssm", [NS, DI, NST])
    o_sconv = dout("o_sconv", [NS * 3, CONV])

    uid = [0]

    SC_NAMES = ("ss", "ssg", "te", "et", "decb", "dts", "a_t", "am", "vcol", "dt_t", "rstd")

    def sbp(ph, name, shape, dt=F32):
        uid[0] += 1
        nm = "%s_%d" % (name, uid[0])
        return Buf(ph.enter_context(nc.sbuf_tensor(nm, list(shape), dt)), nm, sc=(name in SC_NAMES))

    tri_in = din("tri_in", [128, 64])
    ustr_in = din("ustr_in", [128, 64])
    ident = sb("ident", [128, 128])
    onesb = sb("onesb", [128, 128], BF16)
    onesf = sb("onesf", [128, 128])
    epsc = sb("epsc", [128, 1])
    gcols = sb("gcols", [128, 4, DC])
    triV = sb("triV", [128, 64])
    ustr = sb("ustr", [128, 64])
    P.add("sp", lambda e: e.dma_start(out=ident[:], in_=ident_in), writes=[ident], dma=True)
    P.add("sp", lambda e: e.dma_start(out=triV[:], in_=tri_in), writes=[triV], dma=True)
    P.add("sp", lambda e: e.dma_start(out=ustr[:], in_=ustr_in), writes=[ustr], dma=True)
    P.add("dve", lambda e: e.memset(onesb[:], 1.0), writes=[onesb])
    P.add("dve", lambda e: e.memset(onesf[:], 1.0), writes=[onesf])
    identb = sb("identb", [128, 128], BF16)
    P.add("dve", lambda e: e.tensor_copy(out=identb[:], in_=ident[:]), reads=[ident], writes=[identb])
    P.add("dve", lambda e: e.memset(epsc[:], EPS), writes=[epsc])
    for i, nm in enumerate(("n_f1", "n_mix", "n_f2", "n_fin")):
        P.add("sp", lambda e, i=i, nm=nm: e.dma_start(
            out=gcols[:, i, :], in_=W[nm].rearrange("(c p) -> p c", p=128),
            allow_slow_non_contiguous=True), writes=[gcols], dma=True)

    def bc_load(dst, src1d, n):
        P.add("sp", lambda e: e.dma_start(out=dst.t[:, 0:n], in_=src1d.rearrange("(o n) -> o n", o=1).broadcast_to([128, n])),
              writes=[dst], dma=True)

    nqa_b = sb("nqa_b", [128, QL])
    nkv_b = sb("nkv_b", [128, KVL])
    dtb_b = sb("dtb_b", [128, HS])
    A_b = sb("A_b", [128, HS])
    dsk_b = sb("dsk_b", [128, HS])
    bc_load(nqa_b, W["n_qa"], QL)
    bc_load(nkv_b, W["n_kva"], KVL)
    bc_load(dtb_b, W["dt_bias"], HS)
    bc_load(A_b, W["a_log"], HS)
    bc_load(dsk_b, W["d_skip"], HS)
    P.add("act", lambda e: e.activation(out=A_b[:], in_=A_b[:], func=AF.Exp), reads=[A_b], writes=[A_b])
    P.add("dve", lambda e: e.tensor_scalar_mul(out=A_b[:], in0=A_b[:], scalar1=-1.0), reads=[A_b], writes=[A_b])
    cwc = sb("cwc", [128, 48, 5])
    for k in range(4):
        P.add("sp", lambda e, k=k: e.dma_start(out=cwc[:, :, k], in_=W["conv_w"][k, :].rearrange("(c p) -> p c", p=128),
                                               allow_slow_non_contiguous=True), writes=[cwc], dma=True)
    P.add("sp", lambda e: e.dma_start(out=cwc[:, :, 4], in_=W["conv_b"].rearrange("(c p) -> p c", p=128),
                                      allow_slow_non_contiguous=True), writes=[cwc], dma=True)
    bgc = sb("bgc", [128, 32])
    P.add("sp", lambda e: e.dma_start(out=bgc[:], in_=W["b_gate"].rearrange("(c p) -> p c", p=128),
                                      allow_slow_non_contiguous=True), writes=[bgc], dma=True)
    negm = sb("negm", [128, 1])
    P.add("dve", lambda e: e.memset(negm[:], 0.0), writes=[negm])
    P.add("dve", lambda e: e.memset(negm[64:128, :], NEG), writes=[negm])
    halo = sb("halo", [128, 48, 3])
    P.add("dve", lambda e: e.memset(halo[:], 0.0), writes=[halo])
    convout = sb("convout", [128, 48, 16])

    psb = [Buf(st.enter_context(nc.psum_tensor("ps%d" % i, [128, 512], F32)), "ps%d" % i) for i in range(8)]
    ps_ctr = [0]
    ps_mod = [6]

    def next_ps():
        b = psb[ps_ctr[0] % ps_mod[0]]
        ps_ctr[0] += 1
        return b

    ev_ctr = [0]

    def ev_eng():
        ev_ctr[0] += 1
        return "act" if ev_ctr[0] % 2 else "dve"

    def copy(en, dst, src, reads, writes):
        if en == "act":
            P.add("act", lambda e: e.activation(out=dst, in_=src, func=AF.Copy), reads=reads, writes=writes)
        else:
            P.add(en, lambda e: e.tensor_copy(out=dst, in_=src), reads=reads, writes=writes)

    NTMAX = QT + NS * TS
    wbufs = [sb("wb%d" % i, [128, DC * 128], BF16) for i in range(6)]
    wb_ctr = [0]

    def load_w(wap, r0, nr, c0, ncol):
        kc = nr // 128
        wb = wbufs[wb_ctr[0] % len(wbufs)]
        wb_ctr[0] += 1
        view = wb.t[:, 0:kc * ncol].rearrange("p (c n) -> p c n", n=ncol)
        src = wap[r0:r0 + nr, c0:c0 + ncol].rearrange("(c p) n -> p c n", p=128)
        P.add("pool", lambda e: e.dma_start(out=view, in_=src), writes=[wb], dma=True)
        return wb, view

    stage_bufs = [sb("stg%d" % i, [128, D]) for i in range(2)]
    stg_ctr = [0]

    def next_stage():
        b = stage_bufs[stg_ctr[0] % len(stage_bufs)]
        stg_ctr[0] += 1
        return b

    x1T_d = dscr("x1T_d", [128, DC * NTMAX])
    cqT_d = dscr("cqT_d", [128, 6 * NTMAX], BF16)
    latT_d = dscr("latT_d", [128, 5 * SEQ], BF16)
    latS_d = dscr("latS_d", [128, 5 * NS * TS], BF16)
    latStok_d = dscr("latStok_d", [NS * TS, KVL], BF16)
    z_d = dscr("z_d", [NTMAX, DI])
    dt_d = dscr("dt_d", [NTMAX, HS])
    xs_d = dscr("xs_d", [NTMAX, DI], BF16)
    b_d = dscr("b_d", [NTMAX, NG * NST], BF16)
    bcT_d = dscr("bcT_d", [128, 16 * NTMAX], BF16)
    gT_d = dscr("gT_d", [128, 32 * NTMAX])
    oT_d = dscr("oT_d", [128, DC * NTMAX], BF16)
    ynT_d = dscr("ynT_d", [128, 32 * NTMAX], BF16)
    S_d = dscr("S_d", [128, DI])
    mT_d = dscr("mT_d", [128, DC * NTMAX], BF16)

    def v3(buf, n):
        return buf.t.rearrange("p (c n) -> p c n", n=n)

    def load_xT(xT, src_ap, ntok, col0):
        xt = next_stage()
        P.add("sp", lambda e: e.dma_start(out=xt.t[0:ntok, :], in_=src_ap), writes=[xt], dma=True)
        for g in range(4):
            ps = next_ps()

            def tr(e, g=g, ps=ps):
                ins = None
                for j in range(4):
                    c = g * 4 + j
                    ins = e.transpose(out=ps.t[:, j * 128:j * 128 + ntok], in_=xt.t[0:ntok, c * 128:(c + 1) * 128],
                                      identity=ident.t[0:ntok, 0:ntok])
                return ins
            P.add("pe", tr, reads=[xt, ident], writes=[ps])
            copy(ev_eng(), xT.t[:, g * 4:(g + 1) * 4, col0:col0 + ntok],
                 ps.t[:, :].rearrange("p (j t) -> p j t", t=128)[:, :, 0:ntok], [ps], [xT])

    def rmsnorm_fm(xT, sqb, rstd, gi, nt, out, ph_hT=True):
        blks = blocks_of(nt)
        pss = [next_ps() for _ in blks]
        for c in range(DC):
            sq = sqb[c % 2]
            P.add("act", lambda e, c=c, sq=sq: e.activation(out=sq.t[:, 0:nt], in_=xT.t[:, c, 0:nt], func=AF.Square),
                  reads=[xT], writes=[sq])
            for (b0, bw), ps in zip(blks, pss):
                P.add("pe", lambda e, c=c, sq=sq, ps=ps, b0=b0, bw=bw: e.matmul(
                    ps.t[:, 0:bw], lhsT=onesb.t[:, :], rhs=sq.t[:, b0:b0 + bw], start=(c == 0), stop=(c == DC - 1)),
                    reads=[sq, onesb], writes=[ps])
        for (b0, bw), ps in zip(blks, pss):
            P.add("act", lambda e, ps=ps, b0=b0, bw=bw: e.activation(
                out=rstd.t[:, b0:b0 + bw], in_=ps.t[:, 0:bw], func=AF.Sqrt, scale=1.0 / D, bias=epsc.t[:, 0:1]),
                reads=[ps, epsc], writes=[rstd])
            P.add("dve", lambda e, b0=b0, bw=bw: e.reciprocal(out=rstd.t[:, b0:b0 + bw], in_=rstd.t[:, b0:b0 + bw]),
                  reads=[rstd], writes=[rstd], hz=True)
        for c in range(DC):
            P.add("dve", lambda e, c=c: e.scalar_tensor_tensor(
                out=out.t[:, c, 0:nt], in0=xT.t[:, c, 0:nt], scalar=gcols.t[:, gi, c:c + 1], in1=rstd.t[:, 0:nt],
                op0=ALU.mult, op1=ALU.mult), reads=[xT, rstd, gcols], writes=[out])

    def ffn(ph, xT, hT, wg, wu, wd, nt):
        blks = blocks_of(nt)
        hb = sbp(ph, "hid", [128, 11, NTMAX], BF16)
        silu_t = [sbp(ph, "silu", [128, 512]) for i in range(2)]
        sctr = 0
        for fq in range(4):
            for fl in range(11):
                f = fq * 11 + fl
                wgb, wgv = load_w(wg, 0, D, f * 128, 128)
                wub, wuv = load_w(wu, 0, D, f * 128, 128)
                for (b0, bw) in blks:
                    pg, pu = next_ps(), next_ps()

                    def mm(e, wv=wgv, ps=pg, b0=b0, bw=bw):
                        ins = None
                        for c in range(DC):
                            ins = e.matmul(ps.t[:, 0:bw], lhsT=wv[:, c, :], rhs=hT.t[:, c, b0:b0 + bw],
                                           start=(c == 0), stop=(c == DC - 1))
                        return ins
                    P.add("pe", mm, reads=[wgb, hT], writes=[pg])
                    P.add("pe", lambda e, wv=wuv, ps=pu, b0=b0, bw=bw, mm=mm: mm(e, wv, ps, b0, bw),
                          reads=[wub, hT], writes=[pu])
                    sl = silu_t[sctr % 2]
                    sctr += 1
                    P.add("act", lambda e, sl=sl, pg=pg, bw=bw: e.activation(out=sl.t[:, 0:bw], in_=pg.t[:, 0:bw],
                                                                           func=AF.Silu), reads=[pg], writes=[sl])
                    P.add("dve", lambda e, sl=sl, pu=pu, fl=fl, b0=b0, bw=bw: e.tensor_tensor(
                        out=hb.t[:, fl, b0:b0 + bw], in0=sl.t[:, 0:bw], in1=pu.t[:, 0:bw], op=ALU.mult),
                        reads=[sl, pu], writes=[hb])
            for d in range(DC):
                wdb, wdv = load_w(wd, fq * 11 * 128, 11 * 128, d * 128, 128)
                for (b0, bw) in blks:
                    ps = next_ps()

                    def mm2(e, wv=wdv, ps=ps, b0=b0, bw=bw):
                        ins = None
                        for fl in range(11):
                            ins = e.matmul(ps.t[:, 0:bw], lhsT=wv[:, fl, :], rhs=hb.t[:, fl, b0:b0 + bw],
                                           start=(fl == 0), stop=(fl == 10))
                        return ins
                    P.add("pe", mm2, reads=[wdb, hb], writes=[ps])
                    P.add("dve", lambda e, ps=ps, d=d, b0=b0, bw=bw: e.scalar_tensor_tensor(
                        out=xT.t[:, d, b0:b0 + bw], in0=ps.t[:, 0:bw], scalar=0.5, in1=xT.t[:, d, b0:b0 + bw],
                        op0=ALU.mult, op1=ALU.add), reads=[ps, xT], writes=[xT])

    def tr_to(dst_fn, src, ncols_list, ntok, reads, writes):
        for g0 in range(0, len(ncols_list), 4):
            ps = next_ps()
            grp = ncols_list[g0:g0 + 4]

            def tr(e, ps=ps, grp=grp):
                ins = None
                for j, (c0, cw) in enumerate(grp):
                    ins = e.transpose(out=ps.t[0:cw, j * 128:j * 128 + ntok], in_=src.t[0:ntok, c0:c0 + cw],
                                      identity=ident.t[0:ntok, 0:ntok])
                return ins
            P.add("pe", tr, reads=[src, ident], writes=[ps])
            for j, (c0, cw) in enumerate(grp):
                dst_fn(g0 + j, ps, ps.t[0:cw, j * 128:j * 128 + ntok])

    def store_tok(dst_ap, srcT, nchunks, col0, ntok):
        stg = next_stage()
        for g in range((nchunks + 3) // 4):
            ps = next_ps()
            nj = min(4, nchunks - g * 4)

            def tr(e, g=g, ps=ps, nj=nj):
                ins = None
                for j in range(nj):
                    ins = e.transpose(out=ps.t[0:ntok, j * 128:(j + 1) * 128],
                                      in_=srcT.t[:, g * 4 + j, col0:col0 + ntok], identity=ident.t[:, :])
                return ins
            P.add("pe", tr, reads=[srcT, ident], writes=[ps])
            copy(ev_eng(), stg.t[0:ntok, g * 512:g * 512 + nj * 128], ps.t[0:ntok, 0:nj * 128], [ps], [stg])
        P.add("sp", lambda e: e.dma_start(out=dst_ap, in_=stg.t[0:ntok, 0:nchunks * 128]), reads=[stg], dma=True)

    def lin_tm(hT, c0, ncols, tts, cb):
        tiles = []
        c = 0
        while c < ncols:
            cw = min(128, ncols - c)
            wb, wv = load_w(W["w_in"], 0, D, c0 + c, cw)
            tiles.append((c, cw, wb, wv))
            c += cw
        def emit_mm(t0, tn):
            banks = []
            for g0 in range(0, len(tiles), 4):
                ps = next_ps()
                grp = tiles[g0:g0 + 4]

                def mm(e, ps=ps, grp=grp, t0=t0, tn=tn):
                    ins = None
                    for (cc, cw, wb, wv) in grp:
                        off = cc - grp[0][0]
                        for k in range(DC):
                            ins = e.matmul(ps.t[0:tn, off:off + cw], lhsT=hT.t[:, k, t0:t0 + tn], rhs=wv[:, k, :],
                                           start=(k == 0), stop=(k == DC - 1))
                    return ins
                P.add("pe", mm, reads=[hT] + [t[2] for t in grp], writes=[ps])
                banks.append((ps, grp[0][0], sum(t[1] for t in grp)))
            return banks
        pend = emit_mm(*tts[0])
        for ti, (t0, tn) in enumerate(tts):
            banks = pend
            if ti + 1 < len(tts) and len(tiles) <= 8:
                pend = emit_mm(*tts[ti + 1])
                cb(ti, t0, tn, banks)
            else:
                cb(ti, t0, tn, banks)
                if ti + 1 < len(tts):
                    pend = emit_mm(*tts[ti + 1])

    def rs_of(ph_bufs, src, n, tn):
        sqt, ss = ph_bufs
        P.add("act", lambda e: e.activation(out=sqt.t[0:tn, 0:n], in_=src.t[0:tn, 0:n], func=AF.Square),
              reads=[src], writes=[sqt])
        P.add("dve", lambda e: e.reduce_sum(out=ss.t[0:tn, 0:1], in_=sqt.t[0:tn, 0:n], axis=AX.X), reads=[sqt], writes=[ss])
        P.add("act", lambda e: e.activation(out=ss.t[0:tn, 0:1], in_=ss.t[0:tn, 0:1], func=AF.Sqrt, scale=1.0 / n,
                                            bias=epsc.t[0:tn, 0:1]), reads=[ss, epsc], writes=[ss])
        P.add("dve", lambda e: e.reciprocal(out=ss.t[0:tn, 0:1], in_=ss.t[0:tn, 0:1]), reads=[ss], writes=[ss], hz=True)

    def phase1(q, last):
        prefix = not last
        nt = QT + (NS * TS if last else 0)
        tts = [(i * 128, 128) for i in range(QT // 128)] + ([(QT, NS * TS)] if last else [])
        blks = blocks_of(nt)
        with contextlib.ExitStack() as phH:
            hT = sbp(phH, "hT", [128, DC, NTMAX], BF16)
            with contextlib.ExitStack() as phA:
                xT = sbp(phA, "xT", [128, DC, NTMAX])
                sqb = [sbp(phA, "sq", [128, NTMAX], BF16) for i in range(2)]
                rstd = sbp(phA, "rstd", [128, NTMAX])
                for tt in range(QT // 128):
                    load_xT(xT, xp[q * QT + tt * 128:q * QT + (tt + 1) * 128, :], 128, tt * 128)
                if last:
                    load_xT(xT, xs[:, :], NS * TS, QT)
                rmsnorm_fm(xT, sqb, rstd, 0, nt, hT)
                with contextlib.ExitStack() as phF:
                    ffn(phF, xT, hT, W["wg1"], W["wu1"], W["wd1"], nt)
                P.fence()
                if stages.get("upto") == "ffn1":
                    for tt in range(QT // 128):
                        store_tok(y_p[tt * 128:(tt + 1) * 128, :], xT, DC, tt * 128, 128)
                    return
                rmsnorm_fm(xT, sqb, rstd, 1, nt, hT)
                if not prefix:
                    P.add("sp", lambda e: e.dma_start(out=v3(x1T_d, NTMAX)[:, :, 0:nt], in_=xT.t[:, :, 0:nt]),
                          reads=[xT], writes=[x1T_d], dma=True)
            P.fence()
            with contextlib.ExitStack() as phB:
                qa_sb = sbp(phB, "qa_sb", [128, QL])
                sqt = sbp(phB, "sqt", [128, QL])
                ss = sbp(phB, "ss", [128, 1])
                cq = sbp(phB, "cq", [128, QL])
                cqs = sbp(phB, "cqs", [128, 6, 128], BF16)
                kv_sb = sbp(phB, "kv_sb", [128, 576])
                lat = sbp(phB, "lat", [128, 576])
                rtk = sbp(phB, "rtk", [128, 128])
                rt1 = sbp(phB, "rt1", [128, 64])
                rt2 = sbp(phB, "rt2", [128, 64])
                lts = sbp(phB, "lts", [128, 5, 128], BF16)

                def cb_qa(ti, t0, tn, banks):
                    for (ps, co, w) in banks:
                        copy(ev_eng(), qa_sb.t[0:tn, co:co + w], ps.t[0:tn, 0:w], [ps], [qa_sb])
                    rs_of((sqt, ss), qa_sb, QL, tn)
                    P.add("dve", lambda e: e.scalar_tensor_tensor(out=cq.t[0:tn, :], in0=qa_sb.t[0:tn, :], scalar=ss.t[0:tn, 0:1],
                                                                  in1=nqa_b.t[0:tn, :], op0=ALU.mult, op1=ALU.mult),
                          reads=[qa_sb, ss, nqa_b], writes=[cq])
                    tr_to(lambda j, ps, ap: copy(ev_eng(), cqs.t[:, j, 0:tn], ap, [ps], [cqs]), cq,
                          [(j * 128, 128) for j in range(6)], tn, [cq], [cqs])
                    P.add("sp", lambda e: e.dma_start(out=v3(cqT_d, NTMAX)[:, :, t0:t0 + tn], in_=cqs.t[:, :, 0:tn]),
                          reads=[cqs], writes=[cqT_d], dma=True)
                if not prefix:
                    lin_tm(hT, O_QA, QL, tts, cb_qa)

                def cb_kv(ti, t0, tn, banks):
                    for (ps, co, w) in banks:
                        copy(ev_eng(), kv_sb.t[0:tn, co:co + w], ps.t[0:tn, 0:w], [ps], [kv_sb])
                    rs_of((sqt, ss), kv_sb, KVL, tn)
                    P.add("dve", lambda e: e.scalar_tensor_tensor(out=lat.t[0:tn, 0:KVL], in0=kv_sb.t[0:tn, 0:KVL],
                                                                  scalar=ss.t[0:tn, 0:1], in1=nkv_b.t[0:tn, :],
                                                                  op0=ALU.mult, op1=ALU.mult),
                          reads=[kv_sb, ss, nkv_b], writes=[lat])
                    if t0 < QT:
                        P.add("sp", lambda e: e.dma_start(out=rtk.t[0:tn, :], in_=rope_tok[q * QT + t0:q * QT + t0 + tn, :]),
                              writes=[rtk], dma=True)
                    else:
                        for j in range(NS):
                            P.add("sp", lambda e, j=j: e.dma_start(out=rtk.t[j * TS:(j + 1) * TS, :], in_=rope_tok[SEQ:SEQ + TS, :]),
                                  writes=[rtk], dma=True)

                    def rp(e):
                        e.tensor_tensor(out=rt1.t[0:tn, :], in0=kv_sb.t[0:tn, 512:576], in1=rtk.t[0:tn, 0:64], op=ALU.mult)
                        e.tensor_tensor(out=rt2.t[0:tn, 0:32], in0=kv_sb.t[0:tn, 544:576], in1=rtk.t[0:tn, 64:96], op=ALU.mult)
                        e.tensor_tensor(out=rt2.t[0:tn, 32:64], in0=kv_sb.t[0:tn, 512:544], in1=rtk.t[0:tn, 96:128], op=ALU.mult)
                        return e.tensor_tensor(out=lat.t[0:tn, 512:576], in0=rt1.t[0:tn, :], in1=rt2.t[0:tn, :], op=ALU.add)
                    P.add("dve", rp, reads=[kv_sb, rtk], writes=[lat, rt1, rt2])
                    if t0 < QT:
                        if not prefix:
                            P.add("sp", lambda e: e.dma_start(out=o_plat[t0:t0 + tn, :], in_=lat.t[0:tn, 0:KVL]), reads=[lat], dma=True)
                            P.add("sp", lambda e: e.dma_start(out=o_pkr[t0:t0 + tn, :], in_=lat.t[0:tn, 512:576]), reads=[lat], dma=True)
                    else:
                        P.add("sp", lambda e: e.dma_start(out=o_slat[:, :], in_=lat.t[0:tn, 0:KVL]), reads=[lat], dma=True)
                        P.add("pool", lambda e: e.dma_start(out=latStok_d.t[:, :], in_=lat.t[0:tn, 0:KVL]), reads=[lat], writes=[latStok_d], dma=True)
                        P.add("sp", lambda e: e.dma_start(out=o_skr[:, :], in_=lat.t[0:tn, 512:576]), reads=[lat], dma=True)
                    pieces = [(j * 128, 128) for j in range(4)] + [(512, 64)]
                    tr_to(lambda j, ps, ap: copy(ev_eng(), lts.t[0:pieces[j][1], j, 0:tn], ap, [ps], [lts]), lat, pieces, tn,
                          [lat], [lts])
                    if t0 < QT:
                        r0 = q * QT + t0
                        P.add("sp", lambda e: e.dma_start(out=v3(latT_d, SEQ)[:, :, r0:r0 + tn], in_=lts.t[:, :, 0:tn]),
                              reads=[lts], writes=[latT_d], dma=True)
                    else:
                        P.add("sp", lambda e: e.dma_start(out=v3(latS_d, NS * TS)[:, :, :], in_=lts.t[:, :, 0:tn]),
                              reads=[lts], writes=[latS_d], dma=True)
                lin_tm(hT, O_KV, 576, tts, cb_kv)

                zst = [sbp(phB, "zst", [128, 512]) for i in range(2)]
                zc = [0]
                for pc in (range(8) if not prefix else ()):
                    def cb_z(ti, t0, tn, banks, pc=pc):
                        (ps, co, w) = banks[0]
                        zb = zst[zc[0] % 2]
                        zc[0] += 1
                        copy(ev_eng(), zb.t[0:tn, :], ps.t[0:tn, 0:512], [ps], [zb])
                        P.add("sp", lambda e: e.dma_start(out=z_d.t[t0:t0 + tn, pc * 512:(pc + 1) * 512], in_=zb.t[0:tn, :]),
                              reads=[zb], writes=[z_d], dma=True)
                    lin_tm(hT, O_Z + pc * 512, 512, tts, cb_z)

                dts = sbp(phB, "dts", [128, HS])
                vcol = sbp(phB, "vcol", [128, 1])

                def cb_dt(ti, t0, tn, banks):
                    (ps, co, w) = banks[0]
                    P.add("dve", lambda e: e.tensor_tensor(out=dts.t[0:tn, :], in0=ps.t[0:tn, 0:HS], in1=dtb_b.t[0:tn, :], op=ALU.add),
                          reads=[ps, dtb_b], writes=[dts])
                    P.add("act", lambda e: e.activation(out=dts.t[0:tn, :], in_=dts.t[0:tn, :], func=AF.Exp), reads=[dts], writes=[dts])
                    P.add("act", lambda e: e.activation(out=dts.t[0:tn, :], in_=dts.t[0:tn, :], func=AF.Ln, bias=onesf.t[0:tn, 0:1]),
                          reads=[dts, onesf], writes=[dts])
                    if t0 < QT:
                        P.add("sp", lambda e: e.dma_start(out=vcol.t[0:tn, :], in_=valid_in[q * QT + t0:q * QT + t0 + tn, :]), writes=[vcol], dma=True)
                        P.add("dve", lambda e: e.tensor_scalar_mul(out=dts.t[0:tn, :], in0=dts.t[0:tn, :], scalar1=vcol.t[0:tn, 0:1]), reads=[dts, vcol], writes=[dts])
                    P.add("sp", lambda e: e.dma_start(out=dt_d.t[t0:t0 + tn, :], in_=dts.t[0:tn, :]), reads=[dts], writes=[dt_d], dma=True)
                lin_tm(hT, O_DT, HS, tts, cb_dt)

                halo_s = sbp(phB, "halo_s", [128, 48, NS * 3])
                if last:
                    sct = sbp(phB, "sct", [NS * 3, CONV])
                    P.add("sp", lambda e: e.dma_start(out=sct.t[:, :], in_=s_conv_in), writes=[sct], dma=True)
                    tr_to(lambda j, ps, ap: copy(ev_eng(), halo_s.t[:, j, :], ap, [ps], [halo_s]), sct,
                          [(j * 128, 128) for j in range(48)], NS * 3, [sct], [halo_s])
                xb = [sbp(phB, "xb", [128, 3 + QT]) for i in range(2)]
                xbs = [sbp(phB, "xbs", [128, NS, 3 + TS]) for i in range(2)]
                accs = [sbp(phB, "acc", [128, NTMAX]) for i in range(2)]
                fmb = [sbp(phB, "fmb", [128, NTMAX], BF16) for i in range(2)]
                xsts = [sbp(phB, "xst", [128, 9, 512], BF16) for i in range(2)]
                deferred = [None]
                for ct in range(48):
                    wb, wv = load_w(W["w_in"], 0, D, O_XBC + ct * 128, 128)
                    X, XS, FB = xb[ct % 2], xbs[ct % 2], fmb[ct % 2]
                    acc = accs[ct % 2]
                    P.add("pool", lambda e, X=X, ct=ct: e.tensor_copy(out=X.t[:, 0:3], in_=halo.t[:, ct, :]), reads=[halo], writes=[X])
                    if last:
                        P.add("pool", lambda e, XS=XS, ct=ct: e.tensor_copy(
                            out=XS.t[:, :, 0:3], in_=halo_s.t[:, ct, :].rearrange("p (j k) -> p j k", k=3)),
                            reads=[halo_s], writes=[XS])
                    for (b0, bw) in blks:
                        ps = next_ps()

                        def mm(e, wv=wv, ps=ps, b0=b0, bw=bw):
                            ins = None
                            for k in range(DC):
                                ins = e.matmul(ps.t[:, 0:bw], lhsT=wv[:, k, :], rhs=hT.t[:, k, b0:b0 + bw],
                                               start=(k == 0), stop=(k == DC - 1))
                            return ins
                        P.add("pe", mm, reads=[wb, hT], writes=[ps])
                        if b0 < QT:
                            copy(ev_eng(), X.t[:, 3 + b0:3 + b0 + bw], ps.t[:, 0:bw], [ps], [X])
                        else:
                            copy(ev_eng(), XS.t[:, :, 3:3 + TS], ps.t[:, 0:bw].rearrange("p (j t) -> p j t", t=TS), [ps], [XS])

                    def conv(e, X=X, XS=XS, ct=ct, acc=acc):
                        ins = e.tensor_scalar(out=acc.t[:, 0:QT], in0=X.t[:, 0:QT], scalar1=cwc.t[:, ct, 0:1],
                                              scalar2=cwc.t[:, ct, 4:5], op0=ALU.mult, op1=ALU.add)
                        for k in range(1, 4):
                            ins = e.scalar_tensor_tensor(out=acc.t[:, 0:QT], in0=X.t[:, k:k + QT], scalar=cwc.t[:, ct, k:k + 1],
                                                         in1=acc.t[:, 0:QT], op0=ALU.mult, op1=ALU.add)
                        if last:
                            a3 = acc.t[:, QT:NTMAX].rearrange("p (j t) -> p j t", t=TS)
                            ins = e.tensor_scalar(out=a3, in0=XS.t[:, :, 0:TS], scalar1=cwc.t[:, ct, 0:1],
                                                  scalar2=cwc.t[:, ct, 4:5], op0=ALU.mult, op1=ALU.add)
                            for k in range(1, 4):
                                ins = e.scalar_tensor_tensor(out=a3, in0=XS.t[:, :, k:k + TS], scalar=cwc.t[:, ct, k:k + 1],
                                                             in1=a3, op0=ALU.mult, op1=ALU.add)
                        e.tensor_copy(out=halo.t[:, ct, :], in_=X.t[:, QT:QT + 3])
                        ins = e.tensor_copy(out=convout.t[:, ct, 0:3], in_=X.t[:, QT:QT + 3])
                        if last:
                            ins = e.tensor_copy(out=convout.t[:, ct, 3:15].rearrange("p (j k) -> p j k", k=3),
                                                in_=XS.t[:, :, TS:TS + 3])
                        return ins
                    P.add("dve", conv, reads=[X, XS, cwc], writes=[acc, halo, convout])
                    if ct >= 32:
                        P.add("act", lambda e, FB=FB, acc=acc: e.activation(out=FB.t[:, 0:nt], in_=acc.t[:, 0:nt], func=AF.Silu), reads=[acc], writes=[FB])
                    if ct < 40:
                        P.add("act", lambda e, acc=acc: e.activation(out=acc.t[:, 0:nt], in_=acc.t[:, 0:nt], func=AF.Silu), reads=[acc], writes=[acc])
                    if ct >= 32:
                        P.add("sp", lambda e, FB=FB, ct=ct: e.dma_start(out=v3(bcT_d, NTMAX)[:, ct - 32, 0:nt], in_=FB.t[:, 0:nt]),
                              reads=[FB], writes=[bcT_d], dma=True)
                    if deferred[0] is not None:
                        deferred[0]()
                        deferred[0] = None
                    if ct < 40:
                        def emit_tr(ct=ct, acc=acc):
                            XST = xsts[(ct // 4) % 2]
                            for g0 in range(0, len(tts), 4):
                                ps = next_ps()
                                grp = tts[g0:g0 + 4]

                                def tr(e, ps=ps, grp=grp, acc=acc):
                                    ins = None
                                    for j, (t0, tn) in enumerate(grp):
                                        ins = e.transpose(out=ps.t[0:tn, j * 128:(j + 1) * 128], in_=acc.t[:, t0:t0 + tn],
                                                          identity=ident.t[:, :])
                                    return ins
                                P.add("pe", tr, reads=[acc, ident], writes=[ps])
                                if all(tn == 128 for (_, tn) in grp):
                                    ng = len(grp)
                                    copy(ev_eng(), XST.t[:, g0:g0 + ng, (ct % 4) * 128:(ct % 4 + 1) * 128],
                                         ps.t[:, 0:ng * 128].rearrange("p (j c) -> p j c", c=128), [ps], [XST])
                                else:
                                    for j, (t0, tn) in enumerate(grp):
                                        copy(ev_eng(), XST.t[0:tn, g0 + j, (ct % 4) * 128:(ct % 4 + 1) * 128],
                                             ps.t[0:tn, j * 128:(j + 1) * 128], [ps], [XST])
                            if ct % 4 == 3:
                                c0 = (ct // 4) * 512
                                for ti, (t0, tn) in enumerate(tts):
                                    if ct < 32:
                                        P.add("sp", lambda e, ti=ti, t0=t0, tn=tn, c0=c0: e.dma_start(
                                            out=xs_d.t[t0:t0 + tn, c0:c0 + 512], in_=XST.t[0:tn, ti, :]), reads=[XST], writes=[xs_d], dma=True)
                                    else:
                                        P.add("sp", lambda e, ti=ti, t0=t0, tn=tn, c0=c0: e.dma_start(
                                            out=b_d.t[t0:t0 + tn, c0 - DI:c0 - DI + 512], in_=XST.t[0:tn, ti, :]), reads=[XST], writes=[b_d], dma=True)
                        deferred[0] = emit_tr
                if deferred[0] is not None:
                    deferred[0]()
                    deferred[0] = None

                gst = [sbp(phB, "gst", [128, NTMAX]) for i in range(2)]
                for ct in (range(32) if not prefix else ()):
                    wb, wv = load_w(W["w_in"], 0, D, O_G + ct * 128, 128)
                    G = gst[ct % 2]
                    for (b0, bw) in blks:
                        ps = next_ps()

                        def mm(e, wv=wv, ps=ps, b0=b0, bw=bw):
                            ins = None
                            for k in range(DC):
                                ins = e.matmul(ps.t[:, 0:bw], lhsT=wv[:, k, :], rhs=hT.t[:, k, b0:b0 + bw],
                                               start=(k == 0), stop=(k == DC - 1))
                            return ins
                        P.add("pe", mm, reads=[wb, hT], writes=[ps])
                        copy(ev_eng(), G.t[:, b0:b0 + bw], ps.t[:, 0:bw], [ps], [G])
                    P.add("sp", lambda e, G=G, ct=ct: e.dma_start(out=v3(gT_d, NTMAX)[:, ct, 0:nt], in_=G.t[:, 0:nt]),
                          reads=[G], writes=[gT_d], dma=True)
                if last:
                    cst = next_stage()
                    for g0 in range(0, 48, 4):
                        ps = next_ps()

                        def tr(e, ps=ps, g0=g0):
                            ins = None
                            for j in range(4):
                                ins = e.transpose(out=ps.t[0:15, j * 128:(j + 1) * 128], in_=convout.t[:, g0 + j, 0:15],
                                                  identity=ident.t[:, :])
                            return ins
                        P.add("pe", tr, reads=[convout, ident], writes=[ps])
                        half = (g0 // 16)
                        if g0 % 16 == 0 and g0 > 0:
                            pass
                        copy(ev_eng(), cst.t[0:15, (g0 % 16) * 128:(g0 % 16) * 128 + 512], ps.t[0:15, 0:512], [ps], [cst])
                        if g0 % 16 == 12:
                            cc0 = half * 2048
                            P.add("sp", lambda e, cc0=cc0, cst=cst: e.dma_start(out=o_pconv[:, cc0:cc0 + 2048], in_=cst.t[0:3, :]), reads=[cst], dma=True)
                            P.add("sp", lambda e, cc0=cc0, cst=cst: e.dma_start(out=o_sconv[:, cc0:cc0 + 2048], in_=cst.t[3:15, :]), reads=[cst], dma=True)
                            cst = next_stage()
        P.fence()

    cst_p = din("cst_p", [128, 128 + 128 + 1])
    cst_s = din("cst_s", [128, 128 + 128 + 4])

    def attend(kh, krT, vh, qn_ap, qr_ap, ncols, ktiles, out_ap, pT, rden, masked=False):
        po, pd = psb[6], psb[7]
        nk_t = len(ktiles)
        pss = {}

        def emit_sc(i):
            (k0, kn, c_lo, diag) = ktiles[i]
            ps = next_ps()
            pss[i] = ps

            def sc(e, ps=ps, k0=k0, kn=kn, c_lo=c_lo):
                e.matmul(ps.t[0:kn, c_lo:ncols], lhsT=kh.t[:, k0:k0 + kn], rhs=qn_ap[:, c_lo:ncols], start=True, stop=False)
                return e.matmul(ps.t[0:kn, c_lo:ncols], lhsT=krT[0:64, k0:k0 + kn], rhs=qr_ap[0:64, c_lo:ncols], start=False, stop=True)
            P.add("pe", sc, reads=attn_reads, writes=[ps])

        def emit_rest(i):
            (k0, kn, c_lo, diag) = ktiles[i]
            ps = pss.pop(i)
            pt = pT[i % len(pT)]

            def ex(e, ps=ps, pt=pt, kn=kn, c_lo=c_lo, diag=diag, kb=(k0 // 128 if (masked and not diag) else None)):
                if diag:
                    e.activation(out=pt.t[0:kn, c_lo:c_lo + 64], in_=ps.t[0:kn, c_lo:c_lo + 64], func=AF.Exp, scale=SCALE,
                                 bias=negm.t[0:kn, 0:1])
                    return e.activation(out=pt.t[0:kn, c_lo + 64:ncols], in_=ps.t[0:kn, c_lo + 64:ncols], func=AF.Exp, scale=SCALE)
                if kb is not None:
                    return e.activation(out=pt.t[0:kn, c_lo:ncols], in_=ps.t[0:kn, c_lo:ncols], func=AF.Exp, scale=SCALE,
                                        bias=kneg.t[0:kn, kb:kb + 1])
                return e.activation(out=pt.t[0:kn, c_lo:ncols], in_=ps.t[0:kn, c_lo:ncols], func=AF.Exp, scale=SCALE)
            P.add("act", ex, reads=[ps, negm, kneg], writes=[pt])

            def pv(e, pt=pt, i=i, k0=k0, kn=kn, c_lo=c_lo):
                e.matmul(po.t[:, c_lo:ncols], lhsT=vh.t[0:kn, k0 // 128, :], rhs=pt.t[0:kn, c_lo:ncols], start=(i == 0), stop=(i == nk_t - 1))
                return e.matmul(pd.t[:, c_lo:ncols], lhsT=onesb.t[0:kn, :], rhs=pt.t[0:kn, c_lo:ncols], start=(i == 0), stop=(i == nk_t - 1))
            P.add("pe", pv, reads=[pt, vh, onesb], writes=[po, pd])

        LOOK = 2
        for i in range(min(LOOK, nk_t)):
            emit_sc(i)
        for i in range(nk_t):
            if i + LOOK < nk_t:
                emit_sc(i + LOOK)
            emit_rest(i)
        P.add("dve", lambda e: e.reciprocal(out=rden.t[:, 0:ncols], in_=pd.t[:, 0:ncols]), reads=[pd], writes=[rden], hz=True)
        P.add("dve", lambda e: e.tensor_tensor(out=out_ap, in0=po.t[:, 0:ncols], in1=rden.t[:, 0:ncols], op=ALU.mult),
              reads=[po, rden], writes=[oT_all_ref[0]])

    attn_reads = []
    kneg = sb("kneg", [128, SEQ // 128])
    P.add("sp", lambda e: e.dma_start(out=kneg[:], in_=kneg_in), writes=[kneg], dma=True)
    oT_all_ref = [None]

    def phase2_attn(q, last):
        nt = QT + (NS * TS if last else 0)
        blks = blocks_of(nt)
        nk = (q + 1) * QT
        with contextlib.ExitStack() as ph:
            oT_all = sbp(ph, "oT_all", [128, DC, NTMAX], BF16)
            qs_n = sbp(ph, "qs_n", [128, NH, NS * TS], BF16)
            qs_r = sbp(ph, "qs_r", [64, NH, NS * TS], BF16)
            pT = [sbp(ph, "pT", [128, 512], BF16) for i in range(3)]
            rden = sbp(ph, "rden", [128, 512])
            php = contextlib.ExitStack()
            cqT = sbp(php, "cqT", [128, 6, NTMAX], BF16)
            latT = sbp(php, "latT", [128, 5, nk], BF16)
            cosT = sbp(php, "cosT", [64, NTMAX])
            sinT = sbp(php, "sinT", [64, NTMAX])
            qn = sbp(php, "qn", [128, NTMAX], BF16)
            qr = sbp(php, "qr", [64, NTMAX], BF16)
            wsw = sbp(php, "wsw", [128, 6, 64], BF16)
            kh = sbp(php, "kh", [128, SEQ], BF16)
            vh = sbp(php, "vh", [128, SEQ // 128, 128], BF16)
            t1 = sbp(php, "t1", [64, 512])
            t2 = sbp(php, "t2", [64, 512])
            oT_all_ref[0] = oT_all
            attn_reads[:] = [kh, latT, qn, qr]
            P.add("sp", lambda e: e.dma_start(out=cqT.t[:, :, 0:nt], in_=v3(cqT_d, NTMAX)[:, :, 0:nt]), reads=[cqT_d], writes=[cqT], dma=True)
            P.add("sp", lambda e: e.dma_start(out=latT.t[:, :, :], in_=v3(latT_d, SEQ)[:, :, 0:nk]), reads=[latT_d], writes=[latT], dma=True)
            P.add("sp", lambda e: e.dma_start(out=cosT.t[:, 0:QT], in_=ropeT[0, :, q * QT:(q + 1) * QT]), writes=[cosT], dma=True)
            P.add("sp", lambda e: e.dma_start(out=sinT.t[:, 0:QT], in_=ropeT[1, :, q * QT:(q + 1) * QT]), writes=[sinT], dma=True)
            if last:
                for j in range(NS):
                    P.add("sp", lambda e, j=j: e.dma_start(out=cosT.t[:, QT + j * TS:QT + (j + 1) * TS], in_=ropeT[0, :, SEQ:SEQ + TS]), writes=[cosT], dma=True)
                    P.add("sp", lambda e, j=j: e.dma_start(out=sinT.t[:, QT + j * TS:QT + (j + 1) * TS], in_=ropeT[1, :, SEQ:SEQ + TS]), writes=[sinT], dma=True)

            def kv_for_head(wkb, wkv, srcT, nkeys):
                for (k0, kw) in blocks_of(nkeys):
                    ps = next_ps()

                    def mm(e, ps=ps, k0=k0, kw=kw):
                        ins = None
                        for k in range(4):
                            ins = e.matmul(ps.t[:, 0:kw], lhsT=wkv[:, k, 0:128], rhs=srcT.t[:, k, k0:k0 + kw], start=(k == 0), stop=(k == 3))
                        return ins
                    P.add("pe", mm, reads=[wkb, srcT], writes=[ps])
                    copy(ev_eng(), kh.t[:, k0:k0 + kw], ps.t[:, 0:kw], [ps], [kh])
                kts = blocks_of(nkeys, 128)
                for g0 in range(0, len(kts), 4):
                    ps = next_ps()
                    grp = kts[g0:g0 + 4]

                    def mv(e, ps=ps, grp=grp):
                        ins = None
                        for j, (k0, kn) in enumerate(grp):
                            for k in range(4):
                                ins = e.matmul(ps.t[0:kn, j * 128:(j + 1) * 128], lhsT=srcT.t[:, k, k0:k0 + kn], rhs=wkv[:, k, 128:256],
                                               start=(k == 0), stop=(k == 3))
                        return ins
                    P.add("pe", mv, reads=[wkb, srcT], writes=[ps])
                    for j, (k0, kn) in enumerate(grp):
                        copy(ev_eng(), vh.t[0:kn, k0 // 128, :], ps.t[0:kn, j * 128:(j + 1) * 128], [ps], [vh])

            for h in range(NH):
                wqb, wqv = load_w(W["w_qb"], 0, QL, h * 192, 192)
                wkb, wkv = load_w(W["w_kvb"], 0, KVL, h * 256, 256)
                P.add("pool", lambda e, wqv=wqv: e.tensor_copy(out=wsw.t[:, :, 0:32], in_=wqv[:, :, 160:192]), reads=[wqb], writes=[wsw])
                P.add("pool", lambda e, wqv=wqv: e.tensor_copy(out=wsw.t[:, :, 32:64], in_=wqv[:, :, 128:160]), reads=[wqb], writes=[wsw])
                for (b0, bw) in blks:
                    pn, pa, pb_ = next_ps(), next_ps(), next_ps()

                    def mq(e, wqv=wqv, pn=pn, pa=pa, pb_=pb_, b0=b0, bw=bw):
                        ins = None
                        for k in range(6):
                            ins = e.matmul(pn.t[:, 0:bw], lhsT=wqv[:, k, 0:128], rhs=cqT.t[:, k, b0:b0 + bw], start=(k == 0), stop=(k == 5))
                        for k in range(6):
                            ins = e.matmul(pa.t[0:64, 0:bw], lhsT=wqv[:, k, 128:192], rhs=cqT.t[:, k, b0:b0 + bw], start=(k == 0), stop=(k == 5))
                        for k in range(6):
                            ins = e.matmul(pb_.t[0:64, 0:bw], lhsT=wsw.t[:, k, :], rhs=cqT.t[:, k, b0:b0 + bw], start=(k == 0), stop=(k == 5))
                        return ins
                    P.add("pe", mq, reads=[wqb, wsw, cqT], writes=[pn, pa, pb_])
                    copy("act", qn.t[:, b0:b0 + bw], pn.t[:, 0:bw], [pn], [qn])

                    def rp(e, pa=pa, pb_=pb_, b0=b0, bw=bw):
                        e.tensor_tensor(out=t1.t[:, 0:bw], in0=pa.t[0:64, 0:bw], in1=cosT.t[:, b0:b0 + bw], op=ALU.mult)
                        e.tensor_tensor(out=t2.t[:, 0:bw], in0=pb_.t[0:64, 0:bw], in1=sinT.t[:, b0:b0 + bw], op=ALU.mult)
                        return e.tensor_tensor(out=qr.t[:, b0:b0 + bw], in0=t1.t[:, 0:bw], in1=t2.t[:, 0:bw], op=ALU.add)
                    P.add("dve", rp, reads=[pa, pb_, cosT, sinT], writes=[qr, t1, t2])
                if last:
                    P.add("pool", lambda e, h=h: e.tensor_copy(out=qs_n.t[:, h, :], in_=qn.t[:, QT:NTMAX]), reads=[qn], writes=[qs_n])
                    P.add("pool", lambda e, h=h: e.tensor_copy(out=qs_r.t[:, h, :], in_=qr.t[:, QT:NTMAX]), reads=[qr], writes=[qs_r])
                kv_for_head(wkb, wkv, latT, nk)
                for qb in range(2):
                    q0 = qb * 512
                    base_kt = (q * QT + q0) // 128
                    ktiles = []
                    for kt in range(base_kt + 4):
                        j = kt - base_kt
                        ktiles.append((kt * 128, 128, 128 * j if j > 0 else 0, j >= 0))
                    attend(kh, latT.t[:, 4, :], vh, qn.t[:, q0:q0 + 512], qr.t[:, q0:q0 + 512], 512, ktiles,
                           oT_all.t[:, h, q0:q0 + 512], pT, rden, masked=True)
            php.close()
            P.fence()
            if last:
                NKS = PAST + TS
                kts = blocks_of(NKS, 128)
                NKT = len(kts)
                ps_mod[0] = 5
                with contextlib.ExitStack() as ph2:
                    latS = sbp(ph2, "latS", [128, 5, NKS], BF16)
                    ltok = sbp(ph2, "ltok", [128, NKT, KVL], BF16)
                    WukT = sbp(ph2, "WukT", [128, NH, KVL], BF16)
                    Wuv = sbp(ph2, "Wuv", [128, NH, 4, 128], BF16)
                    cst = [sbp(ph2, "cstg", [128, 576]) for i in range(2)]
                    qlat = sbp(ph2, "qlat", [128, 4, 256], BF16)
                    qsr = sbp(ph2, "qsr", [64, 256], BF16)
                    olat = sbp(ph2, "olat", [128, 4, 256], BF16)
                    rdn = sbp(ph2, "rdn", [128, 256])
                    for h in range(NH):
                        wkb, wkv = load_w(W["w_kvb"], 0, KVL, h * 256, 256)
                        ps = next_ps()

                        def trw(e, ps=ps, wkv=wkv):
                            ins = None
                            for k in range(4):
                                ins = e.matmul(ps.t[:, k * 128:(k + 1) * 128], lhsT=wkv[:, k, 0:128], rhs=identb.t[:, :], start=True, stop=True)
                            return ins
                        P.add("pe", trw, reads=[wkb, identb], writes=[ps])
                        copy(ev_eng(), WukT.t[:, h, :], ps.t[:, 0:512], [ps], [WukT])
                        copy("act", Wuv.t[:, h, :, :], wkv[:, :, 128:256], [wkb], [Wuv])
                    for j in range(NS):
                        for kt in range(PAST // 128):
                            cs_ = cst[kt % 2]
                            P.add("sp", lambda e, cs_=cs_, j=j, kt=kt: e.dma_start(out=cs_.t[:, 0:KVL], in_=c_lat[j, kt * 128:(kt + 1) * 128, :]), writes=[cs_], dma=True)
                            P.add("sp", lambda e, cs_=cs_, j=j, kt=kt: e.dma_start(out=cs_.t[:, KVL:576], in_=c_kr[j, kt * 128:(kt + 1) * 128, :]), writes=[cs_], dma=True)
                            pieces = [(c * 128, 128) for c in range(4)] + [(512, 64)]
                            tr_to(lambda c, ps, ap, kt=kt: copy(ev_eng(), latS.t[0:pieces[c][1], c, kt * 128:(kt + 1) * 128], ap, [ps], [latS]),
                                  cs_, pieces, 128, [cs_], [latS])
                            copy("dve", ltok.t[:, kt, :], cs_.t[:, 0:KVL], [cs_], [ltok])
                        P.add("sp", lambda e, j=j: e.dma_start(out=latS.t[:, :, PAST:NKS], in_=v3(latS_d, NS * TS)[:, :, j * TS:(j + 1) * TS]),
                              reads=[latS_d], writes=[latS], dma=True)
                        P.add("sp", lambda e, j=j: e.dma_start(out=ltok.t[0:TS, NKT - 1, :], in_=latStok_d.t[j * TS:(j + 1) * TS, :]),
                              reads=[latStok_d], writes=[ltok], dma=True)
                        copy("act", qsr.t[:, :].rearrange("p (h t) -> p h t", t=TS), qs_r.t[:, :, j * TS:(j + 1) * TS], [qs_r], [qsr])
                        pq = [next_ps(), next_ps()]

                        def mql(e, pq=pq, j=j):
                            ins = None
                            for h in range(NH):
                                for k in range(4):
                                    ins = e.matmul(pq[k // 2].t[:, (k % 2) * 256 + h * TS:(k % 2) * 256 + (h + 1) * TS],
                                                   lhsT=WukT.t[:, h, k * 128:(k + 1) * 128], rhs=qs_n.t[:, h, j * TS:(j + 1) * TS], start=True, stop=True)
                            return ins
                        P.add("pe", mql, reads=[WukT, qs_n], writes=pq)
                        for b2 in range(2):
                            copy(ev_eng(), qlat.t[:, 2 * b2:2 * b2 + 2, :], pq[b2].t[:, 0:512].rearrange("p (k c) -> p k c", c=256), [pq[b2]], [qlat])
                        pacc = [psb[6], psb[7]]
                        pden = psb[5]
                        pss = {}

                        def e_sc(i, j=j):
                            (k0, kn) = kts[i]
                            ps = next_ps()
                            pss[i] = ps

                            def sc(e, ps=ps, k0=k0, kn=kn):
                                for k in range(4):
                                    e.matmul(ps.t[0:kn, 0:256], lhsT=latS.t[:, k, k0:k0 + kn], rhs=qlat.t[:, k, :], start=(k == 0), stop=False)
                                return e.matmul(ps.t[0:kn, 0:256], lhsT=latS.t[0:64, 4, k0:k0 + kn], rhs=qsr.t[:, :], start=False, stop=True)
                            P.add("pe", sc, reads=[latS, qlat, qsr], writes=[ps])

                        def e_rest(i):
                            (k0, kn) = kts[i]
                            ps = pss.pop(i)
                            pt = pT[i % len(pT)]
                            P.add("act", lambda e, ps=ps, pt=pt, kn=kn: e.activation(out=pt.t[0:kn, 0:256], in_=ps.t[0:kn, 0:256], func=AF.Exp, scale=SCALE),
                                  reads=[ps], writes=[pt])

                            def pv(e, pt=pt, i=i, kn=kn):
                                for k in range(4):
                                    e.matmul(pacc[k // 2].t[:, (k % 2) * 256:(k % 2 + 1) * 256], lhsT=ltok.t[0:kn, i, k * 128:(k + 1) * 128],
                                             rhs=pt.t[0:kn, 0:256], start=(i == 0), stop=(i == NKT - 1))
                                return e.matmul(pden.t[:, 0:256], lhsT=onesb.t[0:kn, :], rhs=pt.t[0:kn, 0:256], start=(i == 0), stop=(i == NKT - 1))
                            P.add("pe", pv, reads=[pt, ltok, onesb], writes=[pacc[0], pacc[1], pden])
                        for i in range(2):
                            e_sc(i)
                        for i in range(NKT):
                            if i + 2 < NKT:
                                e_sc(i + 2)
                            e_rest(i)
                        P.add("dve", lambda e: e.reciprocal(out=rdn.t[:, :], in_=pden.t[:, 0:256]), reads=[pden], writes=[rdn], hz=True)
                        for b2 in range(2):
                            P.add("dve", lambda e, b2=b2: e.tensor_tensor(
                                out=olat.t[:, 2 * b2:2 * b2 + 2, :], in0=pacc[b2].t[:, 0:512].rearrange("p (k c) -> p k c", c=256),
                                in1=rdn.t[:, :].unsqueeze(1).broadcast_to([128, 2, 256]), op=ALU.mult), reads=[pacc[b2], rdn], writes=[olat])
                        pf = next_ps()

                        def mfin(e, pf=pf):
                            ins = None
                            for h in range(NH):
                                for k in range(4):
                                    ins = e.matmul(pf.t[:, h * TS:(h + 1) * TS], lhsT=Wuv.t[:, h, k, :], rhs=olat.t[:, k, h * TS:(h + 1) * TS],
                                                   start=(k == 0), stop=(k == 3))
                            return ins
                        P.add("pe", mfin, reads=[Wuv, olat], writes=[pf])
                        copy(ev_eng(), oT_all.t[:, :, QT + j * TS:QT + (j + 1) * TS], pf.t[:, 0:256].rearrange("p (h t) -> p h t", t=TS), [pf], [oT_all])
                ps_mod[0] = 6
            P.add("sp", lambda e: e.dma_start(out=v3(oT_d, NTMAX)[:, :, 0:nt], in_=oT_all.t[:, :, 0:nt]), reads=[oT_all], writes=[oT_d], dma=True)
        P.fence()

    def phase2_ssd(q, first, last):
        prefix = not last
        with contextlib.ExitStack() as ph:
            S32 = sbp(ph, "S32", [128, DI])
            S16 = sbp(ph, "S16", [128, DI], BF16)
            cp = sbp(ph, "cp", [128, 257])
            cs = sbp(ph, "cs", [128, 260])
            nssm_b = sbp(ph, "nssm_b", [128, DI])
            xs_t = sbp(ph, "xs_t", [128, DI], BF16)
            b_t = sbp(ph, "b_t", [128, NG * NST], BF16)
            dt_t = sbp(ph, "dt_t", [128, HS])
            a_t = sbp(ph, "a_t", [128, HS])
            am = sbp(ph, "am", [128, HS])
            te = sbp(ph, "te", [128, HS])
            et = sbp(ph, "et", [128, HS])
            decb = sbp(ph, "decb", [128, HS])
            BT = sbp(ph, "BT", [128, NG, 128], BF16)
            CT = sbp(ph, "CT", [128, NG, 128], BF16)
            xdt = sbp(ph, "xdt", [128, DI], BF16)
            xw = sbp(ph, "xw", [128, DI], BF16)
            xwm = sbp(ph, "xwm", [128, DI], BF16)
            AV2 = sbp(ph, "AV2", [128, 16 * 128])
            MT2 = sbp(ph, "MT2", [128, HS * 128], BF16)
            CBm = sbp(ph, "CBm", [128, NG * 128])
            y_sb = sbp(ph, "y_sb", [128, DI])
            tmp = sbp(ph, "tmp", [128, 512])
            zt = [sbp(ph, "zt", [128, 512]) for i in range(2)]
            ssg = sbp(ph, "ssg", [128, NG])
            ynst = sbp(ph, "ynst", [128, 32, 128], BF16)
            P.add("sp", lambda e: e.dma_start(out=cp.t[:, :], in_=cst_p), writes=[cp], dma=True)
            P.add("sp", lambda e: e.dma_start(out=cs.t[:, :], in_=cst_s), writes=[cs], dma=True)
            bc_load(nssm_b, W["n_ssm"], DI)
            if first:
                P.add("dve", lambda e: e.memset(S32.t[:, :], 0.0), writes=[S32])
            else:
                P.add("sp", lambda e: e.dma_start(out=S32.t[:, :], in_=S_d.t[:, :]), reads=[S_d], writes=[S32], dma=True)
            P.add("act", lambda e: e.activation(out=S16.t[:, :], in_=S32.t[:, :], func=AF.Copy), reads=[S32], writes=[S16])

            def state_in(src2d):
                for hb in range(2):
                    stg = next_stage()
                    P.add("sp", lambda e, stg=stg, hb=hb: e.dma_start(
                        out=stg.t[:, :].rearrange("p (b n) -> p b n", n=128),
                        in_=src2d[hb * 2048:(hb + 1) * 2048, :].rearrange("(b p) n -> p b n", p=128)), writes=[stg], dma=True)
                    tr_to(lambda b, ps, ap, hb=hb: copy(ev_eng(), S32.t[:, (hb * 16 + b) * 128:(hb * 16 + b + 1) * 128], ap, [ps], [S32]),
                          stg, [(b * 128, 128) for b in range(16)], 128, [stg], [S32])
                P.add("act", lambda e: e.activation(out=S16.t[:, :], in_=S32.t[:, :], func=AF.Copy), reads=[S32], writes=[S16])

            def state_out(dst2d):
                for hb in range(2):
                    stg = next_stage()
                    tr_to(lambda b, ps, ap, stg=stg: copy(ev_eng(), stg.t[:, b * 128:(b + 1) * 128], ap, [ps], [stg]),
                          Buf(S32.t[:, hb * 2048:(hb + 1) * 2048], "S32v"), [(b * 128, 128) for b in range(16)], 128, [S32], [stg])
                    P.add("sp", lambda e, stg=stg, hb=hb: e.dma_start(
                        out=dst2d[hb * 2048:(hb + 1) * 2048, :].rearrange("(b p) n -> p b n", p=128),
                        in_=stg.t[:, :].rearrange("p (b n) -> p b n", n=128)), reads=[stg], dma=True)

            tiles = [(i * 128, 128, cp, 1, 256) for i in range(QT // 128)]
            if last:
                tiles.append((QT, NS * TS, cs, NS, 256))
            for (t0, R, C, nseg, mo) in tiles:
                sample = (t0 >= QT)
                if sample:
                    state_out(o_pssm)
                tri2 = C.t[0:R, 0:R]
                us2 = C.t[0:R, 128:128 + R]
                P.add("sp", lambda e, t0=t0, R=R: e.dma_start(out=xs_t.t[0:R, :], in_=xs_d.t[t0:t0 + R, :]), reads=[xs_d], writes=[xs_t], dma=True)
                P.add("sp", lambda e, t0=t0, R=R: e.dma_start(out=b_t.t[0:R, :], in_=b_d.t[t0:t0 + R, :]), reads=[b_d], writes=[b_t], dma=True)
                P.add("sp", lambda e, t0=t0, R=R: e.dma_start(out=dt_t.t[0:R, :], in_=dt_d.t[t0:t0 + R, :]), reads=[dt_d], writes=[dt_t], dma=True)
                P.add("sp", lambda e, t0=t0, R=R: e.dma_start(out=BT.t[:, :, 0:R], in_=v3(bcT_d, NTMAX)[:, 0:8, t0:t0 + R]), reads=[bcT_d], writes=[BT], dma=True)
                P.add("sp", lambda e, t0=t0, R=R: e.dma_start(out=CT.t[:, :, 0:R], in_=v3(bcT_d, NTMAX)[:, 8:16, t0:t0 + R]), reads=[bcT_d], writes=[CT], dma=True)
                P.add("dve", lambda e, R=R: e.tensor_tensor(out=a_t.t[0:R, :], in0=dt_t.t[0:R, :], in1=A_b.t[0:R, :], op=ALU.mult), reads=[dt_t, A_b], writes=[a_t])
                x3 = lambda b, R=R: b.t[0:R, :].rearrange("p (r c) -> p r c", c=64)
                P.add("dve", lambda e, R=R, x3=x3: e.tensor_tensor(out=x3(xdt), in0=x3(xs_t), in1=dt_t.t[0:R, :].unsqueeze(2).broadcast_to([R, HS, 64]), op=ALU.mult),
                      reads=[xs_t, dt_t], writes=[xdt])
                if not prefix:
                    P.add("dve", lambda e, R=R, x3=x3: e.tensor_tensor(out=x3(y_sb), in0=x3(xs_t), in1=dsk_b.t[0:R, :].unsqueeze(2).broadcast_to([R, HS, 64]), op=ALU.mult),
                          reads=[xs_t, dsk_b], writes=[y_sb])
                for (dst, msk) in ((te, us2), (et, tri2)):
                    ps = next_ps()
                    P.add("pe", lambda e, ps=ps, msk=msk, R=R: e.matmul(ps.t[0:R, 0:HS], lhsT=msk, rhs=a_t.t[0:R, :], start=True, stop=True), reads=[a_t, C], writes=[ps])
                    P.add("act", lambda e, ps=ps, dst=dst, R=R: e.activation(out=dst.t[0:R, :], in_=ps.t[0:R, 0:HS], func=AF.Exp), reads=[ps], writes=[dst])
                P.add("dve", lambda e, R=R, x3=x3: e.tensor_tensor(out=x3(xw), in0=x3(xdt), in1=te.t[0:R, :].unsqueeze(2).broadcast_to([R, HS, 64]), op=ALU.mult),
                      reads=[xdt, te], writes=[xw])
                if not prefix:
                    gpb = 512 // R
                    for g0 in range(0, NG, gpb):
                        ps = next_ps()

                        def mcb(e, ps=ps, g0=g0, R=R, gpb=gpb):
                            ins = None
                            for gi in range(gpb):
                                ins = e.matmul(ps.t[0:R, gi * R:(gi + 1) * R], lhsT=BT.t[:, g0 + gi, 0:R], rhs=CT.t[:, g0 + gi, 0:R], start=True, stop=True)
                            return ins
                        P.add("pe", mcb, reads=[BT, CT], writes=[ps])
                        P.add("dve", lambda e, ps=ps, g0=g0, R=R, gpb=gpb, tri2=tri2: e.tensor_tensor(
                            out=CBm.t[0:R, g0 * R:(g0 + gpb) * R].rearrange("p (g l) -> p g l", l=R),
                            in0=ps.t[0:R, 0:gpb * R].rearrange("p (g l) -> p g l", l=R),
                            in1=tri2.unsqueeze(1).broadcast_to([R, gpb, R]), op=ALU.mult), reads=[ps, C], writes=[CBm])
                    for hq in range(4):
                        P.add("dve", lambda e, hq=hq, R=R, tri2=tri2: e.tensor_tensor(
                            out=AV2.t[0:R, 0:16 * R].rearrange("p (r l) -> p r l", l=R),
                            in0=a_t.t[0:R, hq * 16:(hq + 1) * 16].unsqueeze(2).broadcast_to([R, 16, R]),
                            in1=tri2.unsqueeze(1).broadcast_to([R, 16, R]), op=ALU.mult), reads=[a_t, C], writes=[AV2])
                        nb = 16 * R // 512
                        for bk in range(nb):
                            ps = next_ps()
                            P.add("pe", lambda e, ps=ps, bk=bk, us2=us2, R=R: e.matmul(ps.t[0:R, 0:512], lhsT=us2, rhs=AV2.t[0:R, bk * 512:(bk + 1) * 512],
                                                                                  start=True, stop=True), reads=[AV2, C], writes=[ps])
                            o0 = hq * 16 * R + bk * 512
                            P.add("act", lambda e, ps=ps, o0=o0, R=R: e.activation(out=MT2.t[0:R, o0:o0 + 512], in_=ps.t[0:R, 0:512], func=AF.Exp),
                                  reads=[ps], writes=[MT2])
                        P.add("dve", lambda e, hq=hq, R=R: e.tensor_tensor(
                            out=MT2.t[0:R, hq * 16 * R:(hq + 1) * 16 * R].rearrange("p (g r l) -> p g r l", r=8, l=R),
                            in0=MT2.t[0:R, hq * 16 * R:(hq + 1) * 16 * R].rearrange("p (g r l) -> p g r l", r=8, l=R),
                            in1=CBm.t[0:R, hq * 2 * R:(hq * 2 + 2) * R].rearrange("p (g l) -> p g l", l=R).unsqueeze(2).broadcast_to([R, 2, 8, R]),
                            op=ALU.mult), reads=[MT2, CBm], writes=[MT2])
                    for g in range(NG):
                        ps = next_ps()

                        def myd(e, ps=ps, g=g, R=R):
                            ins = None
                            for rr in range(8):
                                r = g * 8 + rr
                                ins = e.matmul(ps.t[0:R, rr * 64:(rr + 1) * 64], lhsT=MT2.t[0:R, r * R:(r + 1) * R], rhs=xdt.t[0:R, r * 64:(r + 1) * 64],
                                               start=True, stop=True)
                            return ins
                        P.add("pe", myd, reads=[MT2, xdt], writes=[ps])
                        P.add("dve", lambda e, ps=ps, g=g, R=R: e.tensor_tensor(out=y_sb.t[0:R, g * 512:(g + 1) * 512], in0=ps.t[0:R, 0:512],
                                                                               in1=y_sb.t[0:R, g * 512:(g + 1) * 512], op=ALU.add), reads=[ps, y_sb], writes=[y_sb])
                for sg in range(nseg):
                    mcol = C.t[0:R, mo + sg:mo + sg + 1]
                    if sample:
                        state_in(s_ssm_in[sg])
                    AM = am if nseg > 1 else a_t
                    XWM = xwm if nseg > 1 else xw
                    if nseg > 1:
                        P.add("dve", lambda e, mcol=mcol, R=R: e.tensor_scalar_mul(out=am.t[0:R, :], in0=a_t.t[0:R, :], scalar1=mcol), reads=[a_t, C], writes=[am])
                    ps = next_ps()
                    P.add("pe", lambda e, ps=ps, R=R, AM=AM: e.matmul(ps.t[:, 0:HS], lhsT=onesf.t[0:R, :], rhs=AM.t[0:R, :], start=True, stop=True), reads=[AM, onesf], writes=[ps])
                    P.add("act", lambda e, ps=ps: e.activation(out=decb.t[:, :], in_=ps.t[:, 0:HS], func=AF.Exp), reads=[ps], writes=[decb])
                    if nseg > 1:
                        P.add("dve", lambda e, mcol=mcol, R=R: e.tensor_scalar_mul(out=xwm.t[0:R, :], in0=xw.t[0:R, :], scalar1=mcol), reads=[xw, C], writes=[xwm])
                    if not prefix:
                        for g in range(NG):
                            ps = next_ps()
                            P.add("pe", lambda e, ps=ps, g=g, R=R: e.matmul(ps.t[0:R, 0:512], lhsT=CT.t[:, g, 0:R], rhs=S16.t[:, g * 512:(g + 1) * 512], start=True, stop=True),
                                  reads=[CT, S16], writes=[ps])
                            P.add("dve", lambda e, ps=ps, g=g, R=R, mcol=mcol: e.scalar_tensor_tensor(
                                out=tmp.t[0:R, :].rearrange("p (r c) -> p r c", c=64), in0=ps.t[0:R, 0:512].rearrange("p (r c) -> p r c", c=64), scalar=mcol,
                                in1=et.t[0:R, g * 8:(g + 1) * 8].unsqueeze(2).broadcast_to([R, 8, 64]), op0=ALU.mult, op1=ALU.mult), reads=[ps, et, C], writes=[tmp])
                            P.add("dve", lambda e, g=g, R=R: e.tensor_tensor(out=y_sb.t[0:R, g * 512:(g + 1) * 512], in0=tmp.t[0:R, :],
                                                                            in1=y_sb.t[0:R, g * 512:(g + 1) * 512], op=ALU.add), reads=[tmp, y_sb], writes=[y_sb])
                    for g in range(NG):
                        ps = next_ps()
                        P.add("pe", lambda e, ps=ps, g=g, R=R, XWM=XWM: e.matmul(ps.t[:, 0:512], lhsT=b_t.t[0:R, g * 128:(g + 1) * 128], rhs=XWM.t[0:R, g * 512:(g + 1) * 512],
                                                                       start=True, stop=True), reads=[b_t, XWM], writes=[ps])
                        P.add("dve", lambda e, g=g: e.tensor_tensor(
                            out=S32.t[:, g * 512:(g + 1) * 512].rearrange("p (r c) -> p r c", c=64),
                            in0=S32.t[:, g * 512:(g + 1) * 512].rearrange("p (r c) -> p r c", c=64),
                            in1=decb.t[:, g * 8:(g + 1) * 8].unsqueeze(2).broadcast_to([128, 8, 64]), op=ALU.mult), reads=[S32, decb], writes=[S32])
                        P.add("dve", lambda e, ps=ps, g=g: e.tensor_tensor(out=S32.t[:, g * 512:(g + 1) * 512], in0=ps.t[:, 0:512],
                                                                          in1=S32.t[:, g * 512:(g + 1) * 512], op=ALU.add), reads=[ps, S32], writes=[S32])
                    if not prefix:
                        P.add("act", lambda e: e.activation(out=S16.t[:, :], in_=S32.t[:, :], func=AF.Copy), reads=[S32], writes=[S16])
                    if sample:
                        state_out(o_sssm[sg])
                if not prefix:
                    for g in range(NG):
                        Z = zt[g % 2]
                        P.add("sp", lambda e, Z=Z, g=g, t0=t0, R=R: e.dma_start(out=Z.t[0:R, :], in_=z_d.t[t0:t0 + R, g * 512:(g + 1) * 512]), reads=[z_d], writes=[Z], dma=True)
                        P.add("act", lambda e, Z=Z, R=R: e.activation(out=Z.t[0:R, :], in_=Z.t[0:R, :], func=AF.Silu), reads=[Z], writes=[Z])
                        P.add("dve", lambda e, Z=Z, g=g, R=R: e.tensor_tensor(out=y_sb.t[0:R, g * 512:(g + 1) * 512], in0=y_sb.t[0:R, g * 512:(g + 1) * 512],
                                                                             in1=Z.t[0:R, :], op=ALU.mult), reads=[Z, y_sb], writes=[y_sb])
                        P.add("act", lambda e, Z=Z, g=g, R=R: e.activation(out=Z.t[0:R, :], in_=y_sb.t[0:R, g * 512:(g + 1) * 512], func=AF.Square), reads=[y_sb], writes=[Z])
                        P.add("dve", lambda e, Z=Z, g=g, R=R: e.reduce_sum(out=ssg.t[0:R, g:g + 1], in_=Z.t[0:R, :], axis=AX.X), reads=[Z], writes=[ssg])
                    P.add("act", lambda e, R=R: e.activation(out=ssg.t[0:R, :], in_=ssg.t[0:R, :], func=AF.Sqrt, scale=1.0 / 512, bias=epsc.t[0:R, 0:1]), reads=[ssg, epsc], writes=[ssg])
                    P.add("dve", lambda e, R=R: e.reciprocal(out=ssg.t[0:R, :], in_=ssg.t[0:R, :]), reads=[ssg], writes=[ssg], hz=True)
                    P.add("dve", lambda e, R=R: e.tensor_tensor(out=y_sb.t[0:R, :].rearrange("p (g c) -> p g c", c=512), in0=y_sb.t[0:R, :].rearrange("p (g c) -> p g c", c=512),
                                                               in1=ssg.t[0:R, :].unsqueeze(2).broadcast_to([R, NG, 512]), op=ALU.mult), reads=[ssg, y_sb], writes=[y_sb])
                    P.add("dve", lambda e, R=R: e.tensor_tensor(out=y_sb.t[0:R, :], in0=y_sb.t[0:R, :], in1=nssm_b.t[0:R, :], op=ALU.mult), reads=[nssm_b, y_sb], writes=[y_sb])
                    tr_to(lambda c, ps, ap, R=R: copy(ev_eng(), ynst.t[:, c, 0:R], ap, [ps], [ynst]), y_sb, [(c * 128, 128) for c in range(32)], R, [y_sb], [ynst])
                    P.add("sp", lambda e, t0=t0, R=R: e.dma_start(out=v3(ynT_d, NTMAX)[:, :, t0:t0 + R], in_=ynst.t[:, :, 0:R]), reads=[ynst], writes=[ynT_d], dma=True)
            if last and not any(t[0] >= QT for t in tiles):
                state_out(o_pssm)
            if not last:
                P.add("sp", lambda e: e.dma_start(out=S_d.t[:, :], in_=S32.t[:, :]), reads=[S32], writes=[S_d], dma=True)
        P.fence()

    def phase3(q, last):
        nt = QT + (NS * TS if last else 0)
        blks = blocks_of(nt)
        if True:
            with contextlib.ExitStack() as ph:
                mst = [sbp(ph, "mst", [128, NTMAX], BF16) for i in range(2)]
                oT = sbp(ph, "oT3", [128, DC, NTMAX], BF16)
                ynT = sbp(ph, "ynT3", [128, 32, NTMAX], BF16)
                gA = [sbp(ph, "gA", [128, NTMAX]) for i in range(2)]
                gB = [sbp(ph, "gB", [128, NTMAX]) for i in range(2)]
                P.add("sp", lambda e: e.dma_start(out=oT.t[:, :, 0:nt], in_=v3(oT_d, NTMAX)[:, :, 0:nt]), reads=[oT_d], writes=[oT], dma=True)
                P.add("sp", lambda e: e.dma_start(out=ynT.t[:, :, 0:nt], in_=v3(ynT_d, NTMAX)[:, :, 0:nt]), reads=[ynT_d], writes=[ynT], dma=True)
                for d in range(DC):
                    wab, wav = load_w(W["w_oa"], 0, D, d * 128, 128)
                    wsb1, wsv1 = load_w(W["w_os"], 0, 2048, d * 128, 128)
                    wsb2, wsv2 = load_w(W["w_os"], 2048, 2048, d * 128, 128)
                    GA, GB = gA[d % 2], gB[d % 2]
                    P.add("sp", lambda e, GA=GA, d=d: e.dma_start(out=GA.t[:, 0:nt], in_=v3(gT_d, NTMAX)[:, d, 0:nt]), reads=[gT_d], writes=[GA], dma=True)
                    P.add("sp", lambda e, GB=GB, d=d: e.dma_start(out=GB.t[:, 0:nt], in_=v3(gT_d, NTMAX)[:, 16 + d, 0:nt]), reads=[gT_d], writes=[GB], dma=True)
                    P.add("act", lambda e, GA=GA, d=d: e.activation(out=GA.t[:, 0:nt], in_=GA.t[:, 0:nt], func=AF.Sigmoid, bias=bgc.t[:, d:d + 1]), reads=[GA, bgc], writes=[GA])
                    P.add("act", lambda e, GB=GB, d=d: e.activation(out=GB.t[:, 0:nt], in_=GB.t[:, 0:nt], func=AF.Sigmoid, bias=bgc.t[:, 16 + d:17 + d]), reads=[GB, bgc], writes=[GB])
                    for (b0, bw) in blks:
                        pa, pb_ = next_ps(), next_ps()

                        def mm(e, pa=pa, pb_=pb_, b0=b0, bw=bw, wav=wav, wsv1=wsv1, wsv2=wsv2):
                            ins = None
                            for k in range(DC):
                                ins = e.matmul(pa.t[:, 0:bw], lhsT=wav[:, k, :], rhs=oT.t[:, k, b0:b0 + bw], start=(k == 0), stop=(k == DC - 1))
                            for k in range(32):
                                wv = wsv1 if k < 16 else wsv2
                                ins = e.matmul(pb_.t[:, 0:bw], lhsT=wv[:, k % 16, :], rhs=ynT.t[:, k, b0:b0 + bw], start=(k == 0), stop=(k == 31))
                            return ins
                        P.add("pe", mm, reads=[wab, wsb1, wsb2, oT, ynT], writes=[pa, pb_])
                        P.add("dve", lambda e, pa=pa, GA=GA, b0=b0, bw=bw: e.tensor_tensor(out=GA.t[:, b0:b0 + bw], in0=pa.t[:, 0:bw], in1=GA.t[:, b0:b0 + bw], op=ALU.mult),
                              reads=[pa, GA], writes=[GA])
                        P.add("dve", lambda e, pb_=pb_, GB=GB, b0=b0, bw=bw: e.tensor_tensor(out=GB.t[:, b0:b0 + bw], in0=pb_.t[:, 0:bw], in1=GB.t[:, b0:b0 + bw], op=ALU.mult),
                              reads=[pb_, GB], writes=[GB])
                    MS = mst[d % 2]
                    P.add("pool", lambda e, GA=GA, GB=GB, MS=MS: e.tensor_tensor(out=MS.t[:, 0:nt], in0=GA.t[:, 0:nt], in1=GB.t[:, 0:nt], op=ALU.add),
                          reads=[GA, GB], writes=[MS])
                    P.add("sp", lambda e, MS=MS, d=d: e.dma_start(out=v3(mT_d, NTMAX)[:, d, 0:nt], in_=MS.t[:, 0:nt]), reads=[MS], writes=[mT_d], dma=True)
            P.fence()
        with contextlib.ExitStack() as phX:
            xT = sbp(phX, "xT3", [128, DC, NTMAX])
            hT = sbp(phX, "hT3", [128, DC, NTMAX], BF16)
            P.add("sp", lambda e: e.dma_start(out=xT.t[:, :, 0:nt], in_=v3(x1T_d, NTMAX)[:, :, 0:nt]), reads=[x1T_d], writes=[xT], dma=True)
            P.add("sp", lambda e: e.dma_start(out=hT.t[:, :, 0:nt], in_=v3(mT_d, NTMAX)[:, :, 0:nt]), reads=[mT_d], writes=[hT], dma=True)
            if True:
                for d in range(DC):
                    wob, wov = load_w(W["w_out"], 0, D, d * 128, 128)
                    for (b0, bw) in blks:
                        ps = next_ps()

                        def mo_(e, ps=ps, b0=b0, bw=bw, wov=wov):
                            ins = None
                            for k in range(DC):
                                ins = e.matmul(ps.t[:, 0:bw], lhsT=wov[:, k, :], rhs=hT.t[:, k, b0:b0 + bw], start=(k == 0), stop=(k == DC - 1))
                            return ins
                        P.add("pe", mo_, reads=[wob, hT], writes=[ps])
                        P.add("dve", lambda e, ps=ps, d=d, b0=b0, bw=bw: e.tensor_tensor(out=xT.t[:, d, b0:b0 + bw], in0=ps.t[:, 0:bw], in1=xT.t[:, d, b0:b0 + bw], op=ALU.add),
                              reads=[ps, xT], writes=[xT])
            P.fence()
            with contextlib.ExitStack() as ph:
                sqb = [sbp(ph, "sq3", [128, NTMAX], BF16) for i in range(2)]
                rstd = sbp(ph, "rstd3", [128, NTMAX])
                rmsnorm_fm(xT, sqb, rstd, 2, nt, hT)
                with contextlib.ExitStack() as phF:
                    ffn(phF, xT, hT, W["wg2"], W["wu2"], W["wd2"], nt)
                P.fence()
                rmsnorm_fm(xT, sqb, rstd, 3, nt, xT)
                for tt in range(QT // 128):
                    store_tok(y_p[tt * 128:(tt + 1) * 128, :], xT, DC, tt * 128, 128)
                if last:
                    store_tok(y_s[:, :], xT, DC, QT, NS * TS)
        P.fence()

    quarters = stages.get("quarters", list(range(NQ)))
    for qi, q in enumerate(quarters):
        last = (qi == len(quarters) - 1)
        phase1(q, last)
        if stages.get("upto") == "ffn1":
            continue
        if last:
            phase2_attn(q, last)
        phase2_ssd(q, qi == 0, last)
        if last:
            phase3(q, last)

    if stages.get("dump"):
        for nm, src in (("dbg_oT", oT_d), ("dbg_ynT", ynT_d), ("dbg_mT", mT_d), ("dbg_x1T", x1T_d), ("dbg_cqT", cqT_d),
                        ("dbg_z", z_d), ("dbg_dt", dt_d), ("dbg_xs", xs_d), ("dbg_gT", gT_d)):
            dst = nc.dram_tensor(nm, list(src.t.shape), src.t.dtype, kind="ExternalOutput").ap()
            P.add("sp", lambda e, dst=dst, src=src: e.dma_start(out=dst, in_=src.t), reads=[src], dma=True)
    P.emit(st)
    st.close()
    return nc


def rope_tables(pos_local):
    half = ROPE // 2
    inv = np.power(np.float32(10000.0), -np.arange(half, dtype=np.float32) / np.float32(half)).astype(np.float32)
    pos = np.concatenate([pos_local, PAST + np.arange(TS)]).astype(np.float32)
    ang = pos[:, None] * inv[None, :]
    cos, sin = np.cos(ang).astype(np.float32), np.sin(ang).astype(np.float32)
    cos2 = np.concatenate([cos, cos], axis=1)
    sin2 = np.concatenate([-sin, sin], axis=1)
    tok = np.ascontiguousarray(np.concatenate([cos2, sin2], axis=1))
    fm = np.ascontiguousarray(np.stack([cos2.T, sin2.T]))
    return fm, tok


def ssd_consts(L, R, nseg):
    c = np.zeros((128, 128 + 128 + nseg), np.float32)
    idx = np.arange(R)
    same = (idx[:, None] // L) == (idx[None, :] // L)
    c[:R, 0:R] = same & (idx[:, None] <= idx[None, :])
    c[:R, 128:128 + R] = same & (idx[None, :] < idx[:, None])
    for s_ in range(nseg):
        c[s_ * L:(s_ + 1) * L, 256 + s_] = 1.0
    return c


_STAGES = {}
_NCORES = [8]
_LAST = [None]
_LAST_RES = [None]
_RUNKW = {}


def kernel(**inp):
    ncores = _NCORES[0]
    nc = build_program(_STAGES)
    f = lambda a: np.ascontiguousarray(np.asarray(a, dtype=np.float32))
    shared = {
        "n_f1": f(inp["norm_ffn1"][0]), "wg1": f(inp["w_ffn1_gate"][0]), "wu1": f(inp["w_ffn1_up"][0]),
        "wd1": f(inp["w_ffn1_down"][0]), "n_mix": f(inp["norm_mix"][0]), "w_in": f(inp["w_in"][0]),
        "b_gate": f(inp["b_gate"][0]).reshape(-1), "n_qa": f(inp["norm_q_a"][0]), "w_qb": f(inp["w_q_b"][0]),
        "n_kva": f(inp["norm_kv_a"][0]), "w_kvb": f(inp["w_kv_b"][0]), "conv_w": f(inp["conv_w"][0]),
        "conv_b": f(inp["conv_b"][0]), "dt_bias": f(inp["dt_bias"][0]), "a_log": f(inp["a_log"][0]),
        "d_skip": f(inp["d_skip"][0]), "n_ssm": f(inp["norm_ssm"][0]), "w_oa": f(inp["w_o_attn"][0]),
        "w_os": f(inp["w_o_ssm"][0]), "w_out": f(inp["w_out"][0]), "n_f2": f(inp["norm_ffn2"][0]),
        "wg2": f(inp["w_ffn2_gate"][0]), "wu2": f(inp["w_ffn2_up"][0]), "wd2": f(inp["w_ffn2_down"][0]),
        "n_fin": f(inp["norm_final"]), "ident_in": np.eye(128, dtype=np.float32),
        "tri_in": np.zeros((128, 64), np.float32), "ustr_in": np.zeros((128, 64), np.float32),
        "cst_p": ssd_consts(128, 128, 1), "cst_s": ssd_consts(16, 64, 4),
    }
    in_maps = []
    for c in range(ncores):
        m = dict(shared)
        b, kq = c // NQ, c % NQ
        npre = (NQ - 1 - kq) * QT
        xl = np.zeros((SEQ, D), np.float32)
        xl[npre:] = f(inp["x_prompt"][b, 0:(kq + 1) * QT])
        pos_local = np.maximum(np.arange(SEQ) - npre, 0)
        fm, tok = rope_tables(pos_local)
        valid = (np.arange(SEQ) >= npre).astype(np.float32).reshape(SEQ, 1)
        kneg = np.where(np.arange(SEQ // 128)[None, :] * 128 >= npre, 0.0, NEG).astype(np.float32)
        m["xp"] = xl
        m["ropeT"] = fm
        m["rope_tok"] = tok
        m["valid_in"] = valid
        m["kneg_in"] = np.ascontiguousarray(np.broadcast_to(kneg, (128, SEQ // 128)))
        sl = slice(NS * c, NS * (c + 1))
        m["xs"] = f(inp["x_sample"][sl]).reshape(NS * TS, D)
        m["c_lat"] = f(inp["cache_kv_latent"][0, sl])
        m["c_kr"] = f(inp["cache_k_rope"][0, sl])
        m["s_ssm_in"] = f(inp["state_ssm"][0, sl]).reshape(NS, DI, NST)
        m["s_conv_in"] = f(inp["state_conv"][0, sl]).reshape(NS * 3, CONV)
        in_maps.append(m)
    res = run_bass_kernel_spmd(nc, in_maps, core_ids=list(range(ncores)), **_RUNKW)
    _LAST_RES[0] = res
    R = list(res.results)
    _LAST[0] = R
    while len(R) < 8:
        R.append(R[len(R) % len(res.results)])
    cat = lambda k: np.concatenate([R[c][k] for c in range(8)], axis=0)
    seqcat = lambda k: np.stack([np.concatenate([R[b * NQ + j][k] for j in range(NQ)], axis=0) for b in range(2)])
    y_prompt = seqcat("y_p")
    y_sample = cat("y_s").reshape(32, TS, D)
    p_lat = seqcat("o_plat")[None]
    p_kr = seqcat("o_pkr")[None]
    p_ssm = np.stack([R[NQ - 1]["o_pssm"], R[2 * NQ - 1]["o_pssm"]]).reshape(1, 2, HS, HS, NST)
    p_conv = np.stack([R[NQ - 1]["o_pconv"], R[2 * NQ - 1]["o_pconv"]])[None]
    s_lat = cat("o_slat").reshape(1, 32, TS, KVL)
    s_kr = cat("o_skr").reshape(1, 32, TS, ROPE)
    s_ssm = cat("o_sssm").reshape(1, 32, HS, HS, NST)
    s_conv = cat("o_sconv").reshape(1, 32, 3, CONV)
    return (y_prompt, y_sample, p_lat, p_kr, p_ssm, p_conv, s_lat, s_kr, s_ssm, s_conv)
```

```python
import contextlib
import numpy as np
import concourse.bass as bass
import concourse.mybir as mybir
from concourse.bass_utils import run_bass_kernel_spmd

F32 = mybir.dt.float32
BF16 = mybir.dt.bfloat16
ALU = mybir.AluOpType
AF = mybir.ActivationFunctionType
AX = mybir.AxisListType

D = 2048
DC = 16
SEQ = 4096
NQ = 4
QT = 1024
NS = 4
TS = 16
PAST = 2048
DFF = 5632
QL = 768
KVL = 512
ROPE = 64
NH = 16
DI = 4096
CONV = 6144
HS = 64
NG = 8
NST = 128
DIN = 15744
O_QA, O_KV, O_Z, O_XBC, O_DT, O_G = 0, 768, 1344, 5440, 11584, 11648
EPS = 1e-6
SCALE = 192 ** -0.5
NEG = -30000.0


class Buf:
    __slots__ = ("t", "name", "lw", "rd", "sc")

    def __init__(self, t, name="", sc=False):
        self.t = t
        self.name = name
        self.lw = None
        self.rd = []
        self.sc = sc

    def __getitem__(self, k):
        return self.t[k]


class Op:
    __slots__ = ("eng", "fn", "deps", "dma", "sig", "sigval", "sem", "waits", "hz")

    def __init__(self, eng, fn, dma):
        self.hz = False
        self.eng = eng
        self.fn = fn
        self.dma = dma
        self.deps = []
        self.sig = False
        self.sigval = 0
        self.sem = None
        self.waits = []


ENGS = ("pe", "act", "dve", "pool", "sp")
KDMA = 8
SAFE_SYNC = True


class Prog:
    def __init__(self, nc):
        self.nc = nc
        self.ops = []
        self.by_eng = {e: [] for e in ENGS}
        self.fence_ops = []
        self.fenced = set()

    def fence(self):
        f = []
        for e in ENGS:
            comp = [o for o in self.by_eng[e] if not o.dma]
            if comp:
                f.append(comp[-1])
            f += [o for o in self.by_eng[e] if o.dma][-KDMA:]
        self.fence_ops = f
        self.fenced = set()

    def add(self, eng, fn, reads=(), writes=(), dma=False, hz=False):
        op = Op(eng, fn, dma)
        op.hz = hz
        deps = {}
        if self.fence_ops and eng not in self.fenced:
            self.fenced.add(eng)
            for d in self.fence_ops:
                deps[id(d)] = d
        strong = set()
        for b in reads:
            if b.lw is not None:
                deps[id(b.lw)] = b.lw
                if b.sc:
                    strong.add(id(b.lw))
        for b in writes:
            if b.lw is not None:
                deps[id(b.lw)] = b.lw
            for r in b.rd:
                deps[id(r)] = r
        for d in deps.values():
            if (not d.dma) and d.eng == eng and not dma and (eng == "pe" or not (SAFE_SYNC or id(d) in strong or d.hz)):
                continue
            op.deps.append(d)
        for b in reads:
            b.rd.append(op)
        for b in writes:
            b.lw = op
            b.rd = []
        self.ops.append(op)
        self.by_eng[eng].append(op)
        return op

    def emit(self, stack):
        nc = self.nc
        for op in self.ops:
            for d in op.deps:
                d.sig = True
            if op.dma:
                op.sig = True
        csem = {e: stack.enter_context(nc.semaphore("cs_" + e)) for e in ("pe", "act", "dve", "pool")}
        dsem = {e: [stack.enter_context(nc.semaphore("ds_%s%d" % (e, i))) for i in range(KDMA)]
                for e in ("sp", "pool")}
        for e in ENGS:
            cnt = 0
            dcnt = 0
            for op in self.by_eng[e]:
                if op.dma:
                    op.sem = dsem[e][dcnt % KDMA]
                    op.sigval = 16 * (dcnt // KDMA + 1)
                    if dcnt >= KDMA:
                        op.waits.append((op.sem, 16 * (dcnt // KDMA)))
                    dcnt += 1
                elif op.sig:
                    cnt += 1
                    op.sem = csem[e]
                    op.sigval = cnt
        for e in ENGS:
            seen = {}
            for op in self.by_eng[e]:
                ws = list(op.waits)
                for d in op.deps:
                    ws.append((d.sem, d.sigval))
                best = {}
                for s, v in ws:
                    k = id(s)
                    if seen.get(k, 0) >= v:
                        continue
                    if k not in best or best[k][1] < v:
                        best[k] = (s, v)
                op.waits = list(best.values())
                for s, v in op.waits:
                    seen[id(s)] = v
        final = []
        for e in ("sp", "pool"):
            last = {}
            for op in self.by_eng[e]:
                if op.dma:
                    last[id(op.sem)] = (op.sem, op.sigval)
            final += list(last.values())
        by_eng = self.by_eng

        def run(eng_name):
            def body(eng):
                for op in by_eng[eng_name]:
                    for s, v in op.waits:
                        eng.wait_ge(s, v)
                    ins = op.fn(eng)
                    if op.sig:
                        ins.then_inc(op.sem, 16 if op.dma else 1)
                if eng_name == "sp":
                    for s, v in final:
                        eng.wait_ge(s, v)
            return body

        block = stack.enter_context(nc.Block())
        block.tensor(run("pe"))
        block.scalar(run("act"))
        block.vector(run("dve"))
        block.gpsimd(run("pool"))
        block.sync(run("sp"))


def blocks_of(n, w=512):
    out = []
    c = 0
    while c < n:
        out.append((c, min(w, n - c)))
        c += w
    return out


def build_program(stages):
    nc = bass.Bass("TRN2", target_bir_lowering=False)
    st = contextlib.ExitStack()
    P = Prog(nc)

    def din(name, shape, dt=F32):
        return nc.dram_tensor(name, list(shape), dt, kind="ExternalInput").ap()

    def dout(name, shape, dt=F32):
        return nc.dram_tensor(name, list(shape), dt, kind="ExternalOutput").ap()

    def dscr(name, shape, dt=F32):
        return Buf(nc.dram_tensor(name, list(shape), dt).ap(), name)

    def sb(name, shape, dt=F32):
        return Buf(st.enter_context(nc.sbuf_tensor(name, list(shape), dt)), name)

    xp = din("xp", [SEQ, D])
    xs = din("xs", [NS * TS, D])
    c_lat = din("c_lat", [NS, PAST, KVL])
    c_kr = din("c_kr", [NS, PAST, ROPE])
    s_ssm_in = din("s_ssm_in", [NS, DI, NST])
    s_conv_in = din("s_conv_in", [NS * 3, CONV])
    W = {}
    for nm, shp in (("n_f1", [D]), ("wg1", [D, DFF]), ("wu1", [D, DFF]), ("wd1", [DFF, D]),
                    ("n_mix", [D]), ("w_in", [D, DIN]), ("b_gate", [2 * D]), ("n_qa", [QL]),
                    ("w_qb", [QL, NH * 192]), ("n_kva", [KVL]), ("w_kvb", [KVL, NH * 256]),
                    ("conv_w", [4, CONV]), ("conv_b", [CONV]), ("dt_bias", [HS]), ("a_log", [HS]),
                    ("d_skip", [HS]), ("n_ssm", [DI]), ("w_oa", [D, D]), ("w_os", [DI, D]),
                    ("w_out", [D, D]), ("n_f2", [D]), ("wg2", [D, DFF]), ("wu2", [D, DFF]),
                    ("wd2", [DFF, D]), ("n_fin", [D])):
        W[nm] = din(nm, shp)
    ropeT = din("ropeT", [2, ROPE, SEQ + TS])
    rope_tok = din("rope_tok", [SEQ + TS, 2 * ROPE])
    ident_in = din("ident_in", [128, 128])

    valid_in = din("valid_in", [SEQ, 1])
    kneg_in = din("kneg_in", [128, SEQ // 128])
    y_p = dout("y_p", [QT, D])
    y_s = dout("y_s", [NS * TS, D])
    o_plat = dout("o_plat", [QT, KVL])
    o_pkr = dout("o_pkr", [QT, ROPE])
    o_pssm = dout("o_pssm", [DI, NST])
    o_pconv = dout("o_pconv", [3, CONV])
    o_slat = dout("o_slat", [NS * TS, KVL])
    o_skr = dout("o_skr", [NS * TS, ROPE])
    o_sssm = dout("o_sssm", [NS, DI, NST])
    o_sconv = dout("o_sconv", [NS * 3, CONV])

    uid = [0]

    SC_NAMES = ("ss", "ssg", "te", "et", "decb", "dts", "a_t", "am", "vcol", "dt_t", "rstd")

    def sbp(ph, name, shape, dt=F32):
        uid[0] += 1
        nm = "%s_%d" % (name, uid[0])
        return Buf(ph.enter_context(nc.sbuf_tensor(nm, list(shape), dt)), nm, sc=(name in SC_NAMES))

    tri_in = din("tri_in", [128, 64])
    ustr_in = din("ustr_in", [128, 64])
    ident = sb("ident", [128, 128])
    onesb = sb("onesb", [128, 128], BF16)
    onesf = sb("onesf", [128, 128])
    epsc = sb("epsc", [128, 1])
    gcols = sb("gcols", [128, 4, DC])
    triV = sb("triV", [128, 64])
    ustr = sb("ustr", [128, 64])
    P.add("sp", lambda e: e.dma_start(out=ident[:], in_=ident_in), writes=[ident], dma=True)
    P.add("sp", lambda e: e.dma_start(out=triV[:], in_=tri_in), writes=[triV], dma=True)
    P.add("sp", lambda e: e.dma_start(out=ustr[:], in_=ustr_in), writes=[ustr], dma=True)
    P.add("dve", lambda e: e.memset(onesb[:], 1.0), writes=[onesb])
    P.add("dve", lambda e: e.memset(onesf[:], 1.0), writes=[onesf])
    identb = sb("identb", [128, 128], BF16)
    P.add("dve", lambda e: e.tensor_copy(out=identb[:], in_=ident[:]), reads=[ident], writes=[identb])
    P.add("dve", lambda e: e.memset(epsc[:], EPS), writes=[epsc])
    for i, nm in enumerate(("n_f1", "n_mix", "n_f2", "n_fin")):
        P.add("sp", lambda e, i=i, nm=nm: e.dma_start(
            out=gcols[:, i, :], in_=W[nm].rearrange("(c p) -> p c", p=128),
            allow_slow_non_contiguous=True), writes=[gcols], dma=True)

    def bc_load(dst, src1d, n):
        P.add("sp", lambda e: e.dma_start(out=dst.t[:, 0:n], in_=src1d.rearrange("(o n) -> o n", o=1).broadcast_to([128, n])),
              writes=[dst], dma=True)

    nqa_b = sb("nqa_b", [128, QL])
    nkv_b = sb("nkv_b", [128, KVL])
    dtb_b = sb("dtb_b", [128, HS])
    A_b = sb("A_b", [128, HS])
    dsk_b = sb("dsk_b", [128, HS])
    bc_load(nqa_b, W["n_qa"], QL)
    bc_load(nkv_b, W["n_kva"], KVL)
    bc_load(dtb_b, W["dt_bias"], HS)
    bc_load(A_b, W["a_log"], HS)
    bc_load(dsk_b, W["d_skip"], HS)
    P.add("act", lambda e: e.activation(out=A_b[:], in_=A_b[:], func=AF.Exp), reads=[A_b], writes=[A_b])
    P.add("dve", lambda e: e.tensor_scalar_mul(out=A_b[:], in0=A_b[:], scalar1=-1.0), reads=[A_b], writes=[A_b])
    cwc = sb("cwc", [128, 48, 5])
    for k in range(4):
        P.add("sp", lambda e, k=k: e.dma_start(out=cwc[:, :, k], in_=W["conv_w"][k, :].rearrange("(c p) -> p c", p=128),
                                               allow_slow_non_contiguous=True), writes=[cwc], dma=True)
    P.add("sp", lambda e: e.dma_start(out=cwc[:, :, 4], in_=W["conv_b"].rearrange("(c p) -> p c", p=128),
                                      allow_slow_non_contiguous=True), writes=[cwc], dma=True)
    bgc = sb("bgc", [128, 32])
    P.add("sp", lambda e: e.dma_start(out=bgc[:], in_=W["b_gate"].rearrange("(c p) -> p c", p=128),
                                      allow_slow_non_contiguous=True), writes=[bgc], dma=True)
    negm = sb("negm", [128, 1])
    P.add("dve", lambda e: e.memset(negm[:], 0.0), writes=[negm])
    P.add("dve", lambda e: e.memset(negm[64:128, :], NEG), writes=[negm])
    halo = sb("halo", [128, 48, 3])
    P.add("dve", lambda e: e.memset(halo[:], 0.0), writes=[halo])
    convout = sb("convout", [128, 48, 16])

    psb = [Buf(st.enter_context(nc.psum_tensor("ps%d" % i, [128, 512], F32)), "ps%d" % i) for i in range(8)]
    ps_ctr = [0]
    ps_mod = [6]

    def next_ps():
        b = psb[ps_ctr[0] % ps_mod[0]]
        ps_ctr[0] += 1
        return b

    ev_ctr = [0]

    def ev_eng():
        ev_ctr[0] += 1
        return "act" if ev_ctr[0] % 2 else "dve"

    def copy(en, dst, src, reads, writes):
        if en == "act":
            P.add("act", lambda e: e.activation(out=dst, in_=src, func=AF.Copy), reads=reads, writes=writes)
        else:
            P.add(en, lambda e: e.tensor_copy(out=dst, in_=src), reads=reads, writes=writes)

    NTMAX = QT + NS * TS
    wbufs = [sb("wb%d" % i, [128, DC * 128], BF16) for i in range(6)]
    wb_ctr = [0]

    def load_w(wap, r0, nr, c0, ncol):
        kc = nr // 128
        wb = wbufs[wb_ctr[0] % len(wbufs)]
        wb_ctr[0] += 1
        view = wb.t[:, 0:kc * ncol].rearrange("p (c n) -> p c n", n=ncol)
        src = wap[r0:r0 + nr, c0:c0 + ncol].rearrange("(c p) n -> p c n", p=128)
        P.add("pool", lambda e: e.dma_start(out=view, in_=src), writes=[wb], dma=True)
        return wb, view

    stage_bufs = [sb("stg%d" % i, [128, D]) for i in range(2)]
    stg_ctr = [0]

    def next_stage():
        b = stage_bufs[stg_ctr[0] % len(stage_bufs)]
        stg_ctr[0] += 1
        return b

    x1T_d = dscr("x1T_d", [128, DC * NTMAX])
    cqT_d = dscr("cqT_d", [128, 6 * NTMAX], BF16)
    latT_d = dscr("latT_d", [128, 5 * SEQ], BF16)
    latS_d = dscr("latS_d", [128, 5 * NS * TS], BF16)
    latStok_d = dscr("latStok_d", [NS * TS, KVL], BF16)
    z_d = dscr("z_d", [NTMAX, DI])
    dt_d = dscr("dt_d", [NTMAX, HS])
    xs_d = dscr("xs_d", [NTMAX, DI], BF16)
    b_d = dscr("b_d", [NTMAX, NG * NST], BF16)
    bcT_d = dscr("bcT_d", [128, 16 * NTMAX], BF16)
    gT_d = dscr("gT_d", [128, 32 * NTMAX])
    oT_d = dscr("oT_d", [128, DC * NTMAX], BF16)
    ynT_d = dscr("ynT_d", [128, 32 * NTMAX], BF16)
    S_d = dscr("S_d", [128, DI])
    mT_d = dscr("mT_d", [128, DC * NTMAX], BF16)

    def v3(buf, n):
        return buf.t.rearrange("p (c n) -> p c n", n=n)

    def load_xT(xT, src_ap, ntok, col0):
        xt = next_stage()
        P.add("sp", lambda e: e.dma_start(out=xt.t[0:ntok, :], in_=src_ap), writes=[xt], dma=True)
        for g in range(4):
            ps = next_ps()

            def tr(e, g=g, ps=ps):
                ins = None
                for j in range(4):
                    c = g * 4 + j
                    ins = e.transpose(out=ps.t[:, j * 128:j * 128 + ntok], in_=xt.t[0:ntok, c * 128:(c + 1) * 128],
                                      identity=ident.t[0:ntok, 0:ntok])
                return ins
            P.add("pe", tr, reads=[xt, ident], writes=[ps])
            copy(ev_eng(), xT.t[:, g * 4:(g + 1) * 4, col0:col0 + ntok],
                 ps.t[:, :].rearrange("p (j t) -> p j t", t=128)[:, :, 0:ntok], [ps], [xT])

    def rmsnorm_fm(xT, sqb, rstd, gi, nt, out, ph_hT=True):
        blks = blocks_of(nt)
        pss = [next_ps() for _ in blks]
        for c in range(DC):
            sq = sqb[c % 2]
            P.add("act", lambda e, c=c, sq=sq: e.activation(out=sq.t[:, 0:nt], in_=xT.t[:, c, 0:nt], func=AF.Square),
                  reads=[xT], writes=[sq])
            for (b0, bw), ps in zip(blks, pss):
                P.add("pe", lambda e, c=c, sq=sq, ps=ps, b0=b0, bw=bw: e.matmul(
                    ps.t[:, 0:bw], lhsT=onesb.t[:, :], rhs=sq.t[:, b0:b0 + bw], start=(c == 0), stop=(c == DC - 1)),
                    reads=[sq, onesb], writes=[ps])
        for (b0, bw), ps in zip(blks, pss):
            P.add("act", lambda e, ps=ps, b0=b0, bw=bw: e.activation(
                out=rstd.t[:, b0:b0 + bw], in_=ps.t[:, 0:bw], func=AF.Sqrt, scale=1.0 / D, bias=epsc.t[:, 0:1]),
                reads=[ps, epsc], writes=[rstd])
            P.add("dve", lambda e, b0=b0, bw=bw: e.reciprocal(out=rstd.t[:, b0:b0 + bw], in_=rstd.t[:, b0:b0 + bw]),
                  reads=[rstd], writes=[rstd], hz=True)
        for c in range(DC):
            P.add("dve", lambda e, c=c: e.scalar_tensor_tensor(
                out=out.t[:, c, 0:nt], in0=xT.t[:, c, 0:nt], scalar=gcols.t[:, gi, c:c + 1], in1=rstd.t[:, 0:nt],
                op0=ALU.mult, op1=ALU.mult), reads=[xT, rstd, gcols], writes=[out])

    def ffn(ph, xT, hT, wg, wu, wd, nt):
        blks = blocks_of(nt)
        hb = sbp(ph, "hid", [128, 11, NTMAX], BF16)
        silu_t = [sbp(ph, "silu", [128, 512]) for i in range(2)]
        sctr = 0
        for fq in range(4):
            for fl in range(11):
                f = fq * 11 + fl
                wgb, wgv = load_w(wg, 0, D, f * 128, 128)
                wub, wuv = load_w(wu, 0, D, f * 128, 128)
                for (b0, bw) in blks:
                    pg, pu = next_ps(), next_ps()

                    def mm(e, wv=wgv, ps=pg, b0=b0, bw=bw):
                        ins = None
                        for c in range(DC):
                            ins = e.matmul(ps.t[:, 0:bw], lhsT=wv[:, c, :], rhs=hT.t[:, c, b0:b0 + bw],
                                           start=(c == 0), stop=(c == DC - 1))
                        return ins
                    P.add("pe", mm, reads=[wgb, hT], writes=[pg])
                    P.add("pe", lambda e, wv=wuv, ps=pu, b0=b0, bw=bw, mm=mm: mm(e, wv, ps, b0, bw),
                          reads=[wub, hT], writes=[pu])
                    sl = silu_t[sctr % 2]
                    sctr += 1
                    P.add("act", lambda e, sl=sl, pg=pg, bw=bw: e.activation(out=sl.t[:, 0:bw], in_=pg.t[:, 0:bw],
                                                                           func=AF.Silu), reads=[pg], writes=[sl])
                    P.add("dve", lambda e, sl=sl, pu=pu, fl=fl, b0=b0, bw=bw: e.tensor_tensor(
                        out=hb.t[:, fl, b0:b0 + bw], in0=sl.t[:, 0:bw], in1=pu.t[:, 0:bw], op=ALU.mult),
                        reads=[sl, pu], writes=[hb])
            for d in range(DC):
                wdb, wdv = load_w(wd, fq * 11 * 128, 11 * 128, d * 128, 128)
                for (b0, bw) in blks:
                    ps = next_ps()

                    def mm2(e, wv=wdv, ps=ps, b0=b0, bw=bw):
                        ins = None
                        for fl in range(11):
                            ins = e.matmul(ps.t[:, 0:bw], lhsT=wv[:, fl, :], rhs=hb.t[:, fl, b0:b0 + bw],
                                           start=(fl == 0), stop=(fl == 10))
                        return ins
                    P.add("pe", mm2, reads=[wdb, hb], writes=[ps])
                    P.add("dve", lambda e, ps=ps, d=d, b0=b0, bw=bw: e.scalar_tensor_tensor(
                        out=xT.t[:, d, b0:b0 + bw], in0=ps.t[:, 0:bw], scalar=0.5, in1=xT.t[:, d, b0:b0 + bw],
                        op0=ALU.mult, op1=ALU.add), reads=[ps, xT], writes=[xT])

    def tr_to(dst_fn, src, ncols_list, ntok, reads, writes):
        for g0 in range(0, len(ncols_list), 4):
            ps = next_ps()
            grp = ncols_list[g0:g0 + 4]

            def tr(e, ps=ps, grp=grp):
                ins = None
                for j, (c0, cw) in enumerate(grp):
                    ins = e.transpose(out=ps.t[0:cw, j * 128:j * 128 + ntok], in_=src.t[0:ntok, c0:c0 + cw],
                                      identity=ident.t[0:ntok, 0:ntok])
                return ins
            P.add("pe", tr, reads=[src, ident], writes=[ps])
            for j, (c0, cw) in enumerate(grp):
                dst_fn(g0 + j, ps, ps.t[0:cw, j * 128:j * 128 + ntok])

    def store_tok(dst_ap, srcT, nchunks, col0, ntok):
        stg = next_stage()
        for g in range((nchunks + 3) // 4):
            ps = next_ps()
            nj = min(4, nchunks - g * 4)

            def tr(e, g=g, ps=ps, nj=nj):
                ins = None
                for j in range(nj):
                    ins = e.transpose(out=ps.t[0:ntok, j * 128:(j + 1) * 128],
                                      in_=srcT.t[:, g * 4 + j, col0:col0 + ntok], identity=ident.t[:, :])
                return ins
            P.add("pe", tr, reads=[srcT, ident], writes=[ps])
            copy(ev_eng(), stg.t[0:ntok, g * 512:g * 512 + nj * 128], ps.t[0:ntok, 0:nj * 128], [ps], [stg])
        P.add("sp", lambda e: e.dma_start(out=dst_ap, in_=stg.t[0:ntok, 0:nchunks * 128]), reads=[stg], dma=True)

    def lin_tm(hT, c0, ncols, tts, cb):
        tiles = []
        c = 0
        while c < ncols:
            cw = min(128, ncols - c)
            wb, wv = load_w(W["w_in"], 0, D, c0 + c, cw)
            tiles.append((c, cw, wb, wv))
            c += cw
        def emit_mm(t0, tn):
            banks = []
            for g0 in range(0, len(tiles), 4):
                ps = next_ps()
                grp = tiles[g0:g0 + 4]

                def mm(e, ps=ps, grp=grp, t0=t0, tn=tn):
                    ins = None
                    for (cc, cw, wb, wv) in grp:
                        off = cc - grp[0][0]
                        for k in range(DC):
                            ins = e.matmul(ps.t[0:tn, off:off + cw], lhsT=hT.t[:, k, t0:t0 + tn], rhs=wv[:, k, :],
                                           start=(k == 0), stop=(k == DC - 1))
                    return ins
                P.add("pe", mm, reads=[hT] + [t[2] for t in grp], writes=[ps])
                banks.append((ps, grp[0][0], sum(t[1] for t in grp)))
            return banks
        pend = emit_mm(*tts[0])
        for ti, (t0, tn) in enumerate(tts):
            banks = pend
            if ti + 1 < len(tts) and len(tiles) <= 8:
                pend = emit_mm(*tts[ti + 1])
                cb(ti, t0, tn, banks)
            else:
                cb(ti, t0, tn, banks)
                if ti + 1 < len(tts):
                    pend = emit_mm(*tts[ti + 1])

    def rs_of(ph_bufs, src, n, tn):
        sqt, ss = ph_bufs
        P.add("act", lambda e: e.activation(out=sqt.t[0:tn, 0:n], in_=src.t[0:tn, 0:n], func=AF.Square),
              reads=[src], writes=[sqt])
        P.add("dve", lambda e: e.reduce_sum(out=ss.t[0:tn, 0:1], in_=sqt.t[0:tn, 0:n], axis=AX.X), reads=[sqt], writes=[ss])
        P.add("act", lambda e: e.activation(out=ss.t[0:tn, 0:1], in_=ss.t[0:tn, 0:1], func=AF.Sqrt, scale=1.0 / n,
                                            bias=epsc.t[0:tn, 0:1]), reads=[ss, epsc], writes=[ss])
        P.add("dve", lambda e: e.reciprocal(out=ss.t[0:tn, 0:1], in_=ss.t[0:tn, 0:1]), reads=[ss], writes=[ss], hz=True)

    def phase1(q, last):
        prefix = not last
        nt = QT + (NS * TS if last else 0)
        tts = [(i * 128, 128) for i in range(QT // 128)] + ([(QT, NS * TS)] if last else [])
        blks = blocks_of(nt)
        with contextlib.ExitStack() as phH:
            hT = sbp(phH, "hT", [128, DC, NTMAX], BF16)
            with contextlib.ExitStack() as phA:
                xT = sbp(phA, "xT", [128, DC, NTMAX])
                sqb = [sbp(phA, "sq", [128, NTMAX], BF16) for i in range(2)]
                rstd = sbp(phA, "rstd", [128, NTMAX])
                for tt in range(QT // 128):
                    load_xT(xT, xp[q * QT + tt * 128:q * QT + (tt + 1) * 128, :], 128, tt * 128)
                if last:
                    load_xT(xT, xs[:, :], NS * TS, QT)
                rmsnorm_fm(xT, sqb, rstd, 0, nt, hT)
                with contextlib.ExitStack() as phF:
                    ffn(phF, xT, hT, W["wg1"], W["wu1"], W["wd1"], nt)
                P.fence()
                if stages.get("upto") == "ffn1":
                    for tt in range(QT // 128):
                        store_tok(y_p[tt * 128:(tt + 1) * 128, :], xT, DC, tt * 128, 128)
                    return
                rmsnorm_fm(xT, sqb, rstd, 1, nt, hT)
                if not prefix:
                    P.add("sp", lambda e: e.dma_start(out=v3(x1T_d, NTMAX)[:, :, 0:nt], in_=xT.t[:, :, 0:nt]),
                          reads=[xT], writes=[x1T_d], dma=True)
            P.fence()
            with contextlib.ExitStack() as phB:
                qa_sb = sbp(phB, "qa_sb", [128, QL])
                sqt = sbp(phB, "sqt", [128, QL])
                ss = sbp(phB, "ss", [128, 1])
                cq = sbp(phB, "cq", [128, QL])
                cqs = sbp(phB, "cqs", [128, 6, 128], BF16)
                kv_sb = sbp(phB, "kv_sb", [128, 576])
                lat = sbp(phB, "lat", [128, 576])
                rtk = sbp(phB, "rtk", [128, 128])
                rt1 = sbp(phB, "rt1", [128, 64])
                rt2 = sbp(phB, "rt2", [128, 64])
                lts = sbp(phB, "lts", [128, 5, 128], BF16)

                def cb_qa(ti, t0, tn, banks):
                    for (ps, co, w) in banks:
                        copy(ev_eng(), qa_sb.t[0:tn, co:co + w], ps.t[0:tn, 0:w], [ps], [qa_sb])
                    rs_of((sqt, ss), qa_sb, QL, tn)
                    P.add("dve", lambda e: e.scalar_tensor_tensor(out=cq.t[0:tn, :], in0=qa_sb.t[0:tn, :], scalar=ss.t[0:tn, 0:1],
                                                                  in1=nqa_b.t[0:tn, :], op0=ALU.mult, op1=ALU.mult),
                          reads=[qa_sb, ss, nqa_b], writes=[cq])
                    tr_to(lambda j, ps, ap: copy(ev_eng(), cqs.t[:, j, 0:tn], ap, [ps], [cqs]), cq,
                          [(j * 128, 128) for j in range(6)], tn, [cq], [cqs])
                    P.add("sp", lambda e: e.dma_start(out=v3(cqT_d, NTMAX)[:, :, t0:t0 + tn], in_=cqs.t[:, :, 0:tn]),
                          reads=[cqs], writes=[cqT_d], dma=True)
                if not prefix:
                    lin_tm(hT, O_QA, QL, tts, cb_qa)

                def cb_kv(ti, t0, tn, banks):
                    for (ps, co, w) in banks:
                        copy(ev_eng(), kv_sb.t[0:tn, co:co + w], ps.t[0:tn, 0:w], [ps], [kv_sb])
                    rs_of((sqt, ss), kv_sb, KVL, tn)
                    P.add("dve", lambda e: e.scalar_tensor_tensor(out=lat.t[0:tn, 0:KVL], in0=kv_sb.t[0:tn, 0:KVL],
                                                                  scalar=ss.t[0:tn, 0:1], in1=nkv_b.t[0:tn, :],
                                                                  op0=ALU.mult, op1=ALU.mult),
                          reads=[kv_sb, ss, nkv_b], writes=[lat])
                    if t0 < QT:
                        P.add("sp", lambda e: e.dma_start(out=rtk.t[0:tn, :], in_=rope_tok[q * QT + t0:q * QT + t0 + tn, :]),
                              writes=[rtk], dma=True)
                    else:
                        for j in range(NS):
                            P.add("sp", lambda e, j=j: e.dma_start(out=rtk.t[j * TS:(j + 1) * TS, :], in_=rope_tok[SEQ:SEQ + TS, :]),
                                  writes=[rtk], dma=True)

                    def rp(e):
                        e.tensor_tensor(out=rt1.t[0:tn, :], in0=kv_sb.t[0:tn, 512:576], in1=rtk.t[0:tn, 0:64], op=ALU.mult)
                        e.tensor_tensor(out=rt2.t[0:tn, 0:32], in0=kv_sb.t[0:tn, 544:576], in1=rtk.t[0:tn, 64:96], op=ALU.mult)
                        e.tensor_tensor(out=rt2.t[0:tn, 32:64], in0=kv_sb.t[0:tn, 512:544], in1=rtk.t[0:tn, 96:128], op=ALU.mult)
                        return e.tensor_tensor(out=lat.t[0:tn, 512:576], in0=rt1.t[0:tn, :], in1=rt2.t[0:tn, :], op=ALU.add)
                    P.add("dve", rp, reads=[kv_sb, rtk], writes=[lat, rt1, rt2])
                    if t0 < QT:
                        if not prefix:
                            P.add("sp", lambda e: e.dma_start(out=o_plat[t0:t0 + tn, :], in_=lat.t[0:tn, 0:KVL]), reads=[lat], dma=True)
                            P.add("sp", lambda e: e.dma_start(out=o_pkr[t0:t0 + tn, :], in_=lat.t[0:tn, 512:576]), reads=[lat], dma=True)
                    else:
                        P.add("sp", lambda e: e.dma_start(out=o_slat[:, :], in_=lat.t[0:tn, 0:KVL]), reads=[lat], dma=True)
                        P.add("pool", lambda e: e.dma_start(out=latStok_d.t[:, :], in_=lat.t[0:tn, 0:KVL]), reads=[lat], writes=[latStok_d], dma=True)
                        P.add("sp", lambda e: e.dma_start(out=o_skr[:, :], in_=lat.t[0:tn, 512:576]), reads=[lat], dma=True)
                    pieces = [(j * 128, 128) for j in range(4)] + [(512, 64)]
                    tr_to(lambda j, ps, ap: copy(ev_eng(), lts.t[0:pieces[j][1], j, 0:tn], ap, [ps], [lts]), lat, pieces, tn,
                          [lat], [lts])
                    if t0 < QT:
                        r0 = q * QT + t0
                        P.add("sp", lambda e: e.dma_start(out=v3(latT_d, SEQ)[:, :, r0:r0 + tn], in_=lts.t[:, :, 0:tn]),
                              reads=[lts], writes=[latT_d], dma=True)
                    else:
                        P.add("sp", lambda e: e.dma_start(out=v3(latS_d, NS * TS)[:, :, :], in_=lts.t[:, :, 0:tn]),
                              reads=[lts], writes=[latS_d], dma=True)
                lin_tm(hT, O_KV, 576, tts, cb_kv)

                zst = [sbp(phB, "zst", [128, 512]) for i in range(2)]
                zc = [0]
                for pc in (range(8) if not prefix else ()):
                    def cb_z(ti, t0, tn, banks, pc=pc):
                        (ps, co, w) = banks[0]
                        zb = zst[zc[0] % 2]
                        zc[0] += 1
                        copy(ev_eng(), zb.t[0:tn, :], ps.t[0:tn, 0:512], [ps], [zb])
                        P.add("sp", lambda e: e.dma_start(out=z_d.t[t0:t0 + tn, pc * 512:(pc + 1) * 512], in_=zb.t[0:tn, :]),
                              reads=[zb], writes=[z_d], dma=True)
                    lin_tm(hT, O_Z + pc * 512, 512, tts, cb_z)

                dts = sbp(phB, "dts", [128, HS])
                vcol = sbp(phB, "vcol", [128, 1])

                def cb_dt(ti, t0, tn, banks):
                    (ps, co, w) = banks[0]
                    P.add("dve", lambda e: e.tensor_tensor(out=dts.t[0:tn, :], in0=ps.t[0:tn, 0:HS], in1=dtb_b.t[0:tn, :], op=ALU.add),
                          reads=[ps, dtb_b], writes=[dts])
                    P.add("act", lambda e: e.activation(out=dts.t[0:tn, :], in_=dts.t[0:tn, :], func=AF.Exp), reads=[dts], writes=[dts])
                    P.add("act", lambda e: e.activation(out=dts.t[0:tn, :], in_=dts.t[0:tn, :], func=AF.Ln, bias=onesf.t[0:tn, 0:1]),
                          reads=[dts, onesf], writes=[dts])
                    if t0 < QT:
                        P.add("sp", lambda e: e.dma_start(out=vcol.t[0:tn, :], in_=valid_in[q * QT + t0:q * QT + t0 + tn, :]), writes=[vcol], dma=True)
                        P.add("dve", lambda e: e.tensor_scalar_mul(out=dts.t[0:tn, :], in0=dts.t[0:tn, :], scalar1=vcol.t[0:tn, 0:1]), reads=[dts, vcol], writes=[dts])
                    P.add("sp", lambda e: e.dma_start(out=dt_d.t[t0:t0 + tn, :], in_=dts.t[0:tn, :]), reads=[dts], writes=[dt_d], dma=True)
                lin_tm(hT, O_DT, HS, tts, cb_dt)

                halo_s = sbp(phB, "halo_s", [128, 48, NS * 3])
                if last:
                    sct = sbp(phB, "sct", [NS * 3, CONV])
                    P.add("sp", lambda e: e.dma_start(out=sct.t[:, :], in_=s_conv_in), writes=[sct], dma=True)
                    tr_to(lambda j, ps, ap: copy(ev_eng(), halo_s.t[:, j, :], ap, [ps], [halo_s]), sct,
                          [(j * 128, 128) for j in range(48)], NS * 3, [sct], [halo_s])
                xb = [sbp(phB, "xb", [128, 3 + QT]) for i in range(2)]
                xbs = [sbp(phB, "xbs", [128, NS, 3 + TS]) for i in range(2)]
                accs = [sbp(phB, "acc", [128, NTMAX]) for i in range(2)]
                fmb = [sbp(phB, "fmb", [128, NTMAX], BF16) for i in range(3)]
                xsts = [sbp(phB, "xst", [128, 9, 512], BF16) for i in range(2)]
                deferred = [None]
                for ct in range(48):
                    wb, wv = load_w(W["w_in"], 0, D, O_XBC + ct * 128, 128)
                    X, XS, FB = xb[ct % 2], xbs[ct % 2], fmb[ct % 3]
                    acc = accs[ct % 2]
                    P.add("pool", lambda e, X=X, ct=ct: e.tensor_copy(out=X.t[:, 0:3], in_=halo.t[:, ct, :]), reads=[halo], writes=[X])
                    if last:
                        P.add("pool", lambda e, XS=XS, ct=ct: e.tensor_copy(
                            out=XS.t[:, :, 0:3], in_=halo_s.t[:, ct, :].rearrange("p (j k) -> p j k", k=3)),
                            reads=[halo_s], writes=[XS])
                    for (b0, bw) in blks:
                        ps = next_ps()

                        def mm(e, wv=wv, ps=ps, b0=b0, bw=bw):
                            ins = None
                            for k in range(DC):
                                ins = e.matmul(ps.t[:, 0:bw], lhsT=wv[:, k, :], rhs=hT.t[:, k, b0:b0 + bw],
                                               start=(k == 0), stop=(k == DC - 1))
                            return ins
                        P.add("pe", mm, reads=[wb, hT], writes=[ps])
                        if b0 < QT:
                            copy(ev_eng(), X.t[:, 3 + b0:3 + b0 + bw], ps.t[:, 0:bw], [ps], [X])
                        else:
                            copy(ev_eng(), XS.t[:, :, 3:3 + TS], ps.t[:, 0:bw].rearrange("p (j t) -> p j t", t=TS), [ps], [XS])

                    def conv(e, X=X, XS=XS, ct=ct, acc=acc):
                        ins = e.tensor_scalar(out=acc.t[:, 0:QT], in0=X.t[:, 0:QT], scalar1=cwc.t[:, ct, 0:1],
                                              scalar2=cwc.t[:, ct, 4:5], op0=ALU.mult, op1=ALU.add)
                        for k in range(1, 4):
                            ins = e.scalar_tensor_tensor(out=acc.t[:, 0:QT], in0=X.t[:, k:k + QT], scalar=cwc.t[:, ct, k:k + 1],
                                                         in1=acc.t[:, 0:QT], op0=ALU.mult, op1=ALU.add)
                        if last:
                            a3 = acc.t[:, QT:NTMAX].rearrange("p (j t) -> p j t", t=TS)
                            ins = e.tensor_scalar(out=a3, in0=XS.t[:, :, 0:TS], scalar1=cwc.t[:, ct, 0:1],
                                                  scalar2=cwc.t[:, ct, 4:5], op0=ALU.mult, op1=ALU.add)
                            for k in range(1, 4):
                                ins = e.scalar_tensor_tensor(out=a3, in0=XS.t[:, :, k:k + TS], scalar=cwc.t[:, ct, k:k + 1],
                                                             in1=a3, op0=ALU.mult, op1=ALU.add)
                        e.tensor_copy(out=halo.t[:, ct, :], in_=X.t[:, QT:QT + 3])
                        ins = e.tensor_copy(out=convout.t[:, ct, 0:3], in_=X.t[:, QT:QT + 3])
                        if last:
                            ins = e.tensor_copy(out=convout.t[:, ct, 3:15].rearrange("p (j k) -> p j k", k=3),
                                                in_=XS.t[:, :, TS:TS + 3])
                        return ins
                    P.add("dve", conv, reads=[X, XS, cwc], writes=[acc, halo, convout])
                    P.add("act", lambda e, FB=FB, acc=acc: e.activation(out=FB.t[:, 0:nt], in_=acc.t[:, 0:nt], func=AF.Silu), reads=[acc], writes=[FB])
                    if ct >= 32:
                        P.add("sp", lambda e, FB=FB, ct=ct: e.dma_start(out=v3(bcT_d, NTMAX)[:, ct - 32, 0:nt], in_=FB.t[:, 0:nt]),
                              reads=[FB], writes=[bcT_d], dma=True)
                    if deferred[0] is not None:
                        deferred[0]()
                        deferred[0] = None
                    if ct < 40:
                        def emit_tr(ct=ct, acc=FB):
                            XST = xsts[(ct // 4) % 2]
                            for g0 in range(0, len(tts), 4):
                                ps = next_ps()
                                grp = tts[g0:g0 + 4]

                                def tr(e, ps=ps, grp=grp, acc=acc):
                                    ins = None
                                    for j, (t0, tn) in enumerate(grp):
                                        ins = e.matmul(ps.t[0:tn, j * 128:(j + 1) * 128], lhsT=acc.t[:, t0:t0 + tn], rhs=identb.t[:, :],
                                                       start=True, stop=True)
                                    return ins
                                P.add("pe", tr, reads=[acc, identb], writes=[ps])
                                if all(tn == 128 for (_, tn) in grp):
                                    ng = len(grp)
                                    copy(ev_eng(), XST.t[:, g0:g0 + ng, (ct % 4) * 128:(ct % 4 + 1) * 128],
                                         ps.t[:, 0:ng * 128].rearrange("p (j c) -> p j c", c=128), [ps], [XST])
                                else:
                                    for j, (t0, tn) in enumerate(grp):
                                        copy(ev_eng(), XST.t[0:tn, g0 + j, (ct % 4) * 128:(ct % 4 + 1) * 128],
                                             ps.t[0:tn, j * 128:(j + 1) * 128], [ps], [XST])
                            if ct % 4 == 3:
                                c0 = (ct // 4) * 512
                                for ti, (t0, tn) in enumerate(tts):
                                    if ct < 32:
                                        P.add("sp", lambda e, ti=ti, t0=t0, tn=tn, c0=c0: e.dma_start(
                                            out=xs_d.t[t0:t0 + tn, c0:c0 + 512], in_=XST.t[0:tn, ti, :]), reads=[XST], writes=[xs_d], dma=True)
                                    else:
                                        P.add("sp", lambda e, ti=ti, t0=t0, tn=tn, c0=c0: e.dma_start(
                                            out=b_d.t[t0:t0 + tn, c0 - DI:c0 - DI + 512], in_=XST.t[0:tn, ti, :]), reads=[XST], writes=[b_d], dma=True)
                        deferred[0] = emit_tr
                if deferred[0] is not None:
                    deferred[0]()
                    deferred[0] = None

                gst = [sbp(phB, "gst", [128, NTMAX]) for i in range(2)]
                for ct in (range(32) if not prefix else ()):
                    wb, wv = load_w(W["w_in"], 0, D, O_G + ct * 128, 128)
                    G = gst[ct % 2]
                    for (b0, bw) in blks:
                        ps = next_ps()

                        def mm(e, wv=wv, ps=ps, b0=b0, bw=bw):
                            ins = None
                            for k in range(DC):
                                ins = e.matmul(ps.t[:, 0:bw], lhsT=wv[:, k, :], rhs=hT.t[:, k, b0:b0 + bw],
                                               start=(k == 0), stop=(k == DC - 1))
                            return ins
                        P.add("pe", mm, reads=[wb, hT], writes=[ps])
                        copy(ev_eng(), G.t[:, b0:b0 + bw], ps.t[:, 0:bw], [ps], [G])
                    P.add("sp", lambda e, G=G, ct=ct: e.dma_start(out=v3(gT_d, NTMAX)[:, ct, 0:nt], in_=G.t[:, 0:nt]),
                          reads=[G], writes=[gT_d], dma=True)
                if last:
                    cst = next_stage()
                    for g0 in range(0, 48, 4):
                        ps = next_ps()

                        def tr(e, ps=ps, g0=g0):
                            ins = None
                            for j in range(4):
                                ins = e.transpose(out=ps.t[0:15, j * 128:(j + 1) * 128], in_=convout.t[:, g0 + j, 0:15],
                                                  identity=ident.t[:, :])
                            return ins
                        P.add("pe", tr, reads=[convout, ident], writes=[ps])
                        half = (g0 // 16)
                        if g0 % 16 == 0 and g0 > 0:
                            pass
                        copy(ev_eng(), cst.t[0:15, (g0 % 16) * 128:(g0 % 16) * 128 + 512], ps.t[0:15, 0:512], [ps], [cst])
                        if g0 % 16 == 12:
                            cc0 = half * 2048
                            P.add("sp", lambda e, cc0=cc0, cst=cst: e.dma_start(out=o_pconv[:, cc0:cc0 + 2048], in_=cst.t[0:3, :]), reads=[cst], dma=True)
                            P.add("sp", lambda e, cc0=cc0, cst=cst: e.dma_start(out=o_sconv[:, cc0:cc0 + 2048], in_=cst.t[3:15, :]), reads=[cst], dma=True)
                            cst = next_stage()
        P.fence()

    cst_p = din("cst_p", [128, 128 + 128 + 1])
    cst_s = din("cst_s", [128, 128 + 128 + 4])

    def attend(kh, krT, vh, qn_ap, qr_ap, ncols, ktiles, out_ap, pT, rden, masked=False):
        po, pd = psb[6], psb[7]
        nk_t = len(ktiles)
        pss = {}

        def emit_sc(i):
            (k0, kn, c_lo, diag) = ktiles[i]
            ps = next_ps()
            pss[i] = ps

            def sc(e, ps=ps, k0=k0, kn=kn, c_lo=c_lo):
                e.matmul(ps.t[0:kn, c_lo:ncols], lhsT=kh.t[:, k0:k0 + kn], rhs=qn_ap[:, c_lo:ncols], start=True, stop=False)
                return e.matmul(ps.t[0:kn, c_lo:ncols], lhsT=krT[0:64, k0:k0 + kn], rhs=qr_ap[0:64, c_lo:ncols], start=False, stop=True)
            P.add("pe", sc, reads=attn_reads, writes=[ps])

        def emit_rest(i):
            (k0, kn, c_lo, diag) = ktiles[i]
            ps = pss.pop(i)
            pt = pT[i % len(pT)]

            def ex(e, ps=ps, pt=pt, kn=kn, c_lo=c_lo, diag=diag, kb=(k0 // 128 if (masked and not diag) else None)):
                if diag:
                    e.activation(out=pt.t[0:kn, c_lo:c_lo + 64], in_=ps.t[0:kn, c_lo:c_lo + 64], func=AF.Exp, scale=SCALE,
                                 bias=negm.t[0:kn, 0:1])
                    return e.activation(out=pt.t[0:kn, c_lo + 64:ncols], in_=ps.t[0:kn, c_lo + 64:ncols], func=AF.Exp, scale=SCALE)
                if kb is not None:
                    return e.activation(out=pt.t[0:kn, c_lo:ncols], in_=ps.t[0:kn, c_lo:ncols], func=AF.Exp, scale=SCALE,
                                        bias=kneg.t[0:kn, kb:kb + 1])
                return e.activation(out=pt.t[0:kn, c_lo:ncols], in_=ps.t[0:kn, c_lo:ncols], func=AF.Exp, scale=SCALE)
            P.add("act", ex, reads=[ps, negm, kneg], writes=[pt])

            def pv(e, pt=pt, i=i, k0=k0, kn=kn, c_lo=c_lo):
                e.matmul(po.t[:, c_lo:ncols], lhsT=vh.t[0:kn, k0 // 128, :], rhs=pt.t[0:kn, c_lo:ncols], start=(i == 0), stop=(i == nk_t - 1))
                return e.matmul(pd.t[:, c_lo:ncols], lhsT=onesb.t[0:kn, :], rhs=pt.t[0:kn, c_lo:ncols], start=(i == 0), stop=(i == nk_t - 1))
            P.add("pe", pv, reads=[pt, vh, onesb], writes=[po, pd])

        LOOK = 2
        for i in range(min(LOOK, nk_t)):
            emit_sc(i)
        for i in range(nk_t):
            if i + LOOK < nk_t:
                emit_sc(i + LOOK)
            emit_rest(i)
        P.add("dve", lambda e: e.reciprocal(out=rden.t[:, 0:ncols], in_=pd.t[:, 0:ncols]), reads=[pd], writes=[rden], hz=True)
        P.add("dve", lambda e: e.tensor_tensor(out=out_ap, in0=po.t[:, 0:ncols], in1=rden.t[:, 0:ncols], op=ALU.mult),
              reads=[po, rden], writes=[oT_all_ref[0]])

    attn_reads = []
    kneg = sb("kneg", [128, SEQ // 128])
    P.add("sp", lambda e: e.dma_start(out=kneg[:], in_=kneg_in), writes=[kneg], dma=True)
    oT_all_ref = [None]

    def phase2_attn(q, last):
        nt = QT + (NS * TS if last else 0)
        blks = blocks_of(nt)
        nk = (q + 1) * QT
        with contextlib.ExitStack() as ph:
            oT_all = sbp(ph, "oT_all", [128, DC, NTMAX], BF16)
            qs_n = sbp(ph, "qs_n", [128, NH, NS * TS], BF16)
            qs_r = sbp(ph, "qs_r", [64, NH, NS * TS], BF16)
            pT = [sbp(ph, "pT", [128, 512], BF16) for i in range(3)]
            rden = sbp(ph, "rden", [128, 512])
            php = contextlib.ExitStack()
            cqT = sbp(php, "cqT", [128, 6, NTMAX], BF16)
            latT = sbp(php, "latT", [128, 5, nk], BF16)
            cosT = sbp(php, "cosT", [64, NTMAX])
            sinT = sbp(php, "sinT", [64, NTMAX])
            qn = sbp(php, "qn", [128, NTMAX], BF16)
            qr = sbp(php, "qr", [64, NTMAX], BF16)
            wsw = sbp(php, "wsw", [128, 6, 64], BF16)
            kh = sbp(php, "kh", [128, SEQ], BF16)
            vh = sbp(php, "vh", [128, SEQ // 128, 128], BF16)
            t1 = sbp(php, "t1", [64, 512])
            t2 = sbp(php, "t2", [64, 512])
            oT_all_ref[0] = oT_all
            attn_reads[:] = [kh, latT, qn, qr]
            P.add("sp", lambda e: e.dma_start(out=cqT.t[:, :, 0:nt], in_=v3(cqT_d, NTMAX)[:, :, 0:nt]), reads=[cqT_d], writes=[cqT], dma=True)
            P.add("sp", lambda e: e.dma_start(out=latT.t[:, :, :], in_=v3(latT_d, SEQ)[:, :, 0:nk]), reads=[latT_d], writes=[latT], dma=True)
            P.add("sp", lambda e: e.dma_start(out=cosT.t[:, 0:QT], in_=ropeT[0, :, q * QT:(q + 1) * QT]), writes=[cosT], dma=True)
            P.add("sp", lambda e: e.dma_start(out=sinT.t[:, 0:QT], in_=ropeT[1, :, q * QT:(q + 1) * QT]), writes=[sinT], dma=True)
            if last:
                for j in range(NS):
                    P.add("sp", lambda e, j=j: e.dma_start(out=cosT.t[:, QT + j * TS:QT + (j + 1) * TS], in_=ropeT[0, :, SEQ:SEQ + TS]), writes=[cosT], dma=True)
                    P.add("sp", lambda e, j=j: e.dma_start(out=sinT.t[:, QT + j * TS:QT + (j + 1) * TS], in_=ropeT[1, :, SEQ:SEQ + TS]), writes=[sinT], dma=True)

            def kv_for_head(wkb, wkv, srcT, nkeys):
                for (k0, kw) in blocks_of(nkeys):
                    ps = next_ps()

                    def mm(e, ps=ps, k0=k0, kw=kw):
                        ins = None
                        for k in range(4):
                            ins = e.matmul(ps.t[:, 0:kw], lhsT=wkv[:, k, 0:128], rhs=srcT.t[:, k, k0:k0 + kw], start=(k == 0), stop=(k == 3))
                        return ins
                    P.add("pe", mm, reads=[wkb, srcT], writes=[ps])
                    copy(ev_eng(), kh.t[:, k0:k0 + kw], ps.t[:, 0:kw], [ps], [kh])
                kts = blocks_of(nkeys, 128)
                for g0 in range(0, len(kts), 4):
                    ps = next_ps()
                    grp = kts[g0:g0 + 4]

                    def mv(e, ps=ps, grp=grp):
                        ins = None
                        for j, (k0, kn) in enumerate(grp):
                            for k in range(4):
                                ins = e.matmul(ps.t[0:kn, j * 128:(j + 1) * 128], lhsT=srcT.t[:, k, k0:k0 + kn], rhs=wkv[:, k, 128:256],
                                               start=(k == 0), stop=(k == 3))
                        return ins
                    P.add("pe", mv, reads=[wkb, srcT], writes=[ps])
                    for j, (k0, kn) in enumerate(grp):
                        copy(ev_eng(), vh.t[0:kn, k0 // 128, :], ps.t[0:kn, j * 128:(j + 1) * 128], [ps], [vh])

            for h in range(NH):
                wqb, wqv = load_w(W["w_qb"], 0, QL, h * 192, 192)
                wkb, wkv = load_w(W["w_kvb"], 0, KVL, h * 256, 256)
                P.add("pool", lambda e, wqv=wqv: e.tensor_copy(out=wsw.t[:, :, 0:32], in_=wqv[:, :, 160:192]), reads=[wqb], writes=[wsw])
                P.add("pool", lambda e, wqv=wqv: e.tensor_copy(out=wsw.t[:, :, 32:64], in_=wqv[:, :, 128:160]), reads=[wqb], writes=[wsw])
                for (b0, bw) in blks:
                    pn, pa, pb_ = next_ps(), next_ps(), next_ps()

                    def mq(e, wqv=wqv, pn=pn, pa=pa, pb_=pb_, b0=b0, bw=bw):
                        ins = None
                        for k in range(6):
                            ins = e.matmul(pn.t[:, 0:bw], lhsT=wqv[:, k, 0:128], rhs=cqT.t[:, k, b0:b0 + bw], start=(k == 0), stop=(k == 5))
                        for k in range(6):
                            ins = e.matmul(pa.t[0:64, 0:bw], lhsT=wqv[:, k, 128:192], rhs=cqT.t[:, k, b0:b0 + bw], start=(k == 0), stop=(k == 5))
                        for k in range(6):
                            ins = e.matmul(pb_.t[0:64, 0:bw], lhsT=wsw.t[:, k, :], rhs=cqT.t[:, k, b0:b0 + bw], start=(k == 0), stop=(k == 5))
                        return ins
                    P.add("pe", mq, reads=[wqb, wsw, cqT], writes=[pn, pa, pb_])
                    copy("act", qn.t[:, b0:b0 + bw], pn.t[:, 0:bw], [pn], [qn])

                    def rp(e, pa=pa, pb_=pb_, b0=b0, bw=bw):
                        e.tensor_tensor(out=t1.t[:, 0:bw], in0=pa.t[0:64, 0:bw], in1=cosT.t[:, b0:b0 + bw], op=ALU.mult)
                        e.tensor_tensor(out=t2.t[:, 0:bw], in0=pb_.t[0:64, 0:bw], in1=sinT.t[:, b0:b0 + bw], op=ALU.mult)
                        return e.tensor_tensor(out=qr.t[:, b0:b0 + bw], in0=t1.t[:, 0:bw], in1=t2.t[:, 0:bw], op=ALU.add)
                    P.add("dve", rp, reads=[pa, pb_, cosT, sinT], writes=[qr, t1, t2])
                if last:
                    P.add("pool", lambda e, h=h: e.tensor_copy(out=qs_n.t[:, h, :], in_=qn.t[:, QT:NTMAX]), reads=[qn], writes=[qs_n])
                    P.add("pool", lambda e, h=h: e.tensor_copy(out=qs_r.t[:, h, :], in_=qr.t[:, QT:NTMAX]), reads=[qr], writes=[qs_r])
                kv_for_head(wkb, wkv, latT, nk)
                for qb in range(2):
                    q0 = qb * 512
                    base_kt = (q * QT + q0) // 128
                    ktiles = []
                    for kt in range(base_kt + 4):
                        j = kt - base_kt
                        ktiles.append((kt * 128, 128, 128 * j if j > 0 else 0, j >= 0))
                    attend(kh, latT.t[:, 4, :], vh, qn.t[:, q0:q0 + 512], qr.t[:, q0:q0 + 512], 512, ktiles,
                           oT_all.t[:, h, q0:q0 + 512], pT, rden, masked=True)
            php.close()
            P.fence()
            if last:
                NKS = PAST + TS
                kts = blocks_of(NKS, 128)
                NKT = len(kts)
                ps_mod[0] = 5
                with contextlib.ExitStack() as ph2:
                    latS = sbp(ph2, "latS", [128, 5, NKS], BF16)
                    ltok = sbp(ph2, "ltok", [128, NKT, KVL], BF16)
                    WukT = sbp(ph2, "WukT", [128, NH, KVL], BF16)
                    Wuv = sbp(ph2, "Wuv", [128, NH, 4, 128], BF16)
                    cst = [sbp(ph2, "cstg", [128, 576]) for i in range(2)]
                    qlat = sbp(ph2, "qlat", [128, 4, 256], BF16)
                    qsr = sbp(ph2, "qsr", [64, 256], BF16)
                    olat = sbp(ph2, "olat", [128, 4, 256], BF16)
                    rdn = sbp(ph2, "rdn", [128, 256])
                    for h in range(NH):
                        wkb, wkv = load_w(W["w_kvb"], 0, KVL, h * 256, 256)
                        ps = next_ps()

                        def trw(e, ps=ps, wkv=wkv):
                            ins = None
                            for k in range(4):
                                ins = e.matmul(ps.t[:, k * 128:(k + 1) * 128], lhsT=wkv[:, k, 0:128], rhs=identb.t[:, :], start=True, stop=True)
                            return ins
                        P.add("pe", trw, reads=[wkb, identb], writes=[ps])
                        copy(ev_eng(), WukT.t[:, h, :], ps.t[:, 0:512], [ps], [WukT])
                        copy("act", Wuv.t[:, h, :, :], wkv[:, :, 128:256], [wkb], [Wuv])
                    for j in range(NS):
                        for kt in range(PAST // 128):
                            cs_ = cst[kt % 2]
                            P.add("sp", lambda e, cs_=cs_, j=j, kt=kt: e.dma_start(out=cs_.t[:, 0:KVL], in_=c_lat[j, kt * 128:(kt + 1) * 128, :]), writes=[cs_], dma=True)
                            P.add("sp", lambda e, cs_=cs_, j=j, kt=kt: e.dma_start(out=cs_.t[:, KVL:576], in_=c_kr[j, kt * 128:(kt + 1) * 128, :]), writes=[cs_], dma=True)
                            pieces = [(c * 128, 128) for c in range(4)] + [(512, 64)]
                            tr_to(lambda c, ps, ap, kt=kt: copy(ev_eng(), latS.t[0:pieces[c][1], c, kt * 128:(kt + 1) * 128], ap, [ps], [latS]),
                                  cs_, pieces, 128, [cs_], [latS])
                            copy("dve", ltok.t[:, kt, :], cs_.t[:, 0:KVL], [cs_], [ltok])
                        P.add("sp", lambda e, j=j: e.dma_start(out=latS.t[:, :, PAST:NKS], in_=v3(latS_d, NS * TS)[:, :, j * TS:(j + 1) * TS]),
                              reads=[latS_d], writes=[latS], dma=True)
                        P.add("sp", lambda e, j=j: e.dma_start(out=ltok.t[0:TS, NKT - 1, :], in_=latStok_d.t[j * TS:(j + 1) * TS, :]),
                              reads=[latStok_d], writes=[ltok], dma=True)
                        copy("act", qsr.t[:, :].rearrange("p (h t) -> p h t", t=TS), qs_r.t[:, :, j * TS:(j + 1) * TS], [qs_r], [qsr])
                        pq = [next_ps(), next_ps()]

                        def mql(e, pq=pq, j=j):
                            ins = None
                            for h in range(NH):
                                for k in range(4):
                                    ins = e.matmul(pq[k // 2].t[:, (k % 2) * 256 + h * TS:(k % 2) * 256 + (h + 1) * TS],
                                                   lhsT=WukT.t[:, h, k * 128:(k + 1) * 128], rhs=qs_n.t[:, h, j * TS:(j + 1) * TS], start=True, stop=True)
                            return ins
                        P.add("pe", mql, reads=[WukT, qs_n], writes=pq)
                        for b2 in range(2):
                            copy(ev_eng(), qlat.t[:, 2 * b2:2 * b2 + 2, :], pq[b2].t[:, 0:512].rearrange("p (k c) -> p k c", c=256), [pq[b2]], [qlat])
                        pacc = [psb[6], psb[7]]
                        pden = psb[5]
                        pss = {}

                        def e_sc(i, j=j):
                            (k0, kn) = kts[i]
                            ps = next_ps()
                            pss[i] = ps

                            def sc(e, ps=ps, k0=k0, kn=kn):
                                for k in range(4):
                                    e.matmul(ps.t[0:kn, 0:256], lhsT=latS.t[:, k, k0:k0 + kn], rhs=qlat.t[:, k, :], start=(k == 0), stop=False)
                                return e.matmul(ps.t[0:kn, 0:256], lhsT=latS.t[0:64, 4, k0:k0 + kn], rhs=qsr.t[:, :], start=False, stop=True)
                            P.add("pe", sc, reads=[latS, qlat, qsr], writes=[ps])

                        def e_rest(i):
                            (k0, kn) = kts[i]
                            ps = pss.pop(i)
                            pt = pT[i % len(pT)]
                            P.add("act", lambda e, ps=ps, pt=pt, kn=kn: e.activation(out=pt.t[0:kn, 0:256], in_=ps.t[0:kn, 0:256], func=AF.Exp, scale=SCALE),
                                  reads=[ps], writes=[pt])

                            def pv(e, pt=pt, i=i, kn=kn):
                                for k in range(4):
                                    e.matmul(pacc[k // 2].t[:, (k % 2) * 256:(k % 2 + 1) * 256], lhsT=ltok.t[0:kn, i, k * 128:(k + 1) * 128],
                                             rhs=pt.t[0:kn, 0:256], start=(i == 0), stop=(i == NKT - 1))
                                return e.matmul(pden.t[:, 0:256], lhsT=onesb.t[0:kn, :], rhs=pt.t[0:kn, 0:256], start=(i == 0), stop=(i == NKT - 1))
                            P.add("pe", pv, reads=[pt, ltok, onesb], writes=[pacc[0], pacc[1], pden])
                        for i in range(2):
                            e_sc(i)
                        for i in range(NKT):
                            if i + 2 < NKT:
                                e_sc(i + 2)
                            e_rest(i)
                        P.add("dve", lambda e: e.reciprocal(out=rdn.t[:, :], in_=pden.t[:, 0:256]), reads=[pden], writes=[rdn], hz=True)
                        for b2 in range(2):
                            P.add("dve", lambda e, b2=b2: e.tensor_tensor(
                                out=olat.t[:, 2 * b2:2 * b2 + 2, :], in0=pacc[b2].t[:, 0:512].rearrange("p (k c) -> p k c", c=256),
                                in1=rdn.t[:, :].unsqueeze(1).broadcast_to([128, 2, 256]), op=ALU.mult), reads=[pacc[b2], rdn], writes=[olat])
                        pf = next_ps()

                        def mfin(e, pf=pf):
                            ins = None
                            for h in range(NH):
                                for k in range(4):
                                    ins = e.matmul(pf.t[:, h * TS:(h + 1) * TS], lhsT=Wuv.t[:, h, k, :], rhs=olat.t[:, k, h * TS:(h + 1) * TS],
                                                   start=(k == 0), stop=(k == 3))
                            return ins
                        P.add("pe", mfin, reads=[Wuv, olat], writes=[pf])
                        copy(ev_eng(), oT_all.t[:, :, QT + j * TS:QT + (j + 1) * TS], pf.t[:, 0:256].rearrange("p (h t) -> p h t", t=TS), [pf], [oT_all])
                ps_mod[0] = 6
            P.add("sp", lambda e: e.dma_start(out=v3(oT_d, NTMAX)[:, :, 0:nt], in_=oT_all.t[:, :, 0:nt]), reads=[oT_all], writes=[oT_d], dma=True)
        P.fence()

    def phase2_ssd(q, first, last):
        prefix = not last
        with contextlib.ExitStack() as ph:
            S32 = sbp(ph, "S32", [128, DI])
            S16 = sbp(ph, "S16", [128, DI], BF16)
            cp = sbp(ph, "cp", [128, 257])
            cs = sbp(ph, "cs", [128, 260])
            nssm_b = sbp(ph, "nssm_b", [128, DI])
            xs_t = sbp(ph, "xs_t", [128, DI], BF16)
            b_t = sbp(ph, "b_t", [128, NG * NST], BF16)
            dt_t = sbp(ph, "dt_t", [128, HS])
            a_t = sbp(ph, "a_t", [128, HS])
            am = sbp(ph, "am", [128, HS])
            te = sbp(ph, "te", [128, HS])
            et = sbp(ph, "et", [128, HS])
            decb = sbp(ph, "decb", [128, HS])
            BT = sbp(ph, "BT", [128, NG, 128], BF16)
            CT = sbp(ph, "CT", [128, NG, 128], BF16)
            xdt = sbp(ph, "xdt", [128, DI], BF16)
            xw = sbp(ph, "xw", [128, DI], BF16)
            xwm = sbp(ph, "xwm", [128, DI], BF16)
            AV2 = sbp(ph, "AV2", [128, 16 * 128])
            MT2 = sbp(ph, "MT2", [128, HS * 128], BF16)
            CBm = sbp(ph, "CBm", [128, NG * 128])
            y_sb = sbp(ph, "y_sb", [128, DI])
            tmp = sbp(ph, "tmp", [128, 512])
            zt = [sbp(ph, "zt", [128, 512]) for i in range(2)]
            ssg = sbp(ph, "ssg", [128, NG])
            ynst = sbp(ph, "ynst", [128, 32, 128], BF16)
            P.add("sp", lambda e: e.dma_start(out=cp.t[:, :], in_=cst_p), writes=[cp], dma=True)
            P.add("sp", lambda e: e.dma_start(out=cs.t[:, :], in_=cst_s), writes=[cs], dma=True)
            bc_load(nssm_b, W["n_ssm"], DI)
            if first:
                P.add("dve", lambda e: e.memset(S32.t[:, :], 0.0), writes=[S32])
            else:
                P.add("sp", lambda e: e.dma_start(out=S32.t[:, :], in_=S_d.t[:, :]), reads=[S_d], writes=[S32], dma=True)
            P.add("act", lambda e: e.activation(out=S16.t[:, :], in_=S32.t[:, :], func=AF.Copy), reads=[S32], writes=[S16])

            def state_in(src2d):
                for hb in range(2):
                    stg = next_stage()
                    P.add("sp", lambda e, stg=stg, hb=hb: e.dma_start(
                        out=stg.t[:, :].rearrange("p (b n) -> p b n", n=128),
                        in_=src2d[hb * 2048:(hb + 1) * 2048, :].rearrange("(b p) n -> p b n", p=128)), writes=[stg], dma=True)
                    tr_to(lambda b, ps, ap, hb=hb: copy(ev_eng(), S32.t[:, (hb * 16 + b) * 128:(hb * 16 + b + 1) * 128], ap, [ps], [S32]),
                          stg, [(b * 128, 128) for b in range(16)], 128, [stg], [S32])
                P.add("act", lambda e: e.activation(out=S16.t[:, :], in_=S32.t[:, :], func=AF.Copy), reads=[S32], writes=[S16])

            def state_out(dst2d):
                for hb in range(2):
                    stg = next_stage()
                    tr_to(lambda b, ps, ap, stg=stg: copy(ev_eng(), stg.t[:, b * 128:(b + 1) * 128], ap, [ps], [stg]),
                          Buf(S32.t[:, hb * 2048:(hb + 1) * 2048], "S32v"), [(b * 128, 128) for b in range(16)], 128, [S32], [stg])
                    P.add("sp", lambda e, stg=stg, hb=hb: e.dma_start(
                        out=dst2d[hb * 2048:(hb + 1) * 2048, :].rearrange("(b p) n -> p b n", p=128),
                        in_=stg.t[:, :].rearrange("p (b n) -> p b n", n=128)), reads=[stg], dma=True)

            tiles = [(i * 128, 128, cp, 1, 256) for i in range(QT // 128)]
            if last:
                tiles.append((QT, NS * TS, cs, NS, 256))
            for (t0, R, C, nseg, mo) in tiles:
                sample = (t0 >= QT)
                if sample:
                    state_out(o_pssm)
                tri2 = C.t[0:R, 0:R]
                us2 = C.t[0:R, 128:128 + R]
                P.add("sp", lambda e, t0=t0, R=R: e.dma_start(out=xs_t.t[0:R, :], in_=xs_d.t[t0:t0 + R, :]), reads=[xs_d], writes=[xs_t], dma=True)
                P.add("sp", lambda e, t0=t0, R=R: e.dma_start(out=b_t.t[0:R, :], in_=b_d.t[t0:t0 + R, :]), reads=[b_d], writes=[b_t], dma=True)
                P.add("sp", lambda e, t0=t0, R=R: e.dma_start(out=dt_t.t[0:R, :], in_=dt_d.t[t0:t0 + R, :]), reads=[dt_d], writes=[dt_t], dma=True)
                P.add("sp", lambda e, t0=t0, R=R: e.dma_start(out=BT.t[:, :, 0:R], in_=v3(bcT_d, NTMAX)[:, 0:8, t0:t0 + R]), reads=[bcT_d], writes=[BT], dma=True)
                P.add("sp", lambda e, t0=t0, R=R: e.dma_start(out=CT.t[:, :, 0:R], in_=v3(bcT_d, NTMAX)[:, 8:16, t0:t0 + R]), reads=[bcT_d], writes=[CT], dma=True)
                P.add("dve", lambda e, R=R: e.tensor_tensor(out=a_t.t[0:R, :], in0=dt_t.t[0:R, :], in1=A_b.t[0:R, :], op=ALU.mult), reads=[dt_t, A_b], writes=[a_t])
                x3 = lambda b, R=R: b.t[0:R, :].rearrange("p (r c) -> p r c", c=64)
                P.add("dve", lambda e, R=R, x3=x3: e.tensor_tensor(out=x3(xdt), in0=x3(xs_t), in1=dt_t.t[0:R, :].unsqueeze(2).broadcast_to([R, HS, 64]), op=ALU.mult),
                      reads=[xs_t, dt_t], writes=[xdt])
                if not prefix:
                    P.add("dve", lambda e, R=R, x3=x3: e.tensor_tensor(out=x3(y_sb), in0=x3(xs_t), in1=dsk_b.t[0:R, :].unsqueeze(2).broadcast_to([R, HS, 64]), op=ALU.mult),
                          reads=[xs_t, dsk_b], writes=[y_sb])
                for (dst, msk) in ((te, us2), (et, tri2)):
                    ps = next_ps()
                    P.add("pe", lambda e, ps=ps, msk=msk, R=R: e.matmul(ps.t[0:R, 0:HS], lhsT=msk, rhs=a_t.t[0:R, :], start=True, stop=True), reads=[a_t, C], writes=[ps])
                    P.add("act", lambda e, ps=ps, dst=dst, R=R: e.activation(out=dst.t[0:R, :], in_=ps.t[0:R, 0:HS], func=AF.Exp), reads=[ps], writes=[dst])
                P.add("dve", lambda e, R=R, x3=x3: e.tensor_tensor(out=x3(xw), in0=x3(xdt), in1=te.t[0:R, :].unsqueeze(2).broadcast_to([R, HS, 64]), op=ALU.mult),
                      reads=[xdt, te], writes=[xw])
                if not prefix:
                    gpb = 512 // R
                    for g0 in range(0, NG, gpb):
                        ps = next_ps()

                        def mcb(e, ps=ps, g0=g0, R=R, gpb=gpb):
                            ins = None
                            for gi in range(gpb):
                                ins = e.matmul(ps.t[0:R, gi * R:(gi + 1) * R], lhsT=BT.t[:, g0 + gi, 0:R], rhs=CT.t[:, g0 + gi, 0:R], start=True, stop=True)
                            return ins
                        P.add("pe", mcb, reads=[BT, CT], writes=[ps])
                        P.add("dve", lambda e, ps=ps, g0=g0, R=R, gpb=gpb, tri2=tri2: e.tensor_tensor(
                            out=CBm.t[0:R, g0 * R:(g0 + gpb) * R].rearrange("p (g l) -> p g l", l=R),
                            in0=ps.t[0:R, 0:gpb * R].rearrange("p (g l) -> p g l", l=R),
                            in1=tri2.unsqueeze(1).broadcast_to([R, gpb, R]), op=ALU.mult), reads=[ps, C], writes=[CBm])
                    avb_ap = xwm.t[:, :].bitcast(F32)

                    def av_of(hq):
                        return (AV2.t, AV2) if hq % 2 == 0 else (avb_ap, xwm)

                    def emit_av(hq, R=R, tri2=tri2):
                        ap, buf = av_of(hq)
                        P.add("dve", lambda e, hq=hq, R=R, tri2=tri2, ap=ap: e.tensor_tensor(
                            out=ap[0:R, 0:16 * R].rearrange("p (r l) -> p r l", l=R),
                            in0=a_t.t[0:R, hq * 16:(hq + 1) * 16].unsqueeze(2).broadcast_to([R, 16, R]),
                            in1=tri2.unsqueeze(1).broadcast_to([R, 16, R]), op=ALU.mult), reads=[a_t, C], writes=[buf])
                    emit_av(0)
                    for hq in range(4):
                        if hq + 1 < 4:
                            emit_av(hq + 1)
                        ap, buf = av_of(hq)
                        nb = 16 * R // 512
                        for bk in range(nb):
                            ps = next_ps()
                            P.add("pe", lambda e, ps=ps, bk=bk, us2=us2, R=R, ap=ap: e.matmul(ps.t[0:R, 0:512], lhsT=us2, rhs=ap[0:R, bk * 512:(bk + 1) * 512],
                                                                                  start=True, stop=True), reads=[buf, C], writes=[ps])
                            o0 = hq * 16 * R + bk * 512
                            P.add("act", lambda e, ps=ps, o0=o0, R=R: e.activation(out=MT2.t[0:R, o0:o0 + 512], in_=ps.t[0:R, 0:512], func=AF.Exp),
                                  reads=[ps], writes=[MT2])
                        P.add("dve", lambda e, hq=hq, R=R: e.tensor_tensor(
                            out=MT2.t[0:R, hq * 16 * R:(hq + 1) * 16 * R].rearrange("p (g r l) -> p g r l", r=8, l=R),
                            in0=MT2.t[0:R, hq * 16 * R:(hq + 1) * 16 * R].rearrange("p (g r l) -> p g r l", r=8, l=R),
                            in1=CBm.t[0:R, hq * 2 * R:(hq * 2 + 2) * R].rearrange("p (g l) -> p g l", l=R).unsqueeze(2).broadcast_to([R, 2, 8, R]),
                            op=ALU.mult), reads=[MT2, CBm], writes=[MT2])
                    for g in range(NG):
                        ps = next_ps()

                        def myd(e, ps=ps, g=g, R=R):
                            ins = None
                            for rr in range(8):
                                r = g * 8 + rr
                                ins = e.matmul(ps.t[0:R, rr * 64:(rr + 1) * 64], lhsT=MT2.t[0:R, r * R:(r + 1) * R], rhs=xdt.t[0:R, r * 64:(r + 1) * 64],
                                               start=True, stop=True)
                            return ins
                        P.add("pe", myd, reads=[MT2, xdt], writes=[ps])
                        P.add("dve", lambda e, ps=ps, g=g, R=R: e.tensor_tensor(out=y_sb.t[0:R, g * 512:(g + 1) * 512], in0=ps.t[0:R, 0:512],
                                                                               in1=y_sb.t[0:R, g * 512:(g + 1) * 512], op=ALU.add), reads=[ps, y_sb], writes=[y_sb])
                for sg in range(nseg):
                    mcol = C.t[0:R, mo + sg:mo + sg + 1]
                    if sample:
                        state_in(s_ssm_in[sg])
                    AM = am if nseg > 1 else a_t
                    XWM = xwm if nseg > 1 else xw
                    if nseg > 1:
                        P.add("dve", lambda e, mcol=mcol, R=R: e.tensor_scalar_mul(out=am.t[0:R, :], in0=a_t.t[0:R, :], scalar1=mcol), reads=[a_t, C], writes=[am])
                    ps = next_ps()
                    P.add("pe", lambda e, ps=ps, R=R, AM=AM: e.matmul(ps.t[:, 0:HS], lhsT=onesf.t[0:R, :], rhs=AM.t[0:R, :], start=True, stop=True), reads=[AM, onesf], writes=[ps])
                    P.add("act", lambda e, ps=ps: e.activation(out=decb.t[:, :], in_=ps.t[:, 0:HS], func=AF.Exp), reads=[ps], writes=[decb])
                    if nseg > 1:
                        P.add("dve", lambda e, mcol=mcol, R=R: e.tensor_scalar_mul(out=xwm.t[0:R, :], in0=xw.t[0:R, :], scalar1=mcol), reads=[xw, C], writes=[xwm])
                    if not prefix:
                        for g in range(NG):
                            ps = next_ps()
                            P.add("pe", lambda e, ps=ps, g=g, R=R: e.matmul(ps.t[0:R, 0:512], lhsT=CT.t[:, g, 0:R], rhs=S16.t[:, g * 512:(g + 1) * 512], start=True, stop=True),
                                  reads=[CT, S16], writes=[ps])
                            P.add("dve", lambda e, ps=ps, g=g, R=R, mcol=mcol: e.scalar_tensor_tensor(
                                out=tmp.t[0:R, :].rearrange("p (r c) -> p r c", c=64), in0=ps.t[0:R, 0:512].rearrange("p (r c) -> p r c", c=64), scalar=mcol,
                                in1=et.t[0:R, g * 8:(g + 1) * 8].unsqueeze(2).broadcast_to([R, 8, 64]), op0=ALU.mult, op1=ALU.mult), reads=[ps, et, C], writes=[tmp])
                            P.add("dve", lambda e, g=g, R=R: e.tensor_tensor(out=y_sb.t[0:R, g * 512:(g + 1) * 512], in0=tmp.t[0:R, :],
                                                                            in1=y_sb.t[0:R, g * 512:(g + 1) * 512], op=ALU.add), reads=[tmp, y_sb], writes=[y_sb])
                    for g in range(NG):
                        ps = next_ps()
                        P.add("pe", lambda e, ps=ps, g=g, R=R, XWM=XWM: e.matmul(ps.t[:, 0:512], lhsT=b_t.t[0:R, g * 128:(g + 1) * 128], rhs=XWM.t[0:R, g * 512:(g + 1) * 512],
                                                                       start=True, stop=True), reads=[b_t, XWM], writes=[ps])
                        P.add("dve", lambda e, g=g: e.tensor_tensor(
                            out=S32.t[:, g * 512:(g + 1) * 512].rearrange("p (r c) -> p r c", c=64),
                            in0=S32.t[:, g * 512:(g + 1) * 512].rearrange("p (r c) -> p r c", c=64),
                            in1=decb.t[:, g * 8:(g + 1) * 8].unsqueeze(2).broadcast_to([128, 8, 64]), op=ALU.mult), reads=[S32, decb], writes=[S32])
                        P.add("dve", lambda e, ps=ps, g=g: e.tensor_tensor(out=S32.t[:, g * 512:(g + 1) * 512], in0=ps.t[:, 0:512],
                                                                          in1=S32.t[:, g * 512:(g + 1) * 512], op=ALU.add), reads=[ps, S32], writes=[S32])
                    if not prefix:
                        P.add("act", lambda e: e.activation(out=S16.t[:, :], in_=S32.t[:, :], func=AF.Copy), reads=[S32], writes=[S16])
                    if sample:
                        state_out(o_sssm[sg])
                if not prefix:
                    for g in range(NG):
                        Z = zt[g % 2]
                        P.add("sp", lambda e, Z=Z, g=g, t0=t0, R=R: e.dma_start(out=Z.t[0:R, :], in_=z_d.t[t0:t0 + R, g * 512:(g + 1) * 512]), reads=[z_d], writes=[Z], dma=True)
                        P.add("act", lambda e, Z=Z, R=R: e.activation(out=Z.t[0:R, :], in_=Z.t[0:R, :], func=AF.Silu), reads=[Z], writes=[Z])
                        P.add("dve", lambda e, Z=Z, g=g, R=R: e.tensor_tensor(out=y_sb.t[0:R, g * 512:(g + 1) * 512], in0=y_sb.t[0:R, g * 512:(g + 1) * 512],
                                                                             in1=Z.t[0:R, :], op=ALU.mult), reads=[Z, y_sb], writes=[y_sb])
                        P.add("act", lambda e, Z=Z, g=g, R=R: e.activation(out=Z.t[0:R, :], in_=y_sb.t[0:R, g * 512:(g + 1) * 512], func=AF.Square), reads=[y_sb], writes=[Z])
                        P.add("dve", lambda e, Z=Z, g=g, R=R: e.reduce_sum(out=ssg.t[0:R, g:g + 1], in_=Z.t[0:R, :], axis=AX.X), reads=[Z], writes=[ssg])
                    P.add("act", lambda e, R=R: e.activation(out=ssg.t[0:R, :], in_=ssg.t[0:R, :], func=AF.Sqrt, scale=1.0 / 512, bias=epsc.t[0:R, 0:1]), reads=[ssg, epsc], writes=[ssg])
                    P.add("dve", lambda e, R=R: e.reciprocal(out=ssg.t[0:R, :], in_=ssg.t[0:R, :]), reads=[ssg], writes=[ssg], hz=True)
                    P.add("dve", lambda e, R=R: e.tensor_tensor(out=y_sb.t[0:R, :].rearrange("p (g c) -> p g c", c=512), in0=y_sb.t[0:R, :].rearrange("p (g c) -> p g c", c=512),
                                                               in1=ssg.t[0:R, :].unsqueeze(2).broadcast_to([R, NG, 512]), op=ALU.mult), reads=[ssg, y_sb], writes=[y_sb])
                    P.add("dve", lambda e, R=R: e.tensor_tensor(out=y_sb.t[0:R, :], in0=y_sb.t[0:R, :], in1=nssm_b.t[0:R, :], op=ALU.mult), reads=[nssm_b, y_sb], writes=[y_sb])
                    tr_to(lambda c, ps, ap, R=R: copy(ev_eng(), ynst.t[:, c, 0:R], ap, [ps], [ynst]), y_sb, [(c * 128, 128) for c in range(32)], R, [y_sb], [ynst])
                    P.add("sp", lambda e, t0=t0, R=R: e.dma_start(out=v3(ynT_d, NTMAX)[:, :, t0:t0 + R], in_=ynst.t[:, :, 0:R]), reads=[ynst], writes=[ynT_d], dma=True)
            if last and not any(t[0] >= QT for t in tiles):
                state_out(o_pssm)
            if not last:
                P.add("sp", lambda e: e.dma_start(out=S_d.t[:, :], in_=S32.t[:, :]), reads=[S32], writes=[S_d], dma=True)
        P.fence()

    def phase3(q, last):
        nt = QT + (NS * TS if last else 0)
        blks = blocks_of(nt)
        if True:
            with contextlib.ExitStack() as ph:
                mst = [sbp(ph, "mst", [128, NTMAX], BF16) for i in range(2)]
                oT = sbp(ph, "oT3", [128, DC, NTMAX], BF16)
                ynT = sbp(ph, "ynT3", [128, 32, NTMAX], BF16)
                gA = [sbp(ph, "gA", [128, NTMAX]) for i in range(2)]
                gB = [sbp(ph, "gB", [128, NTMAX]) for i in range(2)]
                P.add("sp", lambda e: e.dma_start(out=oT.t[:, :, 0:nt], in_=v3(oT_d, NTMAX)[:, :, 0:nt]), reads=[oT_d], writes=[oT], dma=True)
                P.add("sp", lambda e: e.dma_start(out=ynT.t[:, :, 0:nt], in_=v3(ynT_d, NTMAX)[:, :, 0:nt]), reads=[ynT_d], writes=[ynT], dma=True)
                for d in range(DC):
                    wab, wav = load_w(W["w_oa"], 0, D, d * 128, 128)
                    wsb1, wsv1 = load_w(W["w_os"], 0, 2048, d * 128, 128)
                    wsb2, wsv2 = load_w(W["w_os"], 2048, 2048, d * 128, 128)
                    GA, GB = gA[d % 2], gB[d % 2]
                    P.add("sp", lambda e, GA=GA, d=d: e.dma_start(out=GA.t[:, 0:nt], in_=v3(gT_d, NTMAX)[:, d, 0:nt]), reads=[gT_d], writes=[GA], dma=True)
                    P.add("sp", lambda e, GB=GB, d=d: e.dma_start(out=GB.t[:, 0:nt], in_=v3(gT_d, NTMAX)[:, 16 + d, 0:nt]), reads=[gT_d], writes=[GB], dma=True)
                    P.add("act", lambda e, GA=GA, d=d: e.activation(out=GA.t[:, 0:nt], in_=GA.t[:, 0:nt], func=AF.Sigmoid, bias=bgc.t[:, d:d + 1]), reads=[GA, bgc], writes=[GA])
                    P.add("act", lambda e, GB=GB, d=d: e.activation(out=GB.t[:, 0:nt], in_=GB.t[:, 0:nt], func=AF.Sigmoid, bias=bgc.t[:, 16 + d:17 + d]), reads=[GB, bgc], writes=[GB])
                    for (b0, bw) in blks:
                        pa, pb_ = next_ps(), next_ps()

                        def mm(e, pa=pa, pb_=pb_, b0=b0, bw=bw, wav=wav, wsv1=wsv1, wsv2=wsv2):
                            ins = None
                            for k in range(DC):
                                ins = e.matmul(pa.t[:, 0:bw], lhsT=wav[:, k, :], rhs=oT.t[:, k, b0:b0 + bw], start=(k == 0), stop=(k == DC - 1))
                            for k in range(32):
                                wv = wsv1 if k < 16 else wsv2
                                ins = e.matmul(pb_.t[:, 0:bw], lhsT=wv[:, k % 16, :], rhs=ynT.t[:, k, b0:b0 + bw], start=(k == 0), stop=(k == 31))
                            return ins
                        P.add("pe", mm, reads=[wab, wsb1, wsb2, oT, ynT], writes=[pa, pb_])
                        P.add("dve", lambda e, pa=pa, GA=GA, b0=b0, bw=bw: e.tensor_tensor(out=GA.t[:, b0:b0 + bw], in0=pa.t[:, 0:bw], in1=GA.t[:, b0:b0 + bw], op=ALU.mult),
                              reads=[pa, GA], writes=[GA])
                        P.add("dve", lambda e, pb_=pb_, GB=GB, b0=b0, bw=bw: e.tensor_tensor(out=GB.t[:, b0:b0 + bw], in0=pb_.t[:, 0:bw], in1=GB.t[:, b0:b0 + bw], op=ALU.mult),
                              reads=[pb_, GB], writes=[GB])
                    MS = mst[d % 2]
                    P.add("dve", lambda e, GA=GA, GB=GB, MS=MS: e.tensor_tensor(out=MS.t[:, 0:nt], in0=GA.t[:, 0:nt], in1=GB.t[:, 0:nt], op=ALU.add),
                          reads=[GA, GB], writes=[MS])
                    P.add("sp", lambda e, MS=MS, d=d: e.dma_start(out=v3(mT_d, NTMAX)[:, d, 0:nt], in_=MS.t[:, 0:nt]), reads=[MS], writes=[mT_d], dma=True)
            P.fence()
        with contextlib.ExitStack() as phX:
            xT = sbp(phX, "xT3", [128, DC, NTMAX])
            hT = sbp(phX, "hT3", [128, DC, NTMAX], BF16)
            P.add("sp", lambda e: e.dma_start(out=xT.t[:, :, 0:nt], in_=v3(x1T_d, NTMAX)[:, :, 0:nt]), reads=[x1T_d], writes=[xT], dma=True)
            P.add("sp", lambda e: e.dma_start(out=hT.t[:, :, 0:nt], in_=v3(mT_d, NTMAX)[:, :, 0:nt]), reads=[mT_d], writes=[hT], dma=True)
            if True:
                for d in range(DC):
                    wob, wov = load_w(W["w_out"], 0, D, d * 128, 128)
                    for (b0, bw) in blks:
                        ps = next_ps()

                        def mo_(e, ps=ps, b0=b0, bw=bw, wov=wov):
                            ins = None
                            for k in range(DC):
                                ins = e.matmul(ps.t[:, 0:bw], lhsT=wov[:, k, :], rhs=hT.t[:, k, b0:b0 + bw], start=(k == 0), stop=(k == DC - 1))
                            return ins
                        P.add("pe", mo_, reads=[wob, hT], writes=[ps])
                        P.add("dve", lambda e, ps=ps, d=d, b0=b0, bw=bw: e.tensor_tensor(out=xT.t[:, d, b0:b0 + bw], in0=ps.t[:, 0:bw], in1=xT.t[:, d, b0:b0 + bw], op=ALU.add),
                              reads=[ps, xT], writes=[xT])
            P.fence()
            with contextlib.ExitStack() as ph:
                sqb = [sbp(ph, "sq3", [128, NTMAX], BF16) for i in range(2)]
                rstd = sbp(ph, "rstd3", [128, NTMAX])
                rmsnorm_fm(xT, sqb, rstd, 2, nt, hT)
                with contextlib.ExitStack() as phF:
                    ffn(phF, xT, hT, W["wg2"], W["wu2"], W["wd2"], nt)
                P.fence()
                rmsnorm_fm(xT, sqb, rstd, 3, nt, xT)
                for tt in range(QT // 128):
                    store_tok(y_p[tt * 128:(tt + 1) * 128, :], xT, DC, tt * 128, 128)
                if last:
                    store_tok(y_s[:, :], xT, DC, QT, NS * TS)
        P.fence()

    quarters = stages.get("quarters", list(range(NQ)))
    for qi, q in enumerate(quarters):
        last = (qi == len(quarters) - 1)
        phase1(q, last)
        if stages.get("upto") == "ffn1":
            continue
        if last:
            phase2_attn(q, last)
        phase2_ssd(q, qi == 0, last)
        if last:
            phase3(q, last)

    if stages.get("dump"):
        for nm, src in (("dbg_oT", oT_d), ("dbg_ynT", ynT_d), ("dbg_mT", mT_d), ("dbg_x1T", x1T_d), ("dbg_cqT", cqT_d),
                        ("dbg_z", z_d), ("dbg_dt", dt_d), ("dbg_xs", xs_d), ("dbg_gT", gT_d)):
            dst = nc.dram_tensor(nm, list(src.t.shape), src.t.dtype, kind="ExternalOutput").ap()
            P.add("sp", lambda e, dst=dst, src=src: e.dma_start(out=dst, in_=src.t), reads=[src], dma=True)
    P.emit(st)
    st.close()
    return nc


def rope_tables(pos_local):
    half = ROPE // 2
    inv = np.power(np.float32(10000.0), -np.arange(half, dtype=np.float32) / np.float32(half)).astype(np.float32)
    pos = np.concatenate([pos_local, PAST + np.arange(TS)]).astype(np.float32)
    ang = pos[:, None] * inv[None, :]
    cos, sin = np.cos(ang).astype(np.float32), np.sin(ang).astype(np.float32)
    cos2 = np.concatenate([cos, cos], axis=1)
    sin2 = np.concatenate([-sin, sin], axis=1)
    tok = np.ascontiguousarray(np.concatenate([cos2, sin2], axis=1))
    fm = np.ascontiguousarray(np.stack([cos2.T, sin2.T]))
    return fm, tok


def ssd_consts(L, R, nseg):
    c = np.zeros((128, 128 + 128 + nseg), np.float32)
    idx = np.arange(R)
    same = (idx[:, None] // L) == (idx[None, :] // L)
    c[:R, 0:R] = same & (idx[:, None] <= idx[None, :])
    c[:R, 128:128 + R] = same & (idx[None, :] < idx[:, None])
    for s_ in range(nseg):
        c[s_ * L:(s_ + 1) * L, 256 + s_] = 1.0
    return c


_STAGES = {}
_NCORES = [8]
_LAST = [None]
_LAST_RES = [None]
_RUNKW = {}


def kernel(**inp):
    ncores = _NCORES[0]
    nc = build_program(_STAGES)
    f = lambda a: np.ascontiguousarray(np.asarray(a, dtype=np.float32))
    shared = {
        "n_f1": f(inp["norm_ffn1"][0]), "wg1": f(inp["w_ffn1_gate"][0]), "wu1": f(inp["w_ffn1_up"][0]),
        "wd1": f(inp["w_ffn1_down"][0]), "n_mix": f(inp["norm_mix"][0]), "w_in": f(inp["w_in"][0]),
        "b_gate": f(inp["b_gate"][0]).reshape(-1), "n_qa": f(inp["norm_q_a"][0]), "w_qb": f(inp["w_q_b"][0]),
        "n_kva": f(inp["norm_kv_a"][0]), "w_kvb": f(inp["w_kv_b"][0]), "conv_w": f(inp["conv_w"][0]),
        "conv_b": f(inp["conv_b"][0]), "dt_bias": f(inp["dt_bias"][0]), "a_log": f(inp["a_log"][0]),
        "d_skip": f(inp["d_skip"][0]), "n_ssm": f(inp["norm_ssm"][0]), "w_oa": f(inp["w_o_attn"][0]),
        "w_os": f(inp["w_o_ssm"][0]), "w_out": f(inp["w_out"][0]), "n_f2": f(inp["norm_ffn2"][0]),
        "wg2": f(inp["w_ffn2_gate"][0]), "wu2": f(inp["w_ffn2_up"][0]), "wd2": f(inp["w_ffn2_down"][0]),
        "n_fin": f(inp["norm_final"]), "ident_in": np.eye(128, dtype=np.float32),
        "tri_in": np.zeros((128, 64), np.float32), "ustr_in": np.zeros((128, 64), np.float32),
        "cst_p": ssd_consts(128, 128, 1), "cst_s": ssd_consts(16, 64, 4),
    }
    in_maps = []
    for c in range(ncores):
        m = dict(shared)
        b, kq = c // NQ, c % NQ
        npre = (NQ - 1 - kq) * QT
        xl = np.zeros((SEQ, D), np.float32)
        xl[npre:] = f(inp["x_prompt"][b, 0:(kq + 1) * QT])
        pos_local = np.maximum(np.arange(SEQ) - npre, 0)
        fm, tok = rope_tables(pos_local)
        valid = (np.arange(SEQ) >= npre).astype(np.float32).reshape(SEQ, 1)
        kneg = np.where(np.arange(SEQ // 128)[None, :] * 128 >= npre, 0.0, NEG).astype(np.float32)
        m["xp"] = xl
        m["ropeT"] = fm
        m["rope_tok"] = tok
        m["valid_in"] = valid
        m["kneg_in"] = np.ascontiguousarray(np.broadcast_to(kneg, (128, SEQ // 128)))
        sl = slice(NS * c, NS * (c + 1))
        m["xs"] = f(inp["x_sample"][sl]).reshape(NS * TS, D)
        m["c_lat"] = f(inp["cache_kv_latent"][0, sl])
        m["c_kr"] = f(inp["cache_k_rope"][0, sl])
        m["s_ssm_in"] = f(inp["state_ssm"][0, sl]).reshape(NS, DI, NST)
        m["s_conv_in"] = f(inp["state_conv"][0, sl]).reshape(NS * 3, CONV)
        in_maps.append(m)
    res = run_bass_kernel_spmd(nc, in_maps, core_ids=list(range(ncores)), **_RUNKW)
    _LAST_RES[0] = res
    R = list(res.results)
    _LAST[0] = R
    while len(R) < 8:
        R.append(R[len(R) % len(res.results)])
    cat = lambda k: np.concatenate([R[c][k] for c in range(8)], axis=0)
    seqcat = lambda k: np.stack([np.concatenate([R[b * NQ + j][k] for j in range(NQ)], axis=0) for b in range(2)])
    y_prompt = seqcat("y_p")
    y_sample = cat("y_s").reshape(32, TS, D)
    p_lat = seqcat("o_plat")[None]
    p_kr = seqcat("o_pkr")[None]
    p_ssm = np.stack([R[NQ - 1]["o_pssm"], R[2 * NQ - 1]["o_pssm"]]).reshape(1, 2, HS, HS, NST)
    p_conv = np.stack([R[NQ - 1]["o_pconv"], R[2 * NQ - 1]["o_pconv"]])[None]
    s_lat = cat("o_slat").reshape(1, 32, TS, KVL)
    s_kr = cat("o_skr").reshape(1, 32, TS, ROPE)
    s_ssm = cat("o_sssm").reshape(1, 32, HS, HS, NST)
    s_conv = cat("o_sconv").reshape(1, 32, 3, CONV)
    return (y_prompt, y_sample, p_lat, p_kr, p_ssm, p_conv, s_lat, s_kr, s_ssm, s_conv)
```

```python
import contextlib
import numpy as np
import concourse.bass as bass
import concourse.mybir as mybir
from concourse.bass_utils import run_bass_kernel_spmd

F32 = mybir.dt.float32
BF16 = mybir.dt.bfloat16
ALU = mybir.AluOpType
AF = mybir.ActivationFunctionType
AX = mybir.AxisListType

D = 2048
DC = 16
SEQ = 4096
NQ = 4
QT = 1024
NS = 4
TS = 16
PAST = 2048
DFF = 5632
QL = 768
KVL = 512
ROPE = 64
NH = 16
DI = 4096
CONV = 6144
HS = 64
NG = 8
NST = 128
DIN = 15744
O_QA, O_KV, O_Z, O_XBC, O_DT, O_G = 0, 768, 1344, 5440, 11584, 11648
EPS = 1e-6
SCALE = 192 ** -0.5
NEG = -30000.0


class Buf:
    __slots__ = ("t", "name", "lw", "rd", "sc")

    def __init__(self, t, name="", sc=False):
        self.t = t
        self.name = name
        self.lw = None
        self.rd = []
        self.sc = sc

    def __getitem__(self, k):
        return self.t[k]


class Op:
    __slots__ = ("eng", "fn", "deps", "dma", "sig", "sigval", "sem", "waits", "hz")

    def __init__(self, eng, fn, dma):
        self.hz = False
        self.eng = eng
        self.fn = fn
        self.dma = dma
        self.deps = []
        self.sig = False
        self.sigval = 0
        self.sem = None
        self.waits = []


ENGS = ("pe", "act", "dve", "pool", "sp")
KDMA = 8
SAFE_SYNC = True


class Prog:
    def __init__(self, nc):
        self.nc = nc
        self.ops = []
        self.by_eng = {e: [] for e in ENGS}
        self.fence_ops = []
        self.fenced = set()

    def fence(self):
        f = []
        for e in ENGS:
            comp = [o for o in self.by_eng[e] if not o.dma]
            if comp:
                f.append(comp[-1])
            f += [o for o in self.by_eng[e] if o.dma][-KDMA:]
        self.fence_ops = f
        self.fenced = set()

    def add(self, eng, fn, reads=(), writes=(), dma=False, hz=False):
        op = Op(eng, fn, dma)
        op.hz = hz
        deps = {}
        if self.fence_ops and eng not in self.fenced:
            self.fenced.add(eng)
            for d in self.fence_ops:
                deps[id(d)] = d
        strong = set()
        for b in reads:
            if b.lw is not None:
                deps[id(b.lw)] = b.lw
                if b.sc:
                    strong.add(id(b.lw))
        for b in writes:
            if b.lw is not None:
                deps[id(b.lw)] = b.lw
            for r in b.rd:
                deps[id(r)] = r
        for d in deps.values():
            if (not d.dma) and d.eng == eng and not dma and (eng == "pe" or not (SAFE_SYNC or id(d) in strong or d.hz)):
                continue
            op.deps.append(d)
        for b in reads:
            b.rd.append(op)
        for b in writes:
            b.lw = op
            b.rd = []
        self.ops.append(op)
        self.by_eng[eng].append(op)
        return op

    def emit(self, stack):
        nc = self.nc
        for op in self.ops:
            for d in op.deps:
                d.sig = True
            if op.dma:
                op.sig = True
        csem = {e: stack.enter_context(nc.semaphore("cs_" + e)) for e in ("pe", "act", "dve", "pool")}
        KQ = {"sp": KDMA, "pool": 4}
        dsem = {e: [stack.enter_context(nc.semaphore("ds_%s%d" % (e, i))) for i in range(KQ[e])]
                for e in ("sp", "pool")}
        for e in ENGS:
            cnt = 0
            dcnt = 0
            for op in self.by_eng[e]:
                if op.dma:
                    kq = KQ[e]
                    op.sem = dsem[e][dcnt % kq]
                    op.sigval = 16 * (dcnt // kq + 1)
                    if dcnt >= kq:
                        op.waits.append((op.sem, 16 * (dcnt // kq)))
                    dcnt += 1
                elif op.sig:
                    cnt += 1
                    op.sem = csem[e]
                    op.sigval = cnt
        for e in ENGS:
            seen = {}
            for op in self.by_eng[e]:
                ws = list(op.waits)
                for d in op.deps:
                    ws.append((d.sem, d.sigval))
                best = {}
                for s, v in ws:
                    k = id(s)
                    if seen.get(k, 0) >= v:
                        continue
                    if k not in best or best[k][1] < v:
                        best[k] = (s, v)
                op.waits = list(best.values())
                for s, v in op.waits:
                    seen[id(s)] = v
        final = []
        for e in ("sp", "pool"):
            last = {}
            for op in self.by_eng[e]:
                if op.dma:
                    last[id(op.sem)] = (op.sem, op.sigval)
            final += list(last.values())
        by_eng = self.by_eng

        def run(eng_name):
            def body(eng):
                for op in by_eng[eng_name]:
                    for s, v in op.waits:
                        eng.wait_ge(s, v)
                    ins = op.fn(eng)
                    if op.sig:
                        ins.then_inc(op.sem, 16 if op.dma else 1)
                if eng_name == "sp":
                    for s, v in final:
                        eng.wait_ge(s, v)
            return body

        block = stack.enter_context(nc.Block())
        block.tensor(run("pe"))
        block.scalar(run("act"))
        block.vector(run("dve"))
        block.gpsimd(run("pool"))
        block.sync(run("sp"))


def blocks_of(n, w=512):
    out = []
    c = 0
    while c < n:
        out.append((c, min(w, n - c)))
        c += w
    return out


def build_program(stages):
    nc = bass.Bass("TRN2", target_bir_lowering=False)
    st = contextlib.ExitStack()
    P = Prog(nc)

    def din(name, shape, dt=F32):
        return nc.dram_tensor(name, list(shape), dt, kind="ExternalInput").ap()

    def dout(name, shape, dt=F32):
        return nc.dram_tensor(name, list(shape), dt, kind="ExternalOutput").ap()

    def dscr(name, shape, dt=F32):
        return Buf(nc.dram_tensor(name, list(shape), dt).ap(), name)

    def sb(name, shape, dt=F32):
        return Buf(st.enter_context(nc.sbuf_tensor(name, list(shape), dt)), name)

    xp = din("xp", [SEQ, D])
    xs = din("xs", [NS * TS, D])
    c_lat = din("c_lat", [NS, PAST, KVL])
    c_kr = din("c_kr", [NS, PAST, ROPE])
    s_ssm_in = din("s_ssm_in", [NS, DI, NST])
    s_conv_in = din("s_conv_in", [NS * 3, CONV])
    W = {}
    for nm, shp in (("n_f1", [D]), ("wg1", [D, DFF]), ("wu1", [D, DFF]), ("wd1", [DFF, D]),
                    ("n_mix", [D]), ("w_in", [D, DIN]), ("b_gate", [2 * D]), ("n_qa", [QL]),
                    ("w_qb", [QL, NH * 192]), ("n_kva", [KVL]), ("w_kvb", [KVL, NH * 256]),
                    ("conv_w", [4, CONV]), ("conv_b", [CONV]), ("dt_bias", [HS]), ("a_log", [HS]),
                    ("d_skip", [HS]), ("n_ssm", [DI]), ("w_oa", [D, D]), ("w_os", [DI, D]),
                    ("w_out", [D, D]), ("n_f2", [D]), ("wg2", [D, DFF]), ("wu2", [D, DFF]),
                    ("wd2", [DFF, D]), ("n_fin", [D])):
        W[nm] = din(nm, shp)
    ropeT = din("ropeT", [2, ROPE, SEQ + TS])
    rope_tok = din("rope_tok", [SEQ + TS, 2 * ROPE])
    ident_in = din("ident_in", [128, 128])

    valid_in = din("valid_in", [SEQ, 1])
    kneg_in = din("kneg_in", [128, SEQ // 128])
    y_p = dout("y_p", [QT, D])
    y_s = dout("y_s", [NS * TS, D])
    o_plat = dout("o_plat", [QT, KVL])
    o_pkr = dout("o_pkr", [QT, ROPE])
    o_pssm = dout("o_pssm", [DI, NST])
    o_pconv = dout("o_pconv", [3, CONV])
    o_slat = dout("o_slat", [NS * TS, KVL])
    o_skr = dout("o_skr", [NS * TS, ROPE])
    o_sssm = dout("o_sssm", [NS, DI, NST])
    o_sconv = dout("o_sconv", [NS * 3, CONV])

    uid = [0]

    SC_NAMES = ("ss", "ssg", "te", "et", "decb", "dts", "a_t", "am", "vcol", "dt_t", "rstd")

    def sbp(ph, name, shape, dt=F32):
        uid[0] += 1
        nm = "%s_%d" % (name, uid[0])
        return Buf(ph.enter_context(nc.sbuf_tensor(nm, list(shape), dt)), nm, sc=(name in SC_NAMES))

    tri_in = din("tri_in", [128, 64])
    ustr_in = din("ustr_in", [128, 64])
    ident = sb("ident", [128, 128])
    onesb = sb("onesb", [128, 128], BF16)
    onesf = sb("onesf", [128, 128])
    epsc = sb("epsc", [128, 1])
    gcols = sb("gcols", [128, 4, DC])
    triV = sb("triV", [128, 64])
    ustr = sb("ustr", [128, 64])
    P.add("sp", lambda e: e.dma_start(out=ident[:], in_=ident_in), writes=[ident], dma=True)
    P.add("sp", lambda e: e.dma_start(out=triV[:], in_=tri_in), writes=[triV], dma=True)
    P.add("sp", lambda e: e.dma_start(out=ustr[:], in_=ustr_in), writes=[ustr], dma=True)
    P.add("dve", lambda e: e.memset(onesb[:], 1.0), writes=[onesb])
    P.add("dve", lambda e: e.memset(onesf[:], 1.0), writes=[onesf])
    identb = sb("identb", [128, 128], BF16)
    P.add("dve", lambda e: e.tensor_copy(out=identb[:], in_=ident[:]), reads=[ident], writes=[identb])
    P.add("dve", lambda e: e.memset(epsc[:], EPS), writes=[epsc])
    for i, nm in enumerate(("n_f1", "n_mix", "n_f2", "n_fin")):
        P.add("sp", lambda e, i=i, nm=nm: e.dma_start(
            out=gcols[:, i, :], in_=W[nm].rearrange("(c p) -> p c", p=128),
            allow_slow_non_contiguous=True), writes=[gcols], dma=True)

    def bc_load(dst, src1d, n):
        P.add("sp", lambda e: e.dma_start(out=dst.t[:, 0:n], in_=src1d.rearrange("(o n) -> o n", o=1).broadcast_to([128, n])),
              writes=[dst], dma=True)

    nqa_b = sb("nqa_b", [128, QL])
    nkv_b = sb("nkv_b", [128, KVL])
    dtb_b = sb("dtb_b", [128, HS])
    A_b = sb("A_b", [128, HS])
    dsk_b = sb("dsk_b", [128, HS])
    bc_load(nqa_b, W["n_qa"], QL)
    bc_load(nkv_b, W["n_kva"], KVL)
    bc_load(dtb_b, W["dt_bias"], HS)
    bc_load(A_b, W["a_log"], HS)
    bc_load(dsk_b, W["d_skip"], HS)
    P.add("act", lambda e: e.activation(out=A_b[:], in_=A_b[:], func=AF.Exp), reads=[A_b], writes=[A_b])
    P.add("dve", lambda e: e.tensor_scalar_mul(out=A_b[:], in0=A_b[:], scalar1=-1.0), reads=[A_b], writes=[A_b])
    cwc = sb("cwc", [128, 48, 5])
    for k in range(4):
        P.add("sp", lambda e, k=k: e.dma_start(out=cwc[:, :, k], in_=W["conv_w"][k, :].rearrange("(c p) -> p c", p=128),
                                               allow_slow_non_contiguous=True), writes=[cwc], dma=True)
    P.add("sp", lambda e: e.dma_start(out=cwc[:, :, 4], in_=W["conv_b"].rearrange("(c p) -> p c", p=128),
                                      allow_slow_non_contiguous=True), writes=[cwc], dma=True)
    bgc = sb("bgc", [128, 32])
    P.add("sp", lambda e: e.dma_start(out=bgc[:], in_=W["b_gate"].rearrange("(c p) -> p c", p=128),
                                      allow_slow_non_contiguous=True), writes=[bgc], dma=True)
    negm = sb("negm", [128, 1])
    P.add("dve", lambda e: e.memset(negm[:], 0.0), writes=[negm])
    P.add("dve", lambda e: e.memset(negm[64:128, :], NEG), writes=[negm])
    halo = sb("halo", [128, 48, 3])
    P.add("dve", lambda e: e.memset(halo[:], 0.0), writes=[halo])
    convout = sb("convout", [128, 48, 16])

    psb = [Buf(st.enter_context(nc.psum_tensor("ps%d" % i, [128, 512], F32)), "ps%d" % i) for i in range(8)]
    ps_ctr = [0]
    ps_mod = [6]

    def next_ps():
        b = psb[ps_ctr[0] % ps_mod[0]]
        ps_ctr[0] += 1
        return b

    ev_ctr = [0]

    def ev_eng():
        ev_ctr[0] += 1
        return "act" if ev_ctr[0] % 2 else "dve"

    def copy(en, dst, src, reads, writes):
        if en == "act":
            P.add("act", lambda e: e.activation(out=dst, in_=src, func=AF.Copy), reads=reads, writes=writes)
        else:
            P.add(en, lambda e: e.tensor_copy(out=dst, in_=src), reads=reads, writes=writes)

    NTMAX = QT + NS * TS
    wbufs = [sb("wb%d" % i, [128, DC * 128], BF16) for i in range(6)]
    wb_ctr = [0]

    def load_w(wap, r0, nr, c0, ncol):
        kc = nr // 128
        wb = wbufs[wb_ctr[0] % len(wbufs)]
        wb_ctr[0] += 1
        view = wb.t[:, 0:kc * ncol].rearrange("p (c n) -> p c n", n=ncol)
        src = wap[r0:r0 + nr, c0:c0 + ncol].rearrange("(c p) n -> p c n", p=128)
        P.add("pool", lambda e: e.dma_start(out=view, in_=src), writes=[wb], dma=True)
        return wb, view

    stage_bufs = [sb("stg%d" % i, [128, D]) for i in range(2)]
    stg_ctr = [0]

    def next_stage():
        b = stage_bufs[stg_ctr[0] % len(stage_bufs)]
        stg_ctr[0] += 1
        return b

    x1T_d = dscr("x1T_d", [128, DC * NTMAX])
    cqT_d = dscr("cqT_d", [128, 6 * NTMAX], BF16)
    latT_d = dscr("latT_d", [128, 5 * SEQ], BF16)
    latS_d = dscr("latS_d", [128, 5 * NS * TS], BF16)
    latStok_d = dscr("latStok_d", [NS * TS, KVL], BF16)
    z_d = dscr("z_d", [NTMAX, DI])
    dt_d = dscr("dt_d", [NTMAX, HS])
    xs_d = dscr("xs_d", [NTMAX, DI], BF16)
    b_d = dscr("b_d", [NTMAX, NG * NST], BF16)
    bcT_d = dscr("bcT_d", [128, 16 * NTMAX], BF16)
    gT_d = dscr("gT_d", [128, 32 * NTMAX])
    oT_d = dscr("oT_d", [128, DC * NTMAX], BF16)
    ynT_d = dscr("ynT_d", [128, 32 * NTMAX], BF16)
    S_d = dscr("S_d", [128, DI])
    mT_d = dscr("mT_d", [128, DC * NTMAX], BF16)

    def v3(buf, n):
        return buf.t.rearrange("p (c n) -> p c n", n=n)

    def load_xT(xT, src_ap, ntok, col0):
        xt = next_stage()
        P.add("sp", lambda e: e.dma_start(out=xt.t[0:ntok, :], in_=src_ap), writes=[xt], dma=True)
        for g in range(4):
            ps = next_ps()

            def tr(e, g=g, ps=ps):
                ins = None
                for j in range(4):
                    c = g * 4 + j
                    ins = e.transpose(out=ps.t[:, j * 128:j * 128 + ntok], in_=xt.t[0:ntok, c * 128:(c + 1) * 128],
                                      identity=ident.t[0:ntok, 0:ntok])
                return ins
            P.add("pe", tr, reads=[xt, ident], writes=[ps])
            copy(ev_eng(), xT.t[:, g * 4:(g + 1) * 4, col0:col0 + ntok],
                 ps.t[:, :].rearrange("p (j t) -> p j t", t=128)[:, :, 0:ntok], [ps], [xT])

    def rmsnorm_fm(xT, sqb, rstd, gi, nt, out, ph_hT=True):
        blks = blocks_of(nt)
        pss = [next_ps() for _ in blks]
        for c in range(DC):
            sq = sqb[c % 2]
            P.add("act", lambda e, c=c, sq=sq: e.activation(out=sq.t[:, 0:nt], in_=xT.t[:, c, 0:nt], func=AF.Square),
                  reads=[xT], writes=[sq])
            for (b0, bw), ps in zip(blks, pss):
                P.add("pe", lambda e, c=c, sq=sq, ps=ps, b0=b0, bw=bw: e.matmul(
                    ps.t[:, 0:bw], lhsT=onesb.t[:, :], rhs=sq.t[:, b0:b0 + bw], start=(c == 0), stop=(c == DC - 1)),
                    reads=[sq, onesb], writes=[ps])
        for (b0, bw), ps in zip(blks, pss):
            P.add("act", lambda e, ps=ps, b0=b0, bw=bw: e.activation(
                out=rstd.t[:, b0:b0 + bw], in_=ps.t[:, 0:bw], func=AF.Sqrt, scale=1.0 / D, bias=epsc.t[:, 0:1]),
                reads=[ps, epsc], writes=[rstd])
            P.add("dve", lambda e, b0=b0, bw=bw: e.reciprocal(out=rstd.t[:, b0:b0 + bw], in_=rstd.t[:, b0:b0 + bw]),
                  reads=[rstd], writes=[rstd], hz=True)
        for c in range(DC):
            P.add("dve", lambda e, c=c: e.scalar_tensor_tensor(
                out=out.t[:, c, 0:nt], in0=xT.t[:, c, 0:nt], scalar=gcols.t[:, gi, c:c + 1], in1=rstd.t[:, 0:nt],
                op0=ALU.mult, op1=ALU.mult), reads=[xT, rstd, gcols], writes=[out])

    def ffn(ph, xT, hT, wg, wu, wd, nt):
        blks = blocks_of(nt)
        hb = sbp(ph, "hid", [128, 11, NTMAX], BF16)
        silu_t = [sbp(ph, "silu", [128, 512]) for i in range(2)]
        sctr = 0
        for fq in range(4):
            for fl in range(11):
                f = fq * 11 + fl
                wgb, wgv = load_w(wg, 0, D, f * 128, 128)
                wub, wuv = load_w(wu, 0, D, f * 128, 128)
                for (b0, bw) in blks:
                    pg, pu = next_ps(), next_ps()

                    def mm(e, wv=wgv, ps=pg, b0=b0, bw=bw):
                        ins = None
                        for c in range(DC):
                            ins = e.matmul(ps.t[:, 0:bw], lhsT=wv[:, c, :], rhs=hT.t[:, c, b0:b0 + bw],
                                           start=(c == 0), stop=(c == DC - 1))
                        return ins
                    P.add("pe", mm, reads=[wgb, hT], writes=[pg])
                    P.add("pe", lambda e, wv=wuv, ps=pu, b0=b0, bw=bw, mm=mm: mm(e, wv, ps, b0, bw),
                          reads=[wub, hT], writes=[pu])
                    sl = silu_t[sctr % 2]
                    sctr += 1
                    P.add("act", lambda e, sl=sl, pg=pg, bw=bw: e.activation(out=sl.t[:, 0:bw], in_=pg.t[:, 0:bw],
                                                                           func=AF.Silu), reads=[pg], writes=[sl])
                    P.add("dve", lambda e, sl=sl, pu=pu, fl=fl, b0=b0, bw=bw: e.tensor_tensor(
                        out=hb.t[:, fl, b0:b0 + bw], in0=sl.t[:, 0:bw], in1=pu.t[:, 0:bw], op=ALU.mult),
                        reads=[sl, pu], writes=[hb])
            for d in range(DC):
                wdb, wdv = load_w(wd, fq * 11 * 128, 11 * 128, d * 128, 128)
                for (b0, bw) in blks:
                    ps = next_ps()

                    def mm2(e, wv=wdv, ps=ps, b0=b0, bw=bw):
                        ins = None
                        for fl in range(11):
                            ins = e.matmul(ps.t[:, 0:bw], lhsT=wv[:, fl, :], rhs=hb.t[:, fl, b0:b0 + bw],
                                           start=(fl == 0), stop=(fl == 10))
                        return ins
                    P.add("pe", mm2, reads=[wdb, hb], writes=[ps])
                    P.add("dve", lambda e, ps=ps, d=d, b0=b0, bw=bw: e.scalar_tensor_tensor(
                        out=xT.t[:, d, b0:b0 + bw], in0=ps.t[:, 0:bw], scalar=0.5, in1=xT.t[:, d, b0:b0 + bw],
                        op0=ALU.mult, op1=ALU.add), reads=[ps, xT], writes=[xT])

    def tr_to(dst_fn, src, ncols_list, ntok, reads, writes):
        for g0 in range(0, len(ncols_list), 4):
            ps = next_ps()
            grp = ncols_list[g0:g0 + 4]

            def tr(e, ps=ps, grp=grp):
                ins = None
                for j, (c0, cw) in enumerate(grp):
                    ins = e.transpose(out=ps.t[0:cw, j * 128:j * 128 + ntok], in_=src.t[0:ntok, c0:c0 + cw],
                                      identity=ident.t[0:ntok, 0:ntok])
                return ins
            P.add("pe", tr, reads=[src, ident], writes=[ps])
            for j, (c0, cw) in enumerate(grp):
                dst_fn(g0 + j, ps, ps.t[0:cw, j * 128:j * 128 + ntok])

    def store_tok(dst_ap, srcT, nchunks, col0, ntok):
        stg = next_stage()
        for g in range((nchunks + 3) // 4):
            ps = next_ps()
            nj = min(4, nchunks - g * 4)

            def tr(e, g=g, ps=ps, nj=nj):
                ins = None
                for j in range(nj):
                    ins = e.transpose(out=ps.t[0:ntok, j * 128:(j + 1) * 128],
                                      in_=srcT.t[:, g * 4 + j, col0:col0 + ntok], identity=ident.t[:, :])
                return ins
            P.add("pe", tr, reads=[srcT, ident], writes=[ps])
            copy(ev_eng(), stg.t[0:ntok, g * 512:g * 512 + nj * 128], ps.t[0:ntok, 0:nj * 128], [ps], [stg])
        P.add("sp", lambda e: e.dma_start(out=dst_ap, in_=stg.t[0:ntok, 0:nchunks * 128]), reads=[stg], dma=True)

    def lin_tm(hT, c0, ncols, tts, cb):
        tiles = []
        c = 0
        while c < ncols:
            cw = min(128, ncols - c)
            wb, wv = load_w(W["w_in"], 0, D, c0 + c, cw)
            tiles.append((c, cw, wb, wv))
            c += cw
        def emit_mm(t0, tn):
            banks = []
            for g0 in range(0, len(tiles), 4):
                ps = next_ps()
                grp = tiles[g0:g0 + 4]

                def mm(e, ps=ps, grp=grp, t0=t0, tn=tn):
                    ins = None
                    for (cc, cw, wb, wv) in grp:
                        off = cc - grp[0][0]
                        for k in range(DC):
                            ins = e.matmul(ps.t[0:tn, off:off + cw], lhsT=hT.t[:, k, t0:t0 + tn], rhs=wv[:, k, :],
                                           start=(k == 0), stop=(k == DC - 1))
                    return ins
                P.add("pe", mm, reads=[hT] + [t[2] for t in grp], writes=[ps])
                banks.append((ps, grp[0][0], sum(t[1] for t in grp)))
            return banks
        pend = emit_mm(*tts[0])
        for ti, (t0, tn) in enumerate(tts):
            banks = pend
            if ti + 1 < len(tts) and len(tiles) <= 8:
                pend = emit_mm(*tts[ti + 1])
                cb(ti, t0, tn, banks)
            else:
                cb(ti, t0, tn, banks)
                if ti + 1 < len(tts):
                    pend = emit_mm(*tts[ti + 1])

    def rs_of(ph_bufs, src, n, tn):
        sqt, ss = ph_bufs
        P.add("act", lambda e: e.activation(out=sqt.t[0:tn, 0:n], in_=src.t[0:tn, 0:n], func=AF.Square),
              reads=[src], writes=[sqt])
        P.add("dve", lambda e: e.reduce_sum(out=ss.t[0:tn, 0:1], in_=sqt.t[0:tn, 0:n], axis=AX.X), reads=[sqt], writes=[ss])
        P.add("act", lambda e: e.activation(out=ss.t[0:tn, 0:1], in_=ss.t[0:tn, 0:1], func=AF.Sqrt, scale=1.0 / n,
                                            bias=epsc.t[0:tn, 0:1]), reads=[ss, epsc], writes=[ss])
        P.add("dve", lambda e: e.reciprocal(out=ss.t[0:tn, 0:1], in_=ss.t[0:tn, 0:1]), reads=[ss], writes=[ss], hz=True)

    def phase1(q, last):
        prefix = not last
        nt = QT + (NS * TS if last else 0)
        tts = [(i * 128, 128) for i in range(QT // 128)] + ([(QT, NS * TS)] if last else [])
        blks = blocks_of(nt)
        with contextlib.ExitStack() as phH:
            hT = sbp(phH, "hT", [128, DC, NTMAX], BF16)
            with contextlib.ExitStack() as phA:
                xT = sbp(phA, "xT", [128, DC, NTMAX])
                sqb = [sbp(phA, "sq", [128, NTMAX], BF16) for i in range(2)]
                rstd = sbp(phA, "rstd", [128, NTMAX])
                for tt in range(QT // 128):
                    load_xT(xT, xp[q * QT + tt * 128:q * QT + (tt + 1) * 128, :], 128, tt * 128)
                if last:
                    load_xT(xT, xs[:, :], NS * TS, QT)
                rmsnorm_fm(xT, sqb, rstd, 0, nt, hT)
                with contextlib.ExitStack() as phF:
                    ffn(phF, xT, hT, W["wg1"], W["wu1"], W["wd1"], nt)
                P.fence()
                if stages.get("upto") == "ffn1":
                    for tt in range(QT // 128):
                        store_tok(y_p[tt * 128:(tt + 1) * 128, :], xT, DC, tt * 128, 128)
                    return
                rmsnorm_fm(xT, sqb, rstd, 1, nt, hT)
                if not prefix:
                    P.add("sp", lambda e: e.dma_start(out=v3(x1T_d, NTMAX)[:, :, 0:nt], in_=xT.t[:, :, 0:nt]),
                          reads=[xT], writes=[x1T_d], dma=True)
            P.fence()
            with contextlib.ExitStack() as phB:
                qa_sb = sbp(phB, "qa_sb", [128, QL])
                sqt = sbp(phB, "sqt", [128, QL])
                ss = sbp(phB, "ss", [128, 1])
                cq = sbp(phB, "cq", [128, QL])
                cqs = sbp(phB, "cqs", [128, 6, 128], BF16)
                kv_sb = sbp(phB, "kv_sb", [128, 576])
                lat = sbp(phB, "lat", [128, 576])
                rtk = sbp(phB, "rtk", [128, 128])
                rt1 = sbp(phB, "rt1", [128, 64])
                rt2 = sbp(phB, "rt2", [128, 64])
                lts = sbp(phB, "lts", [128, 5, 128], BF16)

                def cb_qa(ti, t0, tn, banks):
                    for (ps, co, w) in banks:
                        copy(ev_eng(), qa_sb.t[0:tn, co:co + w], ps.t[0:tn, 0:w], [ps], [qa_sb])
                    rs_of((sqt, ss), qa_sb, QL, tn)
                    P.add("dve", lambda e: e.scalar_tensor_tensor(out=cq.t[0:tn, :], in0=qa_sb.t[0:tn, :], scalar=ss.t[0:tn, 0:1],
                                                                  in1=nqa_b.t[0:tn, :], op0=ALU.mult, op1=ALU.mult),
                          reads=[qa_sb, ss, nqa_b], writes=[cq])
                    tr_to(lambda j, ps, ap: copy(ev_eng(), cqs.t[:, j, 0:tn], ap, [ps], [cqs]), cq,
                          [(j * 128, 128) for j in range(6)], tn, [cq], [cqs])
                    P.add("sp", lambda e: e.dma_start(out=v3(cqT_d, NTMAX)[:, :, t0:t0 + tn], in_=cqs.t[:, :, 0:tn]),
                          reads=[cqs], writes=[cqT_d], dma=True)
                if not prefix:
                    lin_tm(hT, O_QA, QL, tts, cb_qa)

                def cb_kv(ti, t0, tn, banks):
                    for (ps, co, w) in banks:
                        copy(ev_eng(), kv_sb.t[0:tn, co:co + w], ps.t[0:tn, 0:w], [ps], [kv_sb])
                    rs_of((sqt, ss), kv_sb, KVL, tn)
                    P.add("dve", lambda e: e.scalar_tensor_tensor(out=lat.t[0:tn, 0:KVL], in0=kv_sb.t[0:tn, 0:KVL],
                                                                  scalar=ss.t[0:tn, 0:1], in1=nkv_b.t[0:tn, :],
                                                                  op0=ALU.mult, op1=ALU.mult),
                          reads=[kv_sb, ss, nkv_b], writes=[lat])
                    if t0 < QT:
                        P.add("sp", lambda e: e.dma_start(out=rtk.t[0:tn, :], in_=rope_tok[q * QT + t0:q * QT + t0 + tn, :]),
                              writes=[rtk], dma=True)
                    else:
                        for j in range(NS):
                            P.add("sp", lambda e, j=j: e.dma_start(out=rtk.t[j * TS:(j + 1) * TS, :], in_=rope_tok[SEQ:SEQ + TS, :]),
                                  writes=[rtk], dma=True)

                    def rp(e):
                        e.tensor_tensor(out=rt1.t[0:tn, :], in0=kv_sb.t[0:tn, 512:576], in1=rtk.t[0:tn, 0:64], op=ALU.mult)
                        e.tensor_tensor(out=rt2.t[0:tn, 0:32], in0=kv_sb.t[0:tn, 544:576], in1=rtk.t[0:tn, 64:96], op=ALU.mult)
                        e.tensor_tensor(out=rt2.t[0:tn, 32:64], in0=kv_sb.t[0:tn, 512:544], in1=rtk.t[0:tn, 96:128], op=ALU.mult)
                        return e.tensor_tensor(out=lat.t[0:tn, 512:576], in0=rt1.t[0:tn, :], in1=rt2.t[0:tn, :], op=ALU.add)
                    P.add("dve", rp, reads=[kv_sb, rtk], writes=[lat, rt1, rt2])
                    if t0 < QT:
                        if not prefix:
                            P.add("sp", lambda e: e.dma_start(out=o_plat[t0:t0 + tn, :], in_=lat.t[0:tn, 0:KVL]), reads=[lat], dma=True)
                            P.add("sp", lambda e: e.dma_start(out=o_pkr[t0:t0 + tn, :], in_=lat.t[0:tn, 512:576]), reads=[lat], dma=True)
                    else:
                        P.add("sp", lambda e: e.dma_start(out=o_slat[:, :], in_=lat.t[0:tn, 0:KVL]), reads=[lat], dma=True)
                        P.add("pool", lambda e: e.dma_start(out=latStok_d.t[:, :], in_=lat.t[0:tn, 0:KVL]), reads=[lat], writes=[latStok_d], dma=True)
                        P.add("sp", lambda e: e.dma_start(out=o_skr[:, :], in_=lat.t[0:tn, 512:576]), reads=[lat], dma=True)
                    pieces = [(j * 128, 128) for j in range(4)] + [(512, 64)]
                    tr_to(lambda j, ps, ap: copy(ev_eng(), lts.t[0:pieces[j][1], j, 0:tn], ap, [ps], [lts]), lat, pieces, tn,
                          [lat], [lts])
                    if t0 < QT:
                        r0 = q * QT + t0
                        P.add("sp", lambda e: e.dma_start(out=v3(latT_d, SEQ)[:, :, r0:r0 + tn], in_=lts.t[:, :, 0:tn]),
                              reads=[lts], writes=[latT_d], dma=True)
                    else:
                        P.add("sp", lambda e: e.dma_start(out=v3(latS_d, NS * TS)[:, :, :], in_=lts.t[:, :, 0:tn]),
                              reads=[lts], writes=[latS_d], dma=True)
                lin_tm(hT, O_KV, 576, tts, cb_kv)

                zst = [sbp(phB, "zst", [128, 512]) for i in range(2)]
                zc = [0]
                for pc in (range(8) if not prefix else ()):
                    def cb_z(ti, t0, tn, banks, pc=pc):
                        (ps, co, w) = banks[0]
                        zb = zst[zc[0] % 2]
                        zc[0] += 1
                        copy(ev_eng(), zb.t[0:tn, :], ps.t[0:tn, 0:512], [ps], [zb])
                        P.add("sp", lambda e: e.dma_start(out=z_d.t[t0:t0 + tn, pc * 512:(pc + 1) * 512], in_=zb.t[0:tn, :]),
                              reads=[zb], writes=[z_d], dma=True)
                    lin_tm(hT, O_Z + pc * 512, 512, tts, cb_z)

                dts = sbp(phB, "dts", [128, HS])
                vcol = sbp(phB, "vcol", [128, 1])

                def cb_dt(ti, t0, tn, banks):
                    (ps, co, w) = banks[0]
                    P.add("dve", lambda e: e.tensor_tensor(out=dts.t[0:tn, :], in0=ps.t[0:tn, 0:HS], in1=dtb_b.t[0:tn, :], op=ALU.add),
                          reads=[ps, dtb_b], writes=[dts])
                    P.add("act", lambda e: e.activation(out=dts.t[0:tn, :], in_=dts.t[0:tn, :], func=AF.Exp), reads=[dts], writes=[dts])
                    P.add("act", lambda e: e.activation(out=dts.t[0:tn, :], in_=dts.t[0:tn, :], func=AF.Ln, bias=onesf.t[0:tn, 0:1]),
                          reads=[dts, onesf], writes=[dts])
                    if t0 < QT:
                        P.add("sp", lambda e: e.dma_start(out=vcol.t[0:tn, :], in_=valid_in[q * QT + t0:q * QT + t0 + tn, :]), writes=[vcol], dma=True)
                        P.add("dve", lambda e: e.tensor_scalar_mul(out=dts.t[0:tn, :], in0=dts.t[0:tn, :], scalar1=vcol.t[0:tn, 0:1]), reads=[dts, vcol], writes=[dts])
                    P.add("sp", lambda e: e.dma_start(out=dt_d.t[t0:t0 + tn, :], in_=dts.t[0:tn, :]), reads=[dts], writes=[dt_d], dma=True)
                lin_tm(hT, O_DT, HS, tts, cb_dt)

                halo_s = sbp(phB, "halo_s", [128, 48, NS * 3])
                if last:
                    sct = sbp(phB, "sct", [NS * 3, CONV])
                    P.add("sp", lambda e: e.dma_start(out=sct.t[:, :], in_=s_conv_in), writes=[sct], dma=True)
                    tr_to(lambda j, ps, ap: copy(ev_eng(), halo_s.t[:, j, :], ap, [ps], [halo_s]), sct,
                          [(j * 128, 128) for j in range(48)], NS * 3, [sct], [halo_s])
                xb = [sbp(phB, "xb", [128, 3 + QT]) for i in range(2)]
                xbs = [sbp(phB, "xbs", [128, NS, 3 + TS]) for i in range(2)]
                accs = [sbp(phB, "acc", [128, NTMAX]) for i in range(2)]
                fmb = [sbp(phB, "fmb", [128, NTMAX], BF16) for i in range(3)]
                xsts = [sbp(phB, "xst", [128, 9, 512], BF16) for i in range(2)]
                deferred = [None]
                for ct in range(48):
                    wb, wv = load_w(W["w_in"], 0, D, O_XBC + ct * 128, 128)
                    X, XS, FB = xb[ct % 2], xbs[ct % 2], fmb[ct % 3]
                    acc = accs[ct % 2]
                    P.add("pool", lambda e, X=X, ct=ct: e.tensor_copy(out=X.t[:, 0:3], in_=halo.t[:, ct, :]), reads=[halo], writes=[X])
                    if last:
                        P.add("pool", lambda e, XS=XS, ct=ct: e.tensor_copy(
                            out=XS.t[:, :, 0:3], in_=halo_s.t[:, ct, :].rearrange("p (j k) -> p j k", k=3)),
                            reads=[halo_s], writes=[XS])
                    for (b0, bw) in blks:
                        ps = next_ps()

                        def mm(e, wv=wv, ps=ps, b0=b0, bw=bw):
                            ins = None
                            for k in range(DC):
                                ins = e.matmul(ps.t[:, 0:bw], lhsT=wv[:, k, :], rhs=hT.t[:, k, b0:b0 + bw],
                                               start=(k == 0), stop=(k == DC - 1))
                            return ins
                        P.add("pe", mm, reads=[wb, hT], writes=[ps])
                        if b0 < QT:
                            copy(ev_eng(), X.t[:, 3 + b0:3 + b0 + bw], ps.t[:, 0:bw], [ps], [X])
                        else:
                            copy(ev_eng(), XS.t[:, :, 3:3 + TS], ps.t[:, 0:bw].rearrange("p (j t) -> p j t", t=TS), [ps], [XS])

                    def conv(e, X=X, XS=XS, ct=ct, acc=acc):
                        ins = e.tensor_scalar(out=acc.t[:, 0:QT], in0=X.t[:, 0:QT], scalar1=cwc.t[:, ct, 0:1],
                                              scalar2=cwc.t[:, ct, 4:5], op0=ALU.mult, op1=ALU.add)
                        for k in range(1, 4):
                            ins = e.scalar_tensor_tensor(out=acc.t[:, 0:QT], in0=X.t[:, k:k + QT], scalar=cwc.t[:, ct, k:k + 1],
                                                         in1=acc.t[:, 0:QT], op0=ALU.mult, op1=ALU.add)
                        if last:
                            a3 = acc.t[:, QT:NTMAX].rearrange("p (j t) -> p j t", t=TS)
                            ins = e.tensor_scalar(out=a3, in0=XS.t[:, :, 0:TS], scalar1=cwc.t[:, ct, 0:1],
                                                  scalar2=cwc.t[:, ct, 4:5], op0=ALU.mult, op1=ALU.add)
                            for k in range(1, 4):
                                ins = e.scalar_tensor_tensor(out=a3, in0=XS.t[:, :, k:k + TS], scalar=cwc.t[:, ct, k:k + 1],
                                                             in1=a3, op0=ALU.mult, op1=ALU.add)
                        e.tensor_copy(out=halo.t[:, ct, :], in_=X.t[:, QT:QT + 3])
                        ins = e.tensor_copy(out=convout.t[:, ct, 0:3], in_=X.t[:, QT:QT + 3])
                        if last:
                            ins = e.tensor_copy(out=convout.t[:, ct, 3:15].rearrange("p (j k) -> p j k", k=3),
                                                in_=XS.t[:, :, TS:TS + 3])
                        return ins
                    P.add("dve", conv, reads=[X, XS, cwc], writes=[acc, halo, convout])
                    P.add("act", lambda e, FB=FB, acc=acc: e.activation(out=FB.t[:, 0:nt], in_=acc.t[:, 0:nt], func=AF.Silu), reads=[acc], writes=[FB])
                    if ct >= 32:
                        P.add("sp", lambda e, FB=FB, ct=ct: e.dma_start(out=v3(bcT_d, NTMAX)[:, ct - 32, 0:nt], in_=FB.t[:, 0:nt]),
                              reads=[FB], writes=[bcT_d], dma=True)
                    if deferred[0] is not None:
                        deferred[0]()
                        deferred[0] = None
                    if ct < 40:
                        def emit_tr(ct=ct, acc=FB):
                            XST = xsts[(ct // 4) % 2]
                            for g0 in range(0, len(tts), 4):
                                ps = next_ps()
                                grp = tts[g0:g0 + 4]

                                def tr(e, ps=ps, grp=grp, acc=acc):
                                    ins = None
                                    for j, (t0, tn) in enumerate(grp):
                                        ins = e.matmul(ps.t[0:tn, j * 128:(j + 1) * 128], lhsT=acc.t[:, t0:t0 + tn], rhs=identb.t[:, :],
                                                       start=True, stop=True)
                                    return ins
                                P.add("pe", tr, reads=[acc, identb], writes=[ps])
                                if all(tn == 128 for (_, tn) in grp):
                                    ng = len(grp)
                                    copy(ev_eng(), XST.t[:, g0:g0 + ng, (ct % 4) * 128:(ct % 4 + 1) * 128],
                                         ps.t[:, 0:ng * 128].rearrange("p (j c) -> p j c", c=128), [ps], [XST])
                                else:
                                    for j, (t0, tn) in enumerate(grp):
                                        copy(ev_eng(), XST.t[0:tn, g0 + j, (ct % 4) * 128:(ct % 4 + 1) * 128],
                                             ps.t[0:tn, j * 128:(j + 1) * 128], [ps], [XST])
                            if ct % 4 == 3:
                                c0 = (ct // 4) * 512
                                for ti, (t0, tn) in enumerate(tts):
                                    if ct < 32:
                                        P.add("sp", lambda e, ti=ti, t0=t0, tn=tn, c0=c0: e.dma_start(
                                            out=xs_d.t[t0:t0 + tn, c0:c0 + 512], in_=XST.t[0:tn, ti, :]), reads=[XST], writes=[xs_d], dma=True)
                                    else:
                                        P.add("sp", lambda e, ti=ti, t0=t0, tn=tn, c0=c0: e.dma_start(
                                            out=b_d.t[t0:t0 + tn, c0 - DI:c0 - DI + 512], in_=XST.t[0:tn, ti, :]), reads=[XST], writes=[b_d], dma=True)
                        deferred[0] = emit_tr
                if deferred[0] is not None:
                    deferred[0]()
                    deferred[0] = None

                gst = [sbp(phB, "gst", [128, NTMAX]) for i in range(2)]
                for ct in (range(32) if not prefix else ()):
                    wb, wv = load_w(W["w_in"], 0, D, O_G + ct * 128, 128)
                    G = gst[ct % 2]
                    for (b0, bw) in blks:
                        ps = next_ps()

                        def mm(e, wv=wv, ps=ps, b0=b0, bw=bw):
                            ins = None
                            for k in range(DC):
                                ins = e.matmul(ps.t[:, 0:bw], lhsT=wv[:, k, :], rhs=hT.t[:, k, b0:b0 + bw],
                                               start=(k == 0), stop=(k == DC - 1))
                            return ins
                        P.add("pe", mm, reads=[wb, hT], writes=[ps])
                        copy(ev_eng(), G.t[:, b0:b0 + bw], ps.t[:, 0:bw], [ps], [G])
                    P.add("sp", lambda e, G=G, ct=ct: e.dma_start(out=v3(gT_d, NTMAX)[:, ct, 0:nt], in_=G.t[:, 0:nt]),
                          reads=[G], writes=[gT_d], dma=True)
                if last:
                    cst = next_stage()
                    for g0 in range(0, 48, 4):
                        ps = next_ps()

                        def tr(e, ps=ps, g0=g0):
                            ins = None
                            for j in range(4):
                                ins = e.transpose(out=ps.t[0:15, j * 128:(j + 1) * 128], in_=convout.t[:, g0 + j, 0:15],
                                                  identity=ident.t[:, :])
                            return ins
                        P.add("pe", tr, reads=[convout, ident], writes=[ps])
                        half = (g0 // 16)
                        if g0 % 16 == 0 and g0 > 0:
                            pass
                        copy(ev_eng(), cst.t[0:15, (g0 % 16) * 128:(g0 % 16) * 128 + 512], ps.t[0:15, 0:512], [ps], [cst])
                        if g0 % 16 == 12:
                            cc0 = half * 2048
                            P.add("sp", lambda e, cc0=cc0, cst=cst: e.dma_start(out=o_pconv[:, cc0:cc0 + 2048], in_=cst.t[0:3, :]), reads=[cst], dma=True)
                            P.add("sp", lambda e, cc0=cc0, cst=cst: e.dma_start(out=o_sconv[:, cc0:cc0 + 2048], in_=cst.t[3:15, :]), reads=[cst], dma=True)
                            cst = next_stage()
        P.fence()

    cst_p = din("cst_p", [128, 128 + 128 + 1])
    cst_s = din("cst_s", [128, 128 + 128 + 4])

    def attend(kh, krT, vh, qn_ap, qr_ap, ncols, ktiles, out_ap, pT, rden, masked=False):
        po, pd = psb[6], psb[7]
        nk_t = len(ktiles)
        pss = {}

        def emit_sc(i):
            (k0, kn, c_lo, diag) = ktiles[i]
            ps = next_ps()
            pss[i] = ps

            def sc(e, ps=ps, k0=k0, kn=kn, c_lo=c_lo):
                e.matmul(ps.t[0:kn, c_lo:ncols], lhsT=kh.t[:, k0:k0 + kn], rhs=qn_ap[:, c_lo:ncols], start=True, stop=False)
                return e.matmul(ps.t[0:kn, c_lo:ncols], lhsT=krT[0:64, k0:k0 + kn], rhs=qr_ap[0:64, c_lo:ncols], start=False, stop=True)
            P.add("pe", sc, reads=attn_reads, writes=[ps])

        def emit_rest(i):
            (k0, kn, c_lo, diag) = ktiles[i]
            ps = pss.pop(i)
            pt = pT[i % len(pT)]

            def ex(e, ps=ps, pt=pt, kn=kn, c_lo=c_lo, diag=diag, kb=(k0 // 128 if (masked and not diag) else None)):
                if diag:
                    e.activation(out=pt.t[0:kn, c_lo:c_lo + 64], in_=ps.t[0:kn, c_lo:c_lo + 64], func=AF.Exp, scale=SCALE,
                                 bias=negm.t[0:kn, 0:1])
                    return e.activation(out=pt.t[0:kn, c_lo + 64:ncols], in_=ps.t[0:kn, c_lo + 64:ncols], func=AF.Exp, scale=SCALE)
                if kb is not None:
                    return e.activation(out=pt.t[0:kn, c_lo:ncols], in_=ps.t[0:kn, c_lo:ncols], func=AF.Exp, scale=SCALE,
                                        bias=kneg.t[0:kn, kb:kb + 1])
                return e.activation(out=pt.t[0:kn, c_lo:ncols], in_=ps.t[0:kn, c_lo:ncols], func=AF.Exp, scale=SCALE)
            P.add("act", ex, reads=[ps, negm, kneg], writes=[pt])

            def pv(e, pt=pt, i=i, k0=k0, kn=kn, c_lo=c_lo):
                e.matmul(po.t[:, c_lo:ncols], lhsT=vh.t[0:kn, k0 // 128, :], rhs=pt.t[0:kn, c_lo:ncols], start=(i == 0), stop=(i == nk_t - 1))
                return e.matmul(pd.t[:, c_lo:ncols], lhsT=onesb.t[0:kn, :], rhs=pt.t[0:kn, c_lo:ncols], start=(i == 0), stop=(i == nk_t - 1))
            P.add("pe", pv, reads=[pt, vh, onesb], writes=[po, pd])

        LOOK = 2
        for i in range(min(LOOK, nk_t)):
            emit_sc(i)
        for i in range(nk_t):
            if i + LOOK < nk_t:
                emit_sc(i + LOOK)
            emit_rest(i)
        P.add("dve", lambda e: e.reciprocal(out=rden.t[:, 0:ncols], in_=pd.t[:, 0:ncols]), reads=[pd], writes=[rden], hz=True)
        P.add("dve", lambda e: e.tensor_tensor(out=out_ap, in0=po.t[:, 0:ncols], in1=rden.t[:, 0:ncols], op=ALU.mult),
              reads=[po, rden], writes=[oT_all_ref[0]])

    attn_reads = []
    kneg = sb("kneg", [128, SEQ // 128])
    P.add("sp", lambda e: e.dma_start(out=kneg[:], in_=kneg_in), writes=[kneg], dma=True)
    oT_all_ref = [None]

    def phase2_attn(q, last):
        nt = QT + (NS * TS if last else 0)
        blks = blocks_of(nt)
        nk = (q + 1) * QT
        with contextlib.ExitStack() as ph:
            oT_all = sbp(ph, "oT_all", [128, DC, NTMAX], BF16)
            qs_n = sbp(ph, "qs_n", [128, NH, NS * TS], BF16)
            qs_r = sbp(ph, "qs_r", [64, NH, NS * TS], BF16)
            pT = [sbp(ph, "pT", [128, 512], BF16) for i in range(3)]
            rden = sbp(ph, "rden", [128, 512])
            php = contextlib.ExitStack()
            cqT = sbp(php, "cqT", [128, 6, NTMAX], BF16)
            latT = sbp(php, "latT", [128, 5, nk], BF16)
            cosT = sbp(php, "cosT", [64, NTMAX])
            sinT = sbp(php, "sinT", [64, NTMAX])
            qn = sbp(php, "qn", [128, NTMAX], BF16)
            qr = sbp(php, "qr", [64, NTMAX], BF16)
            wsw = sbp(php, "wsw", [128, 6, 64], BF16)
            kh = sbp(php, "kh", [128, SEQ], BF16)
            vh = sbp(php, "vh", [128, SEQ // 128, 128], BF16)
            t1 = sbp(php, "t1", [64, 512])
            t2 = sbp(php, "t2", [64, 512])
            oT_all_ref[0] = oT_all
            attn_reads[:] = [kh, latT, qn, qr]
            P.add("sp", lambda e: e.dma_start(out=cqT.t[:, :, 0:nt], in_=v3(cqT_d, NTMAX)[:, :, 0:nt]), reads=[cqT_d], writes=[cqT], dma=True)
            P.add("sp", lambda e: e.dma_start(out=latT.t[:, :, :], in_=v3(latT_d, SEQ)[:, :, 0:nk]), reads=[latT_d], writes=[latT], dma=True)
            P.add("sp", lambda e: e.dma_start(out=cosT.t[:, 0:QT], in_=ropeT[0, :, q * QT:(q + 1) * QT]), writes=[cosT], dma=True)
            P.add("sp", lambda e: e.dma_start(out=sinT.t[:, 0:QT], in_=ropeT[1, :, q * QT:(q + 1) * QT]), writes=[sinT], dma=True)
            if last:
                for j in range(NS):
                    P.add("sp", lambda e, j=j: e.dma_start(out=cosT.t[:, QT + j * TS:QT + (j + 1) * TS], in_=ropeT[0, :, SEQ:SEQ + TS]), writes=[cosT], dma=True)
                    P.add("sp", lambda e, j=j: e.dma_start(out=sinT.t[:, QT + j * TS:QT + (j + 1) * TS], in_=ropeT[1, :, SEQ:SEQ + TS]), writes=[sinT], dma=True)

            def kv_for_head(wkb, wkv, srcT, nkeys):
                for (k0, kw) in blocks_of(nkeys):
                    ps = next_ps()

                    def mm(e, ps=ps, k0=k0, kw=kw):
                        ins = None
                        for k in range(4):
                            ins = e.matmul(ps.t[:, 0:kw], lhsT=wkv[:, k, 0:128], rhs=srcT.t[:, k, k0:k0 + kw], start=(k == 0), stop=(k == 3))
                        return ins
                    P.add("pe", mm, reads=[wkb, srcT], writes=[ps])
                    copy(ev_eng(), kh.t[:, k0:k0 + kw], ps.t[:, 0:kw], [ps], [kh])
                kts = blocks_of(nkeys, 128)
                for g0 in range(0, len(kts), 4):
                    ps = next_ps()
                    grp = kts[g0:g0 + 4]

                    def mv(e, ps=ps, grp=grp):
                        ins = None
                        for j, (k0, kn) in enumerate(grp):
                            for k in range(4):
                                ins = e.matmul(ps.t[0:kn, j * 128:(j + 1) * 128], lhsT=srcT.t[:, k, k0:k0 + kn], rhs=wkv[:, k, 128:256],
                                               start=(k == 0), stop=(k == 3))
                        return ins
                    P.add("pe", mv, reads=[wkb, srcT], writes=[ps])
                    for j, (k0, kn) in enumerate(grp):
                        copy(ev_eng(), vh.t[0:kn, k0 // 128, :], ps.t[0:kn, j * 128:(j + 1) * 128], [ps], [vh])

            for h in range(NH):
                wqb, wqv = load_w(W["w_qb"], 0, QL, h * 192, 192)
                wkb, wkv = load_w(W["w_kvb"], 0, KVL, h * 256, 256)
                P.add("pool", lambda e, wqv=wqv: e.tensor_copy(out=wsw.t[:, :, 0:32], in_=wqv[:, :, 160:192]), reads=[wqb], writes=[wsw])
                P.add("pool", lambda e, wqv=wqv: e.tensor_copy(out=wsw.t[:, :, 32:64], in_=wqv[:, :, 128:160]), reads=[wqb], writes=[wsw])
                for (b0, bw) in blks:
                    pn, pa, pb_ = next_ps(), next_ps(), next_ps()

                    def mq(e, wqv=wqv, pn=pn, pa=pa, pb_=pb_, b0=b0, bw=bw):
                        ins = None
                        for k in range(6):
                            ins = e.matmul(pn.t[:, 0:bw], lhsT=wqv[:, k, 0:128], rhs=cqT.t[:, k, b0:b0 + bw], start=(k == 0), stop=(k == 5))
                        for k in range(6):
                            ins = e.matmul(pa.t[0:64, 0:bw], lhsT=wqv[:, k, 128:192], rhs=cqT.t[:, k, b0:b0 + bw], start=(k == 0), stop=(k == 5))
                        for k in range(6):
                            ins = e.matmul(pb_.t[0:64, 0:bw], lhsT=wsw.t[:, k, :], rhs=cqT.t[:, k, b0:b0 + bw], start=(k == 0), stop=(k == 5))
                        return ins
                    P.add("pe", mq, reads=[wqb, wsw, cqT], writes=[pn, pa, pb_])
                    copy("act", qn.t[:, b0:b0 + bw], pn.t[:, 0:bw], [pn], [qn])

                    def rp(e, pa=pa, pb_=pb_, b0=b0, bw=bw):
                        e.tensor_tensor(out=t1.t[:, 0:bw], in0=pa.t[0:64, 0:bw], in1=cosT.t[:, b0:b0 + bw], op=ALU.mult)
                        e.tensor_tensor(out=t2.t[:, 0:bw], in0=pb_.t[0:64, 0:bw], in1=sinT.t[:, b0:b0 + bw], op=ALU.mult)
                        return e.tensor_tensor(out=qr.t[:, b0:b0 + bw], in0=t1.t[:, 0:bw], in1=t2.t[:, 0:bw], op=ALU.add)
                    P.add("dve", rp, reads=[pa, pb_, cosT, sinT], writes=[qr, t1, t2])
                if last:
                    P.add("pool", lambda e, h=h: e.tensor_copy(out=qs_n.t[:, h, :], in_=qn.t[:, QT:NTMAX]), reads=[qn], writes=[qs_n])
                    P.add("pool", lambda e, h=h: e.tensor_copy(out=qs_r.t[:, h, :], in_=qr.t[:, QT:NTMAX]), reads=[qr], writes=[qs_r])
                kv_for_head(wkb, wkv, latT, nk)
                for qb in range(2):
                    q0 = qb * 512
                    base_kt = (q * QT + q0) // 128
                    ktiles = []
                    for kt in range(base_kt + 4):
                        j = kt - base_kt
                        ktiles.append((kt * 128, 128, 128 * j if j > 0 else 0, j >= 0))
                    attend(kh, latT.t[:, 4, :], vh, qn.t[:, q0:q0 + 512], qr.t[:, q0:q0 + 512], 512, ktiles,
                           oT_all.t[:, h, q0:q0 + 512], pT, rden, masked=True)
            php.close()
            P.fence()
            if last:
                NKS = PAST + TS
                kts = blocks_of(NKS, 128)
                NKT = len(kts)
                ps_mod[0] = 5
                with contextlib.ExitStack() as ph2:
                    latS = sbp(ph2, "latS", [128, 5, NKS], BF16)
                    ltok = sbp(ph2, "ltok", [128, NKT, KVL], BF16)
                    WukT = sbp(ph2, "WukT", [128, NH, KVL], BF16)
                    Wuv = sbp(ph2, "Wuv", [128, NH, 4, 128], BF16)
                    cst = [sbp(ph2, "cstg", [128, 576]) for i in range(2)]
                    qlat = sbp(ph2, "qlat", [128, 4, 256], BF16)
                    qsr = sbp(ph2, "qsr", [64, 256], BF16)
                    olat = sbp(ph2, "olat", [128, 4, 256], BF16)
                    rdn = sbp(ph2, "rdn", [128, 256])
                    for h in range(NH):
                        wkb, wkv = load_w(W["w_kvb"], 0, KVL, h * 256, 256)
                        ps = next_ps()

                        def trw(e, ps=ps, wkv=wkv):
                            ins = None
                            for k in range(4):
                                ins = e.matmul(ps.t[:, k * 128:(k + 1) * 128], lhsT=wkv[:, k, 0:128], rhs=identb.t[:, :], start=True, stop=True)
                            return ins
                        P.add("pe", trw, reads=[wkb, identb], writes=[ps])
                        copy(ev_eng(), WukT.t[:, h, :], ps.t[:, 0:512], [ps], [WukT])
                        copy("act", Wuv.t[:, h, :, :], wkv[:, :, 128:256], [wkb], [Wuv])
                    for j in range(NS):
                        for kt in range(PAST // 128):
                            cs_ = cst[kt % 2]
                            P.add("sp", lambda e, cs_=cs_, j=j, kt=kt: e.dma_start(out=cs_.t[:, 0:KVL], in_=c_lat[j, kt * 128:(kt + 1) * 128, :]), writes=[cs_], dma=True)
                            P.add("sp", lambda e, cs_=cs_, j=j, kt=kt: e.dma_start(out=cs_.t[:, KVL:576], in_=c_kr[j, kt * 128:(kt + 1) * 128, :]), writes=[cs_], dma=True)
                            pieces = [(c * 128, 128) for c in range(4)] + [(512, 64)]
                            tr_to(lambda c, ps, ap, kt=kt: copy(ev_eng(), latS.t[0:pieces[c][1], c, kt * 128:(kt + 1) * 128], ap, [ps], [latS]),
                                  cs_, pieces, 128, [cs_], [latS])
                            copy("dve", ltok.t[:, kt, :], cs_.t[:, 0:KVL], [cs_], [ltok])
                        P.add("sp", lambda e, j=j: e.dma_start(out=latS.t[:, :, PAST:NKS], in_=v3(latS_d, NS * TS)[:, :, j * TS:(j + 1) * TS]),
                              reads=[latS_d], writes=[latS], dma=True)
                        P.add("sp", lambda e, j=j: e.dma_start(out=ltok.t[0:TS, NKT - 1, :], in_=latStok_d.t[j * TS:(j + 1) * TS, :]),
                              reads=[latStok_d], writes=[ltok], dma=True)
                        copy("act", qsr.t[:, :].rearrange("p (h t) -> p h t", t=TS), qs_r.t[:, :, j * TS:(j + 1) * TS], [qs_r], [qsr])
                        pq = [next_ps(), next_ps()]

                        def mql(e, pq=pq, j=j):
                            ins = None
                            for h in range(NH):
                                for k in range(4):
                                    ins = e.matmul(pq[k // 2].t[:, (k % 2) * 256 + h * TS:(k % 2) * 256 + (h + 1) * TS],
                                                   lhsT=WukT.t[:, h, k * 128:(k + 1) * 128], rhs=qs_n.t[:, h, j * TS:(j + 1) * TS], start=True, stop=True)
                            return ins
                        P.add("pe", mql, reads=[WukT, qs_n], writes=pq)
                        for b2 in range(2):
                            copy(ev_eng(), qlat.t[:, 2 * b2:2 * b2 + 2, :], pq[b2].t[:, 0:512].rearrange("p (k c) -> p k c", c=256), [pq[b2]], [qlat])
                        pacc = [psb[6], psb[7]]
                        pden = psb[5]
                        pss = {}

                        def e_sc(i, j=j):
                            (k0, kn) = kts[i]
                            ps = next_ps()
                            pss[i] = ps

                            def sc(e, ps=ps, k0=k0, kn=kn):
                                for k in range(4):
                                    e.matmul(ps.t[0:kn, 0:256], lhsT=latS.t[:, k, k0:k0 + kn], rhs=qlat.t[:, k, :], start=(k == 0), stop=False)
                                return e.matmul(ps.t[0:kn, 0:256], lhsT=latS.t[0:64, 4, k0:k0 + kn], rhs=qsr.t[:, :], start=False, stop=True)
                            P.add("pe", sc, reads=[latS, qlat, qsr], writes=[ps])

                        def e_rest(i):
                            (k0, kn) = kts[i]
                            ps = pss.pop(i)
                            pt = pT[i % len(pT)]
                            P.add("act", lambda e, ps=ps, pt=pt, kn=kn: e.activation(out=pt.t[0:kn, 0:256], in_=ps.t[0:kn, 0:256], func=AF.Exp, scale=SCALE),
                                  reads=[ps], writes=[pt])

                            def pv(e, pt=pt, i=i, kn=kn):
                                for k in range(4):
                                    e.matmul(pacc[k // 2].t[:, (k % 2) * 256:(k % 2 + 1) * 256], lhsT=ltok.t[0:kn, i, k * 128:(k + 1) * 128],
                                             rhs=pt.t[0:kn, 0:256], start=(i == 0), stop=(i == NKT - 1))
                                return e.matmul(pden.t[:, 0:256], lhsT=onesb.t[0:kn, :], rhs=pt.t[0:kn, 0:256], start=(i == 0), stop=(i == NKT - 1))
                            P.add("pe", pv, reads=[pt, ltok, onesb], writes=[pacc[0], pacc[1], pden])
                        for i in range(2):
                            e_sc(i)
                        for i in range(NKT):
                            if i + 2 < NKT:
                                e_sc(i + 2)
                            e_rest(i)
                        P.add("dve", lambda e: e.reciprocal(out=rdn.t[:, :], in_=pden.t[:, 0:256]), reads=[pden], writes=[rdn], hz=True)
                        for b2 in range(2):
                            P.add("dve", lambda e, b2=b2: e.tensor_tensor(
                                out=olat.t[:, 2 * b2:2 * b2 + 2, :], in0=pacc[b2].t[:, 0:512].rearrange("p (k c) -> p k c", c=256),
                                in1=rdn.t[:, :].unsqueeze(1).broadcast_to([128, 2, 256]), op=ALU.mult), reads=[pacc[b2], rdn], writes=[olat])
                        pf = next_ps()

                        def mfin(e, pf=pf):
                            ins = None
                            for h in range(NH):
                                for k in range(4):
                                    ins = e.matmul(pf.t[:, h * TS:(h + 1) * TS], lhsT=Wuv.t[:, h, k, :], rhs=olat.t[:, k, h * TS:(h + 1) * TS],
                                                   start=(k == 0), stop=(k == 3))
                            return ins
                        P.add("pe", mfin, reads=[Wuv, olat], writes=[pf])
                        copy(ev_eng(), oT_all.t[:, :, QT + j * TS:QT + (j + 1) * TS], pf.t[:, 0:256].rearrange("p (h t) -> p h t", t=TS), [pf], [oT_all])
                ps_mod[0] = 6
            P.add("sp", lambda e: e.dma_start(out=v3(oT_d, NTMAX)[:, :, 0:nt], in_=oT_all.t[:, :, 0:nt]), reads=[oT_all], writes=[oT_d], dma=True)
        P.fence()

    def phase2_ssd(q, first, last):
        prefix = not last
        with contextlib.ExitStack() as ph:
            S32 = sbp(ph, "S32", [128, DI])
            S16 = sbp(ph, "S16", [128, DI], BF16)
            cp = sbp(ph, "cp", [128, 257])
            cs = sbp(ph, "cs", [128, 260])
            nssm_b = sbp(ph, "nssm_b", [128, DI])
            xs_t = sbp(ph, "xs_t", [128, DI], BF16)
            b_t = sbp(ph, "b_t", [128, NG * NST], BF16)
            dt_t = sbp(ph, "dt_t", [128, HS])
            a_t = sbp(ph, "a_t", [128, HS])
            am = sbp(ph, "am", [128, HS])
            te = sbp(ph, "te", [128, HS])
            et = sbp(ph, "et", [128, HS])
            decb = sbp(ph, "decb", [128, HS])
            BT = sbp(ph, "BT", [128, NG, 128], BF16)
            CT = sbp(ph, "CT", [128, NG, 128], BF16)
            xdt = sbp(ph, "xdt", [128, DI], BF16)
            xw = sbp(ph, "xw", [128, DI], BF16)
            xwm = sbp(ph, "xwm", [128, DI], BF16)
            AV2 = sbp(ph, "AV2", [128, 16 * 128])
            MT2 = sbp(ph, "MT2", [128, HS * 128], BF16)
            CBm = sbp(ph, "CBm", [128, NG * 128])
            y_sb = sbp(ph, "y_sb", [128, DI])
            tmp = sbp(ph, "tmp", [128, 512])
            zt = [sbp(ph, "zt", [128, 512]) for i in range(2)]
            ssg = sbp(ph, "ssg", [128, NG])
            ynst = sbp(ph, "ynst", [128, 32, 128], BF16)
            P.add("sp", lambda e: e.dma_start(out=cp.t[:, :], in_=cst_p), writes=[cp], dma=True)
            P.add("sp", lambda e: e.dma_start(out=cs.t[:, :], in_=cst_s), writes=[cs], dma=True)
            bc_load(nssm_b, W["n_ssm"], DI)
            if first:
                P.add("dve", lambda e: e.memset(S32.t[:, :], 0.0), writes=[S32])
            else:
                P.add("sp", lambda e: e.dma_start(out=S32.t[:, :], in_=S_d.t[:, :]), reads=[S_d], writes=[S32], dma=True)
            P.add("act", lambda e: e.activation(out=S16.t[:, :], in_=S32.t[:, :], func=AF.Copy), reads=[S32], writes=[S16])

            def state_in(src2d):
                for hb in range(2):
                    stg = next_stage()
                    P.add("sp", lambda e, stg=stg, hb=hb: e.dma_start(
                        out=stg.t[:, :].rearrange("p (b n) -> p b n", n=128),
                        in_=src2d[hb * 2048:(hb + 1) * 2048, :].rearrange("(b p) n -> p b n", p=128)), writes=[stg], dma=True)
                    tr_to(lambda b, ps, ap, hb=hb: copy(ev_eng(), S32.t[:, (hb * 16 + b) * 128:(hb * 16 + b + 1) * 128], ap, [ps], [S32]),
                          stg, [(b * 128, 128) for b in range(16)], 128, [stg], [S32])
                P.add("act", lambda e: e.activation(out=S16.t[:, :], in_=S32.t[:, :], func=AF.Copy), reads=[S32], writes=[S16])

            def state_out(dst2d):
                for hb in range(2):
                    stg = next_stage()
                    tr_to(lambda b, ps, ap, stg=stg: copy(ev_eng(), stg.t[:, b * 128:(b + 1) * 128], ap, [ps], [stg]),
                          Buf(S32.t[:, hb * 2048:(hb + 1) * 2048], "S32v"), [(b * 128, 128) for b in range(16)], 128, [S32], [stg])
                    P.add("sp", lambda e, stg=stg, hb=hb: e.dma_start(
                        out=dst2d[hb * 2048:(hb + 1) * 2048, :].rearrange("(b p) n -> p b n", p=128),
                        in_=stg.t[:, :].rearrange("p (b n) -> p b n", n=128)), reads=[stg], dma=True)

            tiles = [(i * 128, 128, cp, 1, 256) for i in range(QT // 128)]
            if last:
                tiles.append((QT, NS * TS, cs, NS, 256))
            for (t0, R, C, nseg, mo) in tiles:
                sample = (t0 >= QT)
                if sample:
                    state_out(o_pssm)
                tri2 = C.t[0:R, 0:R]
                us2 = C.t[0:R, 128:128 + R]
                P.add("sp", lambda e, t0=t0, R=R: e.dma_start(out=xs_t.t[0:R, :], in_=xs_d.t[t0:t0 + R, :]), reads=[xs_d], writes=[xs_t], dma=True)
                P.add("sp", lambda e, t0=t0, R=R: e.dma_start(out=b_t.t[0:R, :], in_=b_d.t[t0:t0 + R, :]), reads=[b_d], writes=[b_t], dma=True)
                P.add("sp", lambda e, t0=t0, R=R: e.dma_start(out=dt_t.t[0:R, :], in_=dt_d.t[t0:t0 + R, :]), reads=[dt_d], writes=[dt_t], dma=True)
                P.add("sp", lambda e, t0=t0, R=R: e.dma_start(out=BT.t[:, :, 0:R], in_=v3(bcT_d, NTMAX)[:, 0:8, t0:t0 + R]), reads=[bcT_d], writes=[BT], dma=True)
                P.add("sp", lambda e, t0=t0, R=R: e.dma_start(out=CT.t[:, :, 0:R], in_=v3(bcT_d, NTMAX)[:, 8:16, t0:t0 + R]), reads=[bcT_d], writes=[CT], dma=True)
                P.add("dve", lambda e, R=R: e.tensor_tensor(out=a_t.t[0:R, :], in0=dt_t.t[0:R, :], in1=A_b.t[0:R, :], op=ALU.mult), reads=[dt_t, A_b], writes=[a_t])
                x3 = lambda b, R=R: b.t[0:R, :].rearrange("p (r c) -> p r c", c=64)
                P.add("dve", lambda e, R=R, x3=x3: e.tensor_tensor(out=x3(xdt), in0=x3(xs_t), in1=dt_t.t[0:R, :].unsqueeze(2).broadcast_to([R, HS, 64]), op=ALU.mult),
                      reads=[xs_t, dt_t], writes=[xdt])
                if not prefix:
                    P.add("dve", lambda e, R=R, x3=x3: e.tensor_tensor(out=x3(y_sb), in0=x3(xs_t), in1=dsk_b.t[0:R, :].unsqueeze(2).broadcast_to([R, HS, 64]), op=ALU.mult),
                          reads=[xs_t, dsk_b], writes=[y_sb])
                for (dst, msk) in ((te, us2), (et, tri2)):
                    ps = next_ps()
                    P.add("pe", lambda e, ps=ps, msk=msk, R=R: e.matmul(ps.t[0:R, 0:HS], lhsT=msk, rhs=a_t.t[0:R, :], start=True, stop=True), reads=[a_t, C], writes=[ps])
                    P.add("act", lambda e, ps=ps, dst=dst, R=R: e.activation(out=dst.t[0:R, :], in_=ps.t[0:R, 0:HS], func=AF.Exp), reads=[ps], writes=[dst])
                P.add("dve", lambda e, R=R, x3=x3: e.tensor_tensor(out=x3(xw), in0=x3(xdt), in1=te.t[0:R, :].unsqueeze(2).broadcast_to([R, HS, 64]), op=ALU.mult),
                      reads=[xdt, te], writes=[xw])
                if not prefix:
                    gpb = 512 // R
                    for g0 in range(0, NG, gpb):
                        ps = next_ps()

                        def mcb(e, ps=ps, g0=g0, R=R, gpb=gpb):
                            ins = None
                            for gi in range(gpb):
                                ins = e.matmul(ps.t[0:R, gi * R:(gi + 1) * R], lhsT=BT.t[:, g0 + gi, 0:R], rhs=CT.t[:, g0 + gi, 0:R], start=True, stop=True)
                            return ins
                        P.add("pe", mcb, reads=[BT, CT], writes=[ps])
                        P.add("dve", lambda e, ps=ps, g0=g0, R=R, gpb=gpb, tri2=tri2: e.tensor_tensor(
                            out=CBm.t[0:R, g0 * R:(g0 + gpb) * R].rearrange("p (g l) -> p g l", l=R),
                            in0=ps.t[0:R, 0:gpb * R].rearrange("p (g l) -> p g l", l=R),
                            in1=tri2.unsqueeze(1).broadcast_to([R, gpb, R]), op=ALU.mult), reads=[ps, C], writes=[CBm])
                    avb_ap = xwm.t[:, :].bitcast(F32)

                    def av_of(hq):
                        return (AV2.t, AV2) if hq % 2 == 0 else (avb_ap, xwm)

                    def emit_av(hq, R=R, tri2=tri2):
                        ap, buf = av_of(hq)
                        P.add("dve", lambda e, hq=hq, R=R, tri2=tri2, ap=ap: e.tensor_tensor(
                            out=ap[0:R, 0:16 * R].rearrange("p (r l) -> p r l", l=R),
                            in0=a_t.t[0:R, hq * 16:(hq + 1) * 16].unsqueeze(2).broadcast_to([R, 16, R]),
                            in1=tri2.unsqueeze(1).broadcast_to([R, 16, R]), op=ALU.mult), reads=[a_t, C], writes=[buf])
                    emit_av(0)
                    for hq in range(4):
                        if hq + 1 < 4:
                            emit_av(hq + 1)
                        ap, buf = av_of(hq)
                        nb = 16 * R // 512
                        for bk in range(nb):
                            ps = next_ps()
                            P.add("pe", lambda e, ps=ps, bk=bk, us2=us2, R=R, ap=ap: e.matmul(ps.t[0:R, 0:512], lhsT=us2, rhs=ap[0:R, bk * 512:(bk + 1) * 512],
                                                                                  start=True, stop=True), reads=[buf, C], writes=[ps])
                            o0 = hq * 16 * R + bk * 512
                            P.add("act", lambda e, ps=ps, o0=o0, R=R: e.activation(out=MT2.t[0:R, o0:o0 + 512], in_=ps.t[0:R, 0:512], func=AF.Exp),
                                  reads=[ps], writes=[MT2])
                        P.add("dve", lambda e, hq=hq, R=R: e.tensor_tensor(
                            out=MT2.t[0:R, hq * 16 * R:(hq + 1) * 16 * R].rearrange("p (g r l) -> p g r l", r=8, l=R),
                            in0=MT2.t[0:R, hq * 16 * R:(hq + 1) * 16 * R].rearrange("p (g r l) -> p g r l", r=8, l=R),
                            in1=CBm.t[0:R, hq * 2 * R:(hq * 2 + 2) * R].rearrange("p (g l) -> p g l", l=R).unsqueeze(2).broadcast_to([R, 2, 8, R]),
                            op=ALU.mult), reads=[MT2, CBm], writes=[MT2])
                    for g in range(NG):
                        ps = next_ps()

                        def myd(e, ps=ps, g=g, R=R):
                            ins = None
                            for rr in range(8):
                                r = g * 8 + rr
                                ins = e.matmul(ps.t[0:R, rr * 64:(rr + 1) * 64], lhsT=MT2.t[0:R, r * R:(r + 1) * R], rhs=xdt.t[0:R, r * 64:(r + 1) * 64],
                                               start=True, stop=True)
                            return ins
                        P.add("pe", myd, reads=[MT2, xdt], writes=[ps])
                        P.add("dve", lambda e, ps=ps, g=g, R=R: e.tensor_tensor(out=y_sb.t[0:R, g * 512:(g + 1) * 512], in0=ps.t[0:R, 0:512],
                                                                               in1=y_sb.t[0:R, g * 512:(g + 1) * 512], op=ALU.add), reads=[ps, y_sb], writes=[y_sb])
                for sg in range(nseg):
                    mcol = C.t[0:R, mo + sg:mo + sg + 1]
                    if sample:
                        state_in(s_ssm_in[sg])
                    AM = am if nseg > 1 else a_t
                    XWM = xwm if nseg > 1 else xw
                    if nseg > 1:
                        P.add("dve", lambda e, mcol=mcol, R=R: e.tensor_scalar_mul(out=am.t[0:R, :], in0=a_t.t[0:R, :], scalar1=mcol), reads=[a_t, C], writes=[am])
                    ps = next_ps()
                    P.add("pe", lambda e, ps=ps, R=R, AM=AM: e.matmul(ps.t[:, 0:HS], lhsT=onesf.t[0:R, :], rhs=AM.t[0:R, :], start=True, stop=True), reads=[AM, onesf], writes=[ps])
                    P.add("act", lambda e, ps=ps: e.activation(out=decb.t[:, :], in_=ps.t[:, 0:HS], func=AF.Exp), reads=[ps], writes=[decb])
                    if nseg > 1:
                        P.add("dve", lambda e, mcol=mcol, R=R: e.tensor_scalar_mul(out=xwm.t[0:R, :], in0=xw.t[0:R, :], scalar1=mcol), reads=[xw, C], writes=[xwm])
                    if not prefix:
                        for g in range(NG):
                            ps = next_ps()
                            P.add("pe", lambda e, ps=ps, g=g, R=R: e.matmul(ps.t[0:R, 0:512], lhsT=CT.t[:, g, 0:R], rhs=S16.t[:, g * 512:(g + 1) * 512], start=True, stop=True),
                                  reads=[CT, S16], writes=[ps])
                            P.add("dve", lambda e, ps=ps, g=g, R=R, mcol=mcol: e.scalar_tensor_tensor(
                                out=tmp.t[0:R, :].rearrange("p (r c) -> p r c", c=64), in0=ps.t[0:R, 0:512].rearrange("p (r c) -> p r c", c=64), scalar=mcol,
                                in1=et.t[0:R, g * 8:(g + 1) * 8].unsqueeze(2).broadcast_to([R, 8, 64]), op0=ALU.mult, op1=ALU.mult), reads=[ps, et, C], writes=[tmp])
                            P.add("dve", lambda e, g=g, R=R: e.tensor_tensor(out=y_sb.t[0:R, g * 512:(g + 1) * 512], in0=tmp.t[0:R, :],
                                                                            in1=y_sb.t[0:R, g * 512:(g + 1) * 512], op=ALU.add), reads=[tmp, y_sb], writes=[y_sb])
                    for g in range(NG):
                        ps = next_ps()
                        P.add("pe", lambda e, ps=ps, g=g, R=R, XWM=XWM: e.matmul(ps.t[:, 0:512], lhsT=b_t.t[0:R, g * 128:(g + 1) * 128], rhs=XWM.t[0:R, g * 512:(g + 1) * 512],
                                                                       start=True, stop=True), reads=[b_t, XWM], writes=[ps])
                        P.add("dve", lambda e, g=g: e.tensor_tensor(
                            out=S32.t[:, g * 512:(g + 1) * 512].rearrange("p (r c) -> p r c", c=64),
                            in0=S32.t[:, g * 512:(g + 1) * 512].rearrange("p (r c) -> p r c", c=64),
                            in1=decb.t[:, g * 8:(g + 1) * 8].unsqueeze(2).broadcast_to([128, 8, 64]), op=ALU.mult), reads=[S32, decb], writes=[S32])
                        P.add("dve", lambda e, ps=ps, g=g: e.tensor_tensor(out=S32.t[:, g * 512:(g + 1) * 512], in0=ps.t[:, 0:512],
                                                                          in1=S32.t[:, g * 512:(g + 1) * 512], op=ALU.add), reads=[ps, S32], writes=[S32])
                    if not prefix:
                        P.add("act", lambda e: e.activation(out=S16.t[:, :], in_=S32.t[:, :], func=AF.Copy), reads=[S32], writes=[S16])
                    if sample:
                        state_out(o_sssm[sg])
                if not prefix:
                    for g in range(NG):
                        Z = zt[g % 2]
                        P.add("sp", lambda e, Z=Z, g=g, t0=t0, R=R: e.dma_start(out=Z.t[0:R, :], in_=z_d.t[t0:t0 + R, g * 512:(g + 1) * 512]), reads=[z_d], writes=[Z], dma=True)
                        P.add("act", lambda e, Z=Z, R=R: e.activation(out=Z.t[0:R, :], in_=Z.t[0:R, :], func=AF.Silu), reads=[Z], writes=[Z])
                        P.add("dve", lambda e, Z=Z, g=g, R=R: e.tensor_tensor(out=y_sb.t[0:R, g * 512:(g + 1) * 512], in0=y_sb.t[0:R, g * 512:(g + 1) * 512],
                                                                             in1=Z.t[0:R, :], op=ALU.mult), reads=[Z, y_sb], writes=[y_sb])
                        P.add("act", lambda e, Z=Z, g=g, R=R: e.activation(out=Z.t[0:R, :], in_=y_sb.t[0:R, g * 512:(g + 1) * 512], func=AF.Square), reads=[y_sb], writes=[Z])
                        P.add("dve", lambda e, Z=Z, g=g, R=R: e.reduce_sum(out=ssg.t[0:R, g:g + 1], in_=Z.t[0:R, :], axis=AX.X), reads=[Z], writes=[ssg])
                    P.add("act", lambda e, R=R: e.activation(out=ssg.t[0:R, :], in_=ssg.t[0:R, :], func=AF.Sqrt, scale=1.0 / 512, bias=epsc.t[0:R, 0:1]), reads=[ssg, epsc], writes=[ssg])
                    P.add("dve", lambda e, R=R: e.reciprocal(out=ssg.t[0:R, :], in_=ssg.t[0:R, :]), reads=[ssg], writes=[ssg], hz=True)
                    P.add("dve", lambda e, R=R: e.tensor_tensor(out=y_sb.t[0:R, :].rearrange("p (g c) -> p g c", c=512), in0=y_sb.t[0:R, :].rearrange("p (g c) -> p g c", c=512),
                                                               in1=ssg.t[0:R, :].unsqueeze(2).broadcast_to([R, NG, 512]), op=ALU.mult), reads=[ssg, y_sb], writes=[y_sb])
                    P.add("dve", lambda e, R=R: e.tensor_tensor(out=y_sb.t[0:R, :], in0=y_sb.t[0:R, :], in1=nssm_b.t[0:R, :], op=ALU.mult), reads=[nssm_b, y_sb], writes=[y_sb])
                    tr_to(lambda c, ps, ap, R=R: copy(ev_eng(), ynst.t[:, c, 0:R], ap, [ps], [ynst]), y_sb, [(c * 128, 128) for c in range(32)], R, [y_sb], [ynst])
                    P.add("sp", lambda e, t0=t0, R=R: e.dma_start(out=v3(ynT_d, NTMAX)[:, :, t0:t0 + R], in_=ynst.t[:, :, 0:R]), reads=[ynst], writes=[ynT_d], dma=True)
            if last and not any(t[0] >= QT for t in tiles):
                state_out(o_pssm)
            if not last:
                P.add("sp", lambda e: e.dma_start(out=S_d.t[:, :], in_=S32.t[:, :]), reads=[S32], writes=[S_d], dma=True)
        P.fence()

    def phase3(q, last):
        nt = QT + (NS * TS if last else 0)
        blks = blocks_of(nt)
        if True:
            with contextlib.ExitStack() as ph:
                mst = [sbp(ph, "mst", [128, NTMAX], BF16) for i in range(2)]
                oT = sbp(ph, "oT3", [128, DC, NTMAX], BF16)
                ynT = sbp(ph, "ynT3", [128, 32, NTMAX], BF16)
                gA = [sbp(ph, "gA", [128, NTMAX]) for i in range(2)]
                gB = [sbp(ph, "gB", [128, NTMAX]) for i in range(2)]
                P.add("sp", lambda e: e.dma_start(out=oT.t[:, :, 0:nt], in_=v3(oT_d, NTMAX)[:, :, 0:nt]), reads=[oT_d], writes=[oT], dma=True)
                P.add("sp", lambda e: e.dma_start(out=ynT.t[:, :, 0:nt], in_=v3(ynT_d, NTMAX)[:, :, 0:nt]), reads=[ynT_d], writes=[ynT], dma=True)
                for d in range(DC):
                    wab, wav = load_w(W["w_oa"], 0, D, d * 128, 128)
                    wsb1, wsv1 = load_w(W["w_os"], 0, 2048, d * 128, 128)
                    wsb2, wsv2 = load_w(W["w_os"], 2048, 2048, d * 128, 128)
                    GA, GB = gA[d % 2], gB[d % 2]
                    P.add("sp", lambda e, GA=GA, d=d: e.dma_start(out=GA.t[:, 0:nt], in_=v3(gT_d, NTMAX)[:, d, 0:nt]), reads=[gT_d], writes=[GA], dma=True)
                    P.add("sp", lambda e, GB=GB, d=d: e.dma_start(out=GB.t[:, 0:nt], in_=v3(gT_d, NTMAX)[:, 16 + d, 0:nt]), reads=[gT_d], writes=[GB], dma=True)
                    P.add("act", lambda e, GA=GA, d=d: e.activation(out=GA.t[:, 0:nt], in_=GA.t[:, 0:nt], func=AF.Sigmoid, bias=bgc.t[:, d:d + 1]), reads=[GA, bgc], writes=[GA])
                    P.add("act", lambda e, GB=GB, d=d: e.activation(out=GB.t[:, 0:nt], in_=GB.t[:, 0:nt], func=AF.Sigmoid, bias=bgc.t[:, 16 + d:17 + d]), reads=[GB, bgc], writes=[GB])
                    for (b0, bw) in blks:
                        pa, pb_ = next_ps(), next_ps()

                        def mm(e, pa=pa, pb_=pb_, b0=b0, bw=bw, wav=wav, wsv1=wsv1, wsv2=wsv2):
                            ins = None
                            for k in range(DC):
                                ins = e.matmul(pa.t[:, 0:bw], lhsT=wav[:, k, :], rhs=oT.t[:, k, b0:b0 + bw], start=(k == 0), stop=(k == DC - 1))
                            for k in range(32):
                                wv = wsv1 if k < 16 else wsv2
                                ins = e.matmul(pb_.t[:, 0:bw], lhsT=wv[:, k % 16, :], rhs=ynT.t[:, k, b0:b0 + bw], start=(k == 0), stop=(k == 31))
                            return ins
                        P.add("pe", mm, reads=[wab, wsb1, wsb2, oT, ynT], writes=[pa, pb_])
                        P.add("dve", lambda e, pa=pa, GA=GA, b0=b0, bw=bw: e.tensor_tensor(out=GA.t[:, b0:b0 + bw], in0=pa.t[:, 0:bw], in1=GA.t[:, b0:b0 + bw], op=ALU.mult),
                              reads=[pa, GA], writes=[GA])
                        P.add("dve", lambda e, pb_=pb_, GB=GB, b0=b0, bw=bw: e.tensor_tensor(out=GB.t[:, b0:b0 + bw], in0=pb_.t[:, 0:bw], in1=GB.t[:, b0:b0 + bw], op=ALU.mult),
                              reads=[pb_, GB], writes=[GB])
                    MS = mst[d % 2]
                    P.add("dve", lambda e, GA=GA, GB=GB, MS=MS: e.tensor_tensor(out=MS.t[:, 0:nt], in0=GA.t[:, 0:nt], in1=GB.t[:, 0:nt], op=ALU.add),
                          reads=[GA, GB], writes=[MS])
                    P.add("sp", lambda e, MS=MS, d=d: e.dma_start(out=v3(mT_d, NTMAX)[:, d, 0:nt], in_=MS.t[:, 0:nt]), reads=[MS], writes=[mT_d], dma=True)
            P.fence()
        with contextlib.ExitStack() as phX:
            xT = sbp(phX, "xT3", [128, DC, NTMAX])
            hT = sbp(phX, "hT3", [128, DC, NTMAX], BF16)
            P.add("sp", lambda e: e.dma_start(out=xT.t[:, :, 0:nt], in_=v3(x1T_d, NTMAX)[:, :, 0:nt]), reads=[x1T_d], writes=[xT], dma=True)
            P.add("sp", lambda e: e.dma_start(out=hT.t[:, :, 0:nt], in_=v3(mT_d, NTMAX)[:, :, 0:nt]), reads=[mT_d], writes=[hT], dma=True)
            if True:
                for d in range(DC):
                    wob, wov = load_w(W["w_out"], 0, D, d * 128, 128)
                    for (b0, bw) in blks:
                        ps = next_ps()

                        def mo_(e, ps=ps, b0=b0, bw=bw, wov=wov):
                            ins = None
                            for k in range(DC):
                                ins = e.matmul(ps.t[:, 0:bw], lhsT=wov[:, k, :], rhs=hT.t[:, k, b0:b0 + bw], start=(k == 0), stop=(k == DC - 1))
                            return ins
                        P.add("pe", mo_, reads=[wob, hT], writes=[ps])
                        P.add("dve", lambda e, ps=ps, d=d, b0=b0, bw=bw: e.tensor_tensor(out=xT.t[:, d, b0:b0 + bw], in0=ps.t[:, 0:bw], in1=xT.t[:, d, b0:b0 + bw], op=ALU.add),
                              reads=[ps, xT], writes=[xT])
            P.fence()
            with contextlib.ExitStack() as ph:
                sqb = [sbp(ph, "sq3", [128, NTMAX], BF16) for i in range(2)]
                rstd = sbp(ph, "rstd3", [128, NTMAX])
                rmsnorm_fm(xT, sqb, rstd, 2, nt, hT)
                with contextlib.ExitStack() as phF:
                    ffn(phF, xT, hT, W["wg2"], W["wu2"], W["wd2"], nt)
                P.fence()
                rmsnorm_fm(xT, sqb, rstd, 3, nt, xT)
                for tt in range(QT // 128):
                    store_tok(y_p[tt * 128:(tt + 1) * 128, :], xT, DC, tt * 128, 128)
                if last:
                    store_tok(y_s[:, :], xT, DC, QT, NS * TS)
        P.fence()

    quarters = stages.get("quarters", list(range(NQ)))
    for qi, q in enumerate(quarters):
        last = (qi == len(quarters) - 1)
        phase1(q, last)
        if stages.get("upto") == "ffn1":
            continue
        if last:
            phase2_attn(q, last)
        phase2_ssd(q, qi == 0, last)
        if last:
            phase3(q, last)

    if stages.get("dump"):
        for nm, src in (("dbg_oT", oT_d), ("dbg_ynT", ynT_d), ("dbg_mT", mT_d), ("dbg_x1T", x1T_d), ("dbg_cqT", cqT_d),
                        ("dbg_z", z_d), ("dbg_dt", dt_d), ("dbg_xs", xs_d), ("dbg_gT", gT_d)):
            dst = nc.dram_tensor(nm, list(src.t.shape), src.t.dtype, kind="ExternalOutput").ap()
            P.add("sp", lambda e, dst=dst, src=src: e.dma_start(out=dst, in_=src.t), reads=[src], dma=True)
    P.emit(st)
    st.close()
    return nc


def rope_tables(pos_local):
    half = ROPE // 2
    inv = np.power(np.float32(10000.0), -np.arange(half, dtype=np.float32) / np.float32(half)).astype(np.float32)
    pos = np.concatenate([pos_local, PAST + np.arange(TS)]).astype(np.float32)
    ang = pos[:, None] * inv[None, :]
    cos, sin = np.cos(ang).astype(np.float32), np.sin(ang).astype(np.float32)
    cos2 = np.concatenate([cos, cos], axis=1)
    sin2 = np.concatenate([-sin, sin], axis=1)
    tok = np.ascontiguousarray(np.concatenate([cos2, sin2], axis=1))
    fm = np.ascontiguousarray(np.stack([cos2.T, sin2.T]))
    return fm, tok


def ssd_consts(L, R, nseg):
    c = np.zeros((128, 128 + 128 + nseg), np.float32)
    idx = np.arange(R)
    same = (idx[:, None] // L) == (idx[None, :] // L)
    c[:R, 0:R] = same & (idx[:, None] <= idx[None, :])
    c[:R, 128:128 + R] = same & (idx[None, :] < idx[:, None])
    for s_ in range(nseg):
        c[s_ * L:(s_ + 1) * L, 256 + s_] = 1.0
    return c


_STAGES = {}
_NCORES = [8]
_LAST = [None]
_LAST_RES = [None]
_RUNKW = {}


def kernel(**inp):
    ncores = _NCORES[0]
    nc = build_program(_STAGES)
    f = lambda a: np.ascontiguousarray(np.asarray(a, dtype=np.float32))
    shared = {
        "n_f1": f(inp["norm_ffn1"][0]), "wg1": f(inp["w_ffn1_gate"][0]), "wu1": f(inp["w_ffn1_up"][0]),
        "wd1": f(inp["w_ffn1_down"][0]), "n_mix": f(inp["norm_mix"][0]), "w_in": f(inp["w_in"][0]),
        "b_gate": f(inp["b_gate"][0]).reshape(-1), "n_qa": f(inp["norm_q_a"][0]), "w_qb": f(inp["w_q_b"][0]),
        "n_kva": f(inp["norm_kv_a"][0]), "w_kvb": f(inp["w_kv_b"][0]), "conv_w": f(inp["conv_w"][0]),
        "conv_b": f(inp["conv_b"][0]), "dt_bias": f(inp["dt_bias"][0]), "a_log": f(inp["a_log"][0]),
        "d_skip": f(inp["d_skip"][0]), "n_ssm": f(inp["norm_ssm"][0]), "w_oa": f(inp["w_o_attn"][0]),
        "w_os": f(inp["w_o_ssm"][0]), "w_out": f(inp["w_out"][0]), "n_f2": f(inp["norm_ffn2"][0]),
        "wg2": f(inp["w_ffn2_gate"][0]), "wu2": f(inp["w_ffn2_up"][0]), "wd2": f(inp["w_ffn2_down"][0]),
        "n_fin": f(inp["norm_final"]), "ident_in": np.eye(128, dtype=np.float32),
        "tri_in": np.zeros((128, 64), np.float32), "ustr_in": np.zeros((128, 64), np.float32),
        "cst_p": ssd_consts(128, 128, 1), "cst_s": ssd_consts(16, 64, 4),
    }
    in_maps = []
    for c in range(ncores):
        m = dict(shared)
        b, kq = c // NQ, c % NQ
        npre = (NQ - 1 - kq) * QT
        xl = np.zeros((SEQ, D), np.float32)
        xl[npre:] = f(inp["x_prompt"][b, 0:(kq + 1) * QT])
        pos_local = np.maximum(np.arange(SEQ) - npre, 0)
        fm, tok = rope_tables(pos_local)
        valid = (np.arange(SEQ) >= npre).astype(np.float32).reshape(SEQ, 1)
        kneg = np.where(np.arange(SEQ // 128)[None, :] * 128 >= npre, 0.0, NEG).astype(np.float32)
        m["xp"] = xl
        m["ropeT"] = fm
        m["rope_tok"] = tok
        m["valid_in"] = valid
        m["kneg_in"] = np.ascontiguousarray(np.broadcast_to(kneg, (128, SEQ // 128)))
        sl = slice(NS * c, NS * (c + 1))
        m["xs"] = f(inp["x_sample"][sl]).reshape(NS * TS, D)
        m["c_lat"] = f(inp["cache_kv_latent"][0, sl])
        m["c_kr"] = f(inp["cache_k_rope"][0, sl])
        m["s_ssm_in"] = f(inp["state_ssm"][0, sl]).reshape(NS, DI, NST)
        m["s_conv_in"] = f(inp["state_conv"][0, sl]).reshape(NS * 3, CONV)
        in_maps.append(m)
    res = run_bass_kernel_spmd(nc, in_maps, core_ids=list(range(ncores)), **_RUNKW)
    _LAST_RES[0] = res
    R = list(res.results)
    _LAST[0] = R
    while len(R) < 8:
        R.append(R[len(R) % len(res.results)])
    cat = lambda k: np.concatenate([R[c][k] for c in range(8)], axis=0)
    seqcat = lambda k: np.stack([np.concatenate([R[b * NQ + j][k] for j in range(NQ)], axis=0) for b in range(2)])
    y_prompt = seqcat("y_p")
    y_sample = cat("y_s").reshape(32, TS, D)
    p_lat = seqcat("o_plat")[None]
    p_kr = seqcat("o_pkr")[None]
    p_ssm = np.stack([R[NQ - 1]["o_pssm"], R[2 * NQ - 1]["o_pssm"]]).reshape(1, 2, HS, HS, NST)
    p_conv = np.stack([R[NQ - 1]["o_pconv"], R[2 * NQ - 1]["o_pconv"]])[None]
    s_lat = cat("o_slat").reshape(1, 32, TS, KVL)
    s_kr = cat("o_skr").reshape(1, 32, TS, ROPE)
    s_ssm = cat("o_sssm").reshape(1, 32, HS, HS, NST)
    s_conv = cat("o_sconv").reshape(1, 32, 3, CONV)
    return (y_prompt, y_sample, p_lat, p_kr, p_ssm, p_conv, s_lat, s_kr, s_ssm, s_conv)
```
